# Optimizing a Trainium2 kernel written in Bass

```python
import functools
import jax, jax.numpy as jnp
from jax import lax
import numpy as np

D_MODEL = 2048
BATCH = 2
SEQ = 4096
DEPTH = 1
DEC_BATCH = 128
DEC_SEQ = 1
PAST_LEN = 2048
PAGE_SIZE = 128

M_HEADS = 4
M_DK = 128
M_DV = 256
F_HEADS = 8
F_DH = 128
D_FF = 5632
CONV_W = 3
CHUNK = 128
Q_BLOCK = 128
NORM_EPS = 1e-6

M_QK = M_HEADS * M_DK
M_V = M_HEADS * M_DV
F_W = F_HEADS * F_DH
SPLIT_SIZES = (M_QK, M_QK, M_V, M_V, M_HEADS, M_HEADS, F_W, F_W, F_W, F_HEADS, D_MODEL, D_MODEL)
SPLIT_POINTS = tuple(sum(SPLIT_SIZES[:i + 1]) for i in range(len(SPLIT_SIZES) - 1))
P_IN = sum(SPLIT_SIZES)
IDX_M_FORGET = 5
IDX_F_FORGET = 9

kernel_name = 'hybrid_mlstm_fox_convffn_adaln_step'


def rmsnorm(x, w):
    xf = x.astype(jnp.float32)
    y = xf * lax.rsqrt(jnp.mean(xf * xf, axis=-1, keepdims=True) + NORM_EPS)
    return (y * w.astype(jnp.float32)).astype(x.dtype)


def ada_mod(c, w_ada, b_ada):
    m = jax.nn.silu(c) @ w_ada + b_ada
    return [t[:, None, :] for t in jnp.split(m, 6, axis=-1)]


def modulate(h, shift, scale):
    return h * (1.0 + scale) + shift


def split_heads(t, n):
    return t.reshape(t.shape[0], t.shape[1], n, -1)


def mlstm_chunk(carry, inputs):
    C0, n0, m0 = carry
    q, k, v, ig, lf = inputs
    L = q.shape[2]
    b = jnp.cumsum(lf, axis=-1)
    m = b + jnp.maximum(m0[..., None], lax.cummax(ig - b, axis=2))
    causal = jnp.tril(jnp.ones((L, L), dtype=bool))
    log_d = b[..., :, None] - b[..., None, :] + ig[..., None, :] - m[..., :, None]
    dmat = jnp.exp(jnp.where(causal, log_d, -jnp.inf))
    inter = jnp.exp(b + m0[..., None] - m)
    s = jnp.einsum('bhtd,bhsd->bhts', q, k) * dmat
    num = jnp.einsum('bhts,bhsv->bhtv', s, v) + inter[..., None] * jnp.einsum('bhvd,bhtd->bhtv', C0, q)
    den = jnp.sum(s, axis=-1) + inter * jnp.einsum('bhd,bhtd->bht', n0, q)
    h = num / jnp.maximum(jnp.abs(den), jnp.exp(-m))[..., None]
    m_new = m[..., -1]
    w_end = jnp.exp(ig + b[..., -1:] - b - m_new[..., None])
    decay = jnp.exp(b[..., -1] + m0 - m_new)
    C_new = decay[..., None, None] * C0 + jnp.einsum('bhs,bhsv,bhsd->bhvd', w_end, v, k)
    n_new = decay[..., None] * n0 + jnp.einsum('bhs,bhsd->bhd', w_end, k)
    return (C_new, n_new, m_new), h


def mlstm_prompt(q, k, v, ig, lf):
    B, H, T, _ = q.shape
    nc = T // CHUNK

    def to_chunks(a):
        return jnp.moveaxis(a.reshape((B, H, nc, CHUNK) + a.shape[3:]), 2, 0)

    xs = tuple(to_chunks(a) for a in (q, k, v, ig, lf))
    carry0 = (jnp.zeros((B, H, M_DV, M_DK), jnp.float32),
              jnp.zeros((B, H, M_DK), jnp.float32),
              jnp.zeros((B, H), jnp.float32))
    state, hs = lax.scan(mlstm_chunk, carry0, xs)
    h = jnp.moveaxis(hs, 0, 2).reshape(B, H, T, M_DV)
    return h, state


def mlstm_from_state(C0, n0, m0, q, k, v, ig, lf):
    carry0 = (C0.astype(jnp.float32), n0.astype(jnp.float32), m0.astype(jnp.float32))
    state, h = mlstm_chunk(carry0, (q, k, v, ig, lf))
    return h, state


def mlstm_output(h, mo, hnorm_w):
    B, H, T, _ = h.shape
    hn = h * lax.rsqrt(jnp.mean(h * h, axis=-1, keepdims=True) + NORM_EPS)
    hn = hn * hnorm_w.reshape(M_HEADS, 1, M_DV).astype(jnp.float32)
    hn = hn.transpose(0, 2, 1, 3).reshape(B, T, M_V)
    return (jax.nn.sigmoid(mo.astype(jnp.float32)) * hn).astype(mo.dtype)


def fox_prompt(q, k, v, lf):
    B, T, H, Dh = q.shape
    nb = T // Q_BLOCK
    scale = Dh ** -0.5
    F = jnp.cumsum(lf, axis=1)
    FkT = F.transpose(0, 2, 1)
    qb = q.reshape(B, nb, Q_BLOCK, H, Dh).swapaxes(0, 1)
    Fb = F.reshape(B, nb, Q_BLOCK, H).swapaxes(0, 1)
    kpos = jnp.arange(T)

    def block(args):
        qi, Fi, i = args
        qpos = i * Q_BLOCK + jnp.arange(Q_BLOCK)
        s = jnp.einsum('bqhd,bkhd->bhqk', qi, k).astype(jnp.float32) * scale
        s = s + (Fi.transpose(0, 2, 1)[..., :, None] - FkT[:, :, None, :])
        s = jnp.where(qpos[:, None] >= kpos[None, :], s, -jnp.inf)
        p = jax.nn.softmax(s, axis=-1).astype(v.dtype)
        return jnp.einsum('bhqk,bkhd->bqhd', p, v)

    out = lax.map(block, (qb, Fb, jnp.arange(nb)))
    return out.swapaxes(0, 1).reshape(B, T, H * Dh)


def fox_sample(cache_k, cache_v, cache_logf, page_table, layer, q, k, v, lf):
    DB, S, H, Dh = q.shape
    n_pages = page_table.shape[1]
    P = n_pages * cache_k.shape[2]
    scale = Dh ** -0.5
    pk = cache_k[layer, page_table].reshape(DB, P, H, Dh).astype(k.dtype)
    pv = cache_v[layer, page_table].reshape(DB, P, H, Dh).astype(v.dtype)
    plf = cache_logf[layer, page_table].reshape(DB, P, H).astype(jnp.float32)
    k_all = jnp.concatenate([pk, k], axis=1)
    v_all = jnp.concatenate([pv, v], axis=1)
    F = jnp.cumsum(jnp.concatenate([plf, lf], axis=1), axis=1)
    Fq = F[:, P:]
    s = jnp.einsum('bqhd,bkhd->bhqk', q, k_all).astype(jnp.float32) * scale
    s = s + (Fq.transpose(0, 2, 1)[..., :, None] - F.transpose(0, 2, 1)[:, :, None, :])
    qpos = P + jnp.arange(S)
    kpos = jnp.arange(P + S)
    s = jnp.where(qpos[:, None] >= kpos[None, :], s, -jnp.inf)
    p = jax.nn.softmax(s, axis=-1).astype(v.dtype)
    out = jnp.einsum('bhqk,bkhd->bqhd', p, v_all)
    return out.reshape(DB, S, H * Dh)


def conv_ffn(h, conv_state, w_ffn_in, conv_w, conv_b, w_ffn_out):
    T = h.shape[1]
    a, g = jnp.split(h @ w_ffn_in, 2, axis=-1)
    ap = jnp.concatenate([conv_state.astype(a.dtype), a], axis=1)
    ac = sum(conv_w[j] * ap[:, j:j + T] for j in range(CONV_W)) + conv_b
    out = (jax.nn.gelu(ac) * g) @ w_ffn_out
    return out, ap[:, ap.shape[1] - (CONV_W - 1):]


def hybrid_layer(x, c, mlstm_fn, fox_fn, conv_state,
                 w_ada, b_ada, norm1_w, w_in, b_in, m_hnorm_w, f_qnorm_w, f_knorm_w,
                 w_proj_a, w_proj_b, w_out, norm2_w, w_ffn_in, conv_w, conv_b, w_ffn_out):
    B, T, _ = x.shape
    sh1, sc1, g1, sh2, sc2, g2 = ada_mod(c, w_ada, b_ada)
    h = modulate(rmsnorm(x, norm1_w), sh1, sc1)
    mq, mk, mv, mo, mi, mf, fq, fk, fv, ff, ga, gb = jnp.split(h @ w_in + b_in, SPLIT_POINTS, axis=-1)
    q_m = split_heads(mq, M_HEADS).transpose(0, 2, 1, 3).astype(jnp.float32)
    k_m = split_heads(mk, M_HEADS).transpose(0, 2, 1, 3).astype(jnp.float32) * (M_DK ** -0.5)
    v_m = split_heads(mv, M_HEADS).transpose(0, 2, 1, 3).astype(jnp.float32)
    ig = mi.astype(jnp.float32).transpose(0, 2, 1)
    lf_m = jax.nn.log_sigmoid(mf.astype(jnp.float32)).transpose(0, 2, 1)
    h_m, m_state = mlstm_fn(q_m, k_m, v_m, ig, lf_m)
    y_m = mlstm_output(h_m, mo, m_hnorm_w)
    q_f = rmsnorm(split_heads(fq, F_HEADS), f_qnorm_w)
    k_f = rmsnorm(split_heads(fk, F_HEADS), f_knorm_w)
    v_f = split_heads(fv, F_HEADS)
    lf_f = jax.nn.log_sigmoid(ff.astype(jnp.float32))
    y_f = fox_fn(q_f, k_f, v_f, lf_f)
    merged = jax.nn.sigmoid(ga) * (y_m @ w_proj_a) + jax.nn.sigmoid(gb) * (y_f @ w_proj_b)
    x = x + g1 * (merged @ w_out)
    h2 = modulate(rmsnorm(x, norm2_w), sh2, sc2)
    f_out, conv_new = conv_ffn(h2, conv_state, w_ffn_in, conv_w, conv_b, w_ffn_out)
    x = x + g2 * f_out
    return x, m_state, (k_f, v_f, lf_f.astype(x.dtype)), conv_new


def setup_inputs(seed: int = 0) -> dict:
    key = jax.random.key(seed)
    ks = jax.random.split(key, 32)
    n_pages = PAST_LEN // PAGE_SIZE
    n_pool = (DEC_BATCH * n_pages * 5) // 4

    def nrm(k, shape, scale):
        return scale * jax.random.normal(k, shape, jnp.float32)

    forget_offsets = [jnp.zeros((s,), jnp.float32) for s in SPLIT_SIZES]
    forget_offsets[IDX_M_FORGET] = jnp.linspace(3.0, 6.0, M_HEADS)
    forget_offsets[IDX_F_FORGET] = jnp.linspace(1.0, 4.0, F_HEADS)
    b_in = nrm(ks[17], (DEPTH, P_IN), 0.02) + jnp.concatenate(forget_offsets)[None, :]

    return {
        'x_prompt': nrm(ks[0], (BATCH, SEQ, D_MODEL), 1.0),
        'x_sample': nrm(ks[1], (DEC_BATCH, DEC_SEQ, D_MODEL), 1.0),
        'c_prompt': nrm(ks[2], (BATCH, D_MODEL), 1.0),
        'c_sample': nrm(ks[3], (DEC_BATCH, D_MODEL), 1.0),
        'cache_k': nrm(ks[4], (DEPTH, n_pool, PAGE_SIZE, F_HEADS, F_DH), 1.0),
        'cache_v': nrm(ks[5], (DEPTH, n_pool, PAGE_SIZE, F_HEADS, F_DH), 1.0),
        'cache_logf': jax.nn.log_sigmoid(2.0 + nrm(ks[6], (DEPTH, n_pool, PAGE_SIZE, F_HEADS), 1.0)),
        'page_table': jax.random.permutation(ks[7], n_pool)[:DEC_BATCH * n_pages].reshape(DEC_BATCH, n_pages).astype(jnp.int32),
        'state_C': nrm(ks[8], (DEPTH, DEC_BATCH, M_HEADS, M_DV, M_DK), 0.1),
        'state_n': nrm(ks[9], (DEPTH, DEC_BATCH, M_HEADS, M_DK), 0.1),
        'state_m': nrm(ks[10], (DEPTH, DEC_BATCH, M_HEADS), 1.0),
        'state_conv': nrm(ks[11], (DEPTH, DEC_BATCH, CONV_W - 1, D_FF), 1.0),
        'w_ada': nrm(ks[12], (DEPTH, D_MODEL, 6 * D_MODEL), 0.5 * D_MODEL ** -0.5),
        'b_ada': nrm(ks[13], (DEPTH, 6 * D_MODEL), 0.02),
        'norm1_w': 1.0 + nrm(ks[14], (DEPTH, D_MODEL), 0.02),
        'w_in': nrm(ks[15], (DEPTH, D_MODEL, P_IN), D_MODEL ** -0.5),
        'b_in': b_in,
        'm_hnorm_w': 1.0 + nrm(ks[18], (DEPTH, M_V), 0.02),
        'f_qnorm_w': 1.0 + nrm(ks[19], (DEPTH, F_DH), 0.02),
        'f_knorm_w': 1.0 + nrm(ks[20], (DEPTH, F_DH), 0.02),
        'w_proj_a': nrm(ks[21], (DEPTH, M_V, D_MODEL), M_V ** -0.5),
        'w_proj_b': nrm(ks[22], (DEPTH, F_W, D_MODEL), F_W ** -0.5),
        'w_out': nrm(ks[23], (DEPTH, D_MODEL, D_MODEL), D_MODEL ** -0.5),
        'norm2_w': 1.0 + nrm(ks[24], (DEPTH, D_MODEL), 0.02),
        'w_ffn_in': nrm(ks[25], (DEPTH, D_MODEL, 2 * D_FF), D_MODEL ** -0.5),
        'conv_w': nrm(ks[26], (DEPTH, CONV_W, D_FF), CONV_W ** -0.5),
        'conv_b': nrm(ks[27], (DEPTH, D_FF), 0.02),
        'w_ffn_out': nrm(ks[28], (DEPTH, D_FF, D_MODEL), D_FF ** -0.5),
    }


def reference(x_prompt, x_sample, c_prompt, c_sample, cache_k, cache_v, cache_logf, page_table,
              state_C, state_n, state_m, state_conv,
              w_ada, b_ada, norm1_w, w_in, b_in, m_hnorm_w, f_qnorm_w, f_knorm_w,
              w_proj_a, w_proj_b, w_out, norm2_w, w_ffn_in, conv_w, conv_b, w_ffn_out):
    yp, ys = x_prompt, x_sample
    outs_p = [[] for _ in range(7)]
    outs_s = [[] for _ in range(7)]
    for l in range(DEPTH):
        lw = (w_ada[l], b_ada[l], norm1_w[l], w_in[l], b_in[l], m_hnorm_w[l], f_qnorm_w[l], f_knorm_w[l],
              w_proj_a[l], w_proj_b[l], w_out[l], norm2_w[l], w_ffn_in[l], conv_w[l], conv_b[l], w_ffn_out[l])
        conv0 = jnp.zeros((yp.shape[0], CONV_W - 1, D_FF), yp.dtype)
        yp, (Cp, nP, mP), (kp, vp, lfp), convp = hybrid_layer(
            yp, c_prompt, mlstm_prompt, fox_prompt, conv0, *lw)
        ys, (Cs, nS, mS), (kS, vS, lfS), convs = hybrid_layer(
            ys, c_sample,
            functools.partial(mlstm_from_state, state_C[l], state_n[l], state_m[l]),
            functools.partial(fox_sample, cache_k, cache_v, cache_logf, page_table, l),
            state_conv[l], *lw)
        for lst, val in zip(outs_p, (kp, vp, lfp, Cp, nP, mP, convp)):
            lst.append(val)
        for lst, val in zip(outs_s, (kS, vS, lfS, Cs, nS, mS, convs)):
            lst.append(val)
    k_prompt, v_prompt, logf_prompt, C_prompt, n_prompt, m_prompt, conv_prompt = [jnp.stack(a) for a in outs_p]
    k_sample, v_sample, logf_sample, C_sample, n_sample, m_sample, conv_sample = [jnp.stack(a) for a in outs_s]
    return (yp, ys, k_prompt, v_prompt, logf_prompt, C_prompt, n_prompt, m_prompt, conv_prompt,
            k_sample, v_sample, logf_sample, C_sample, n_sample, m_sample, conv_sample)
```

```python
import contextlib
import numpy as np
import concourse.bass as bass
import concourse.mybir as mybir
from concourse.bass_utils import run_bass_kernel_spmd

F32 = mybir.dt.float32
BF16 = mybir.dt.bfloat16
I32 = mybir.dt.int32
AF = mybir.ActivationFunctionType
ALU = mybir.AluOpType
AX = mybir.AxisListType

D = 2048
KC = 16
T = 4096
NG = 8
GT = 512
NS = 16
DFF = 5632
FC = 44
PIN = 10256
EPS = 1e-6
NPOOL = 2560
C_MQ, C_MK, C_MV, C_MO, C_MI, C_MF, C_FQ, C_FK, C_FV, C_FF, C_GA, C_GB = (
    0, 512, 1024, 2048, 3072, 3076, 3080, 4104, 5128, 6152, 6160, 8208)


class Trk:
    def __init__(self, nc, es):
        self.nc = nc
        self.es = es
        self.eng = {'pe': nc.tensor, 'act': nc.scalar, 'dve': nc.vector, 'pool': nc.gpsimd, 'sp': nc.sync}
        self.sem = {}
        self.cnt = {}
        for e in self.eng:
            self.sem[e] = es.enter_context(nc.semaphore('s_' + e))
            self.cnt[e] = 0
        self.waited = {e: {} for e in self.eng}
        self.bw = {}
        self.br = {}
        self.dsem = {}
        self.dcnt = {}
        self.ninst = 0

    def _deps(self, reads, writes):
        deps = []
        for r in reads:
            if r in self.bw:
                deps.append(self.bw[r])
        for w in writes:
            if w in self.bw:
                deps.append(self.bw[w])
            deps.extend(self.br.get(w, []))
        return deps

    def _wait(self, e, deps):
        need = {}
        for (k, v) in deps:
            if k == e and e == 'pe':
                continue
            if v > need.get(k, 0):
                need[k] = v
        for k, v in need.items():
            if self.waited[e].get(k, 0) >= v:
                continue
            s = self.sem[k] if k in self.sem else self.dsem[k]
            self.eng[e].wait_ge(s, v)
            self.waited[e][k] = v

    def _commit(self, tok, reads, writes):
        for r in reads:
            self.br.setdefault(r, []).append(tok)
        for w in writes:
            self.bw[w] = tok
            self.br[w] = []

    def op(self, e, fn, reads=(), writes=()):
        self._wait(e, self._deps(reads, writes))
        ins = fn(self.eng[e])
        self.cnt[e] += 1
        ins.then_inc(self.sem[e], 1)
        self._commit((e, self.cnt[e]), reads, writes)
        self.ninst += 1

    def dma(self, q, key, out, in_, reads=(), writes=(), **kw):
        self._wait(q, self._deps(reads, writes))
        if key not in self.dsem:
            self.dsem[key] = self.es.enter_context(self.nc.semaphore('d_' + key))
            self.dcnt[key] = 0
        ins = self.eng[q].dma_start(out=out, in_=in_, **kw)
        self.dcnt[key] += 16
        ins.then_inc(self.dsem[key], 16)
        self._commit((key, self.dcnt[key]), reads, writes)
        self.ninst += 1

    def barrier(self):
        for e in self.eng:
            for k in self.eng:
                if k != e and self.cnt[k] > self.waited[e].get(k, 0):
                    self.eng[e].wait_ge(self.sem[k], self.cnt[k])
                    self.waited[e][k] = self.cnt[k]
            for k, sm_ in self.dsem.items():
                if self.dcnt[k] > self.waited[e].get(k, 0):
                    self.eng[e].wait_ge(sm_, self.dcnt[k])
                    self.waited[e][k] = self.dcnt[k]

    def finish(self):
        for k, s in self.dsem.items():
            self.eng['sp'].wait_ge(s, self.dcnt[k])
        for e in self.eng:
            if e != 'sp' and self.cnt[e] > 0:
                self.eng['sp'].wait_ge(self.sem[e], self.cnt[e])


def build_program(stage=99, ng=NG):
    nc = bass.Bass("TRN2", target_bir_lowering=False)

    def din(name, shape, dt=F32):
        return nc.dram_tensor(name, list(shape), dt, kind="ExternalInput").ap()

    def dout(name, shape, dt=F32):
        return nc.dram_tensor(name, list(shape), dt, kind="ExternalOutput").ap()

    def dscr(name, shape, dt=F32):
        return nc.dram_tensor(name, list(shape), dt, kind="Internal").ap()

    xw = din("xw", [T, D])
    xs = din("xs", [NS, D])
    cT_in = din("cT", [128, KC, NS + 1])
    w_ada = din("w_ada", [D, 6 * D])
    w_in = din("w_in", [D, PIN])
    w_pa = din("w_pa", [1024, D])
    w_pb = din("w_pb", [1024, D])
    w_out = din("w_out", [D, D])
    w_f1 = din("w_f1", [D, 2 * DFF])
    w_f2 = din("w_f2", [DFF, D])
    b_adaT = din("b_adaT", [128, 96])
    b_ada_row = din("b_ada_row", [1, 6 * D])
    n1wT = din("n1wT", [128, KC])
    n2wT = din("n2wT", [128, KC])
    b_in_row = din("b_in_row", [1, PIN])
    b_gT = din("b_gT", [128, 32])
    b_gates = din("b_gates", [8, 2])
    hnw_rep = din("hnw_rep", [128, 1024])
    qnw_rep = din("qnw_rep", [128, 128])
    knw_rep = din("knw_rep", [128, 128])
    cwT = din("cwT", [128, FC, 3])
    cbT = din("cbT", [128, FC])
    ident_in = din("ident", [128, 128])
    maskT_in = din("maskT", [128, 128])
    cache_k = din("cache_k", [NPOOL * 128, 1024])
    cache_v = din("cache_v", [NPOOL * 128, 1024])
    cache_lf = din("cache_lf", [NPOOL, 1024])
    ptab = din("ptab", [1, NS * 16], I32)
    ptab_col = din("ptab_col", [128, 2], I32)
    st_C = din("st_C", [NS, 4, 256, 128])
    st_n = din("st_n", [NS, 512])
    st_m = din("st_m", [NS, 4])
    st_convT = din("st_convT", [128, FC, NS, 2])
    st_conv = din("st_conv", [NS, 2, DFF])
    sel_in = din("sel", [NS, NS, 128])
    hsel_in = din("hsel", [8, NS, NS])
    kmask_in = din("kmask", [128, 32])
    vgate_in = din("vgate", [8, NG, 2])
    halo_valid_in = din("halo_valid", [128, 1])
    bdm_in = din("bdm", [8, 1024])
    pidx_in = din("pidx", [128, 1])
    Lmat_in = din("Lmat", [128, 128])
    Emat_in = din("Emat", [NS, 2, 128])
    b_moT_in = din("b_moT", [128, 8])
    hnwT_in = din("hnwT", [128, 8])

    y_p = dout("y_p", [1024, D])
    y_s = dout("y_s", [NS, D])
    k_p = dout("k_p", [1024, 1024])
    v_p = dout("v_p", [1024, 1024])
    lf_p = dout("lf_p", [1024, 8])
    C_p = dout("C_p", [4, 256, 128])
    n_p = dout("n_p", [4, 128])
    m_p = dout("m_p", [4, 1])
    cv_p = dout("cv_p", [2, DFF])
    k_s = dout("k_s", [NS, 1024])
    v_s = dout("v_s", [NS, 1024])
    lf_s = dout("lf_s", [NS, 8])
    C_s = dout("C_s", [NS, 4, 256, 128])
    n_s = dout("n_s", [NS, 512])
    m_s = dout("m_s", [NS, 4])
    cv_s = dout("cv_s", [NS, 2, DFF])
    dbg_ymT = dout("dbg_ymT", [128, 8, GT], BF16)
    dbg_yfT = dout("dbg_yfT", [128, 8, GT], BF16)
    dbg_x1 = dout("dbg_x1", [GT, D])

    kT_scr = dscr("kT_scr", [8, 128, T], BF16)
    v_scr = dscr("v_scr", [8, 128, 32, 130], BF16)
    F_scr = dscr("F_scr", [8, T])
    F3_scr = dscr("F3_scr", [3, 8, T], BF16)
    U_scr = dscr("U_scr", [4, T])
    B_scr = dscr("B_scr", [4, T])
    Ae_scr = dscr("Ae_scr", [4, 33])
    G_scr = dscr("G_scr", [8, GT])
    gsm_scr = dscr("gsm_scr", [NS, 2, D], BF16)

    es = contextlib.ExitStack()
    with es:
        tk = Trk(nc, es)

        es_loop = contextlib.ExitStack()
        es_samp = contextlib.ExitStack()

        def sb(name, shape, dt=F32, st=None):
            return (st or es).enter_context(nc.sbuf_tensor("sb_" + name, list(shape), dt))

        def ps(name, shape, dt=F32):
            return es.enter_context(nc.psum_tensor("pp_" + name, list(shape), dt))

        uid = [0]

        class Buf:
            def __init__(self, t, n, k=None):
                self.t = t
                self.n = n
                self.k = k or n

            def __getitem__(self, k):
                return self.t[k]

        def sbuf(name, shape, dt=F32, st=None):
            uid[0] += 1
            t = (st or es).enter_context(nc.sbuf_tensor("sb%d_%s" % (uid[0], name), list(shape), dt))
            return Buf(t, "%s_%d" % (name, uid[0]), name)

        class Phase:
            def __init__(self):
                self.st = contextlib.ExitStack()

            def sb(self, name, shape, dt=F32):
                return sbuf(name, shape, dt, st=self.st)

            def close(self):
                tk.barrier()
                self.st.close()

        ident = sbuf("ident", [128, 128])
        identb = sbuf("identb", [128, 128], BF16)
        maskT = sbuf("maskT", [128, 128])
        maskTb = sbuf("maskTb", [128, 128], BF16)
        ones_b = sbuf("ones_b", [128, 128], BF16)
        cTs = sbuf("cTs", [128, KC, NS + 1])
        cTb = sbuf("cTb", [128, KC, NS + 1], BF16)
        modT = sbuf("modT", [128, 4, KC, NS + 1])
        grep = sbuf("grep", [128, 2, D], BF16)
        b_adaT_s = sbuf("b_adaT_s", [128, 96])
        brow = [sbuf("brow0", [1, 512], BF16), sbuf("brow1", [1, 512], BF16)]
        n1wT_s = sbuf("n1wT_s", [128, KC])
        n2wT_s = sbuf("n2wT_s", [128, KC])
        A1 = sbuf("A1", [128, KC, NS + 1])
        A2 = sbuf("A2", [128, KC, NS + 1])
        b_gT_s = sbuf("b_gT_s", [128, 32])
        b_gates_s = sbuf("b_gates_s", [8, 2])
        hnw_s = sbuf("hnw_s", [128, 1024])
        qnw_s = sbuf("qnw_s", [128, 128])
        knw_s = sbuf("knw_s", [128, 128])
        cwT_s = sbuf("cwT_s", [128, FC, 3])
        cbT_s = sbuf("cbT_s", [128, FC])
        wg_m = sbuf("wg_m", [128, KC, 8], BF16)
        wg_f = sbuf("wg_f", [128, KC, 8], BF16)
        st1 = sbuf("st1", [128, 8])
        nrm8 = sbuf("nrm8", [128, 8])
        St = sbuf("St", [128, 4, 257])
        aprev = sbuf("aprev", [128, FC, 2])
        nFk = sbuf("nFk", [128, 8, 32])
        zcol = sbuf("zcol", [8, 1])
        kmask = sbuf("kmask", [128, 32])
        vgate = sbuf("vgate", [8, NG, 2])
        halo_valid = sbuf("halo_valid", [128, 1])
        Bc = sbuf("Bc", [4, 1])
        Ac = sbuf("Ac", [4, 1])
        Fc = sbuf("Fc", [8, 1])
        Ucol = sbuf("Ucol", [128, 4, 4])
        Bcol = sbuf("Bcol", [128, 4, 4])
        ALb = sbuf("ALb", [128, 4, 5])
        wcol = sbuf("wcol", [128, 4, 4])
        sccol = sbuf("sccol", [128, 4, 4])
        thrcol = sbuf("thrcol", [128, 4, 4])
        xn = sbuf("xn", [128, D], BF16)
        wb = [sbuf("wb0", [128, 8192], BF16), sbuf("wb1", [128, 8192], BF16)]
        pss = [Buf(ps("ps%d" % i, [128, 512]), "ps%d" % i) for i in range(8)]

        def names(bufs):
            return [b if isinstance(b, str) else b.n for b in bufs]

        def load(q, dst, dst_ap, src):
            tk.dma(q, dst.k, dst_ap, src, writes=[dst.n])

        def store(q, dst, src, src_ap, extra_writes=(), extra_reads=()):
            tk.dma(q, "st_" + src.k, dst, src_ap, reads=[src.n] + list(extra_reads), writes=list(extra_writes))

        wslot = [0]

        def load_w(src_ap, shape_view, bias_ap=None):
            i = wslot[0] % 2
            wslot[0] += 1
            if bias_ap is not None:
                tk.dma('pool', brow[i].k, brow[i][0:1, 0:bias_ap.shape[-1]], bias_ap, writes=[brow[i].n])
            n = 1
            for s_ in shape_view[1:]:
                n *= s_
            flat = wb[i][:, 0:n]
            view = flat.rearrange("p (a b) -> p a b", a=shape_view[1]) if len(shape_view) == 3 else flat
            tk.dma('pool', wb[i].k, view, src_ap, writes=[wb[i].n])
            return view, wb[i], brow[i]

        def mm(out, out_ap, lhsT, rhs, reads, start, stop):
            tk.op('pe', lambda e: e.matmul(out_ap, lhsT, rhs, start=start, stop=stop), reads=names(reads), writes=[out.n])

        def tr(out, out_ap, src, in_ap, idt, idt_ap):
            tk.op('pe', lambda e: e.transpose(out_ap, in_ap, idt_ap), reads=[src.n, idt.n], writes=[out.n])

        def act(out, out_ap, in_ap, reads, func, bias=None, scale=None, accum=None, extra_writes=()):
            kw = {}
            if bias is not None:
                kw['bias'] = bias
            if scale is not None:
                kw['scale'] = scale
            if accum is not None:
                kw['accum_out'] = accum
            tk.op('act', lambda e: e.activation(out_ap, in_ap, func, **kw), reads=names(reads),
                  writes=[out.n] + names(extra_writes))

        def tt(out, out_ap, a, b, op, reads, eng='dve'):
            tk.op(eng, lambda e: e.tensor_tensor(out_ap, a, b, op), reads=names(reads), writes=[out.n])

        def ts(out, out_ap, a, s1, s2, op0, op1, reads, eng='dve'):
            if op1 is None:
                tk.op(eng, lambda e: e.tensor_scalar(out_ap, a, s1, None, op0), reads=names(reads), writes=[out.n])
            else:
                tk.op(eng, lambda e: e.tensor_scalar(out_ap, a, s1, s2, op0, op1), reads=names(reads), writes=[out.n])

        def stt(out, out_ap, in0, scalar, in1, op0, op1, reads):
            tk.op('dve', lambda e: e.scalar_tensor_tensor(out_ap, in0, scalar, in1, op0, op1), reads=names(reads),
                  writes=[out.n])

        def cp(out, out_ap, a, reads, eng='dve'):
            tk.op(eng, lambda e: e.tensor_copy(out_ap, a), reads=names(reads), writes=[out.n])

        def memset(out, out_ap, v):
            tk.op('dve', lambda e: e.memset(out_ap, v), writes=[out.n])

        def logsig(dst, dst_ap, src_ap, reads):
            act(dst, dst_ap, src_ap, reads, AF.Exp, scale=-1.0)
            act(dst, dst_ap, dst_ap, [dst], AF.Ln, bias=1.0)
            ts(dst, dst_ap, dst_ap, -1.0, None, ALU.mult, None, [dst])

        def rsqrt_mean(dst, dst_ap, src_ap, reads, n):
            ts(dst, dst_ap, src_ap, 1.0 / n, EPS, ALU.mult, ALU.add, reads)
            act(dst, dst_ap, dst_ap, [dst], AF.Ln)
            act(dst, dst_ap, dst_ap, [dst], AF.Exp, scale=-0.5)

        NCA = nc.allow_non_contiguous_dma(reason="tiny re-layout DMAs")
        es.enter_context(NCA)

        for (dst, src) in [(ident, ident_in), (maskT, maskT_in), (cTs, cT_in), (b_adaT_s, b_adaT), (n1wT_s, n1wT),
                           (n2wT_s, n2wT), (b_gT_s, b_gT), (b_gates_s, b_gates), (hnw_s, hnw_rep), (qnw_s, qnw_rep),
                           (knw_s, knw_rep), (cwT_s, cwT), (cbT_s, cbT), (kmask, kmask_in), (vgate, vgate_in),
                           (halo_valid, halo_valid_in)]:
            load('sp', dst, dst[:], src)
        w_in_v = w_in.rearrange("(kc p) c -> p kc c", p=128)
        tk.dma('pool', wg_m.k, wg_m[:], w_in_v[:, :, C_MI:C_MI + 8], writes=[wg_m.n])
        tk.dma('pool', wg_f.k, wg_f[:], w_in_v[:, :, C_FF:C_FF + 8], writes=[wg_f.n])
        cp(identb, identb[:], ident[:], [ident])
        cp(maskTb, maskTb[:], maskT[:], [maskT])
        memset(ones_b, ones_b[:], 1.0)
        memset(zcol, zcol[:], 0.0)
        memset(Bc, Bc[:], 0.0)
        memset(Ac, Ac[:], 0.0)
        memset(Fc, Fc[:], 0.0)
        memset(St, St[:], 0.0)
        memset(aprev, aprev[:], 0.0)
        store('sp', Ae_scr[:, 0:1], zcol, zcol[0:4, 0:1], extra_writes=["Ae_scr"])

        ph = Phase()
        cTrep = ph.sb("cTrep", [128, KC, 128], BF16)
        gsm = ph.sb("gsm", [NS, 2, D], BF16)
        act(cTb, cTb[:], cTs[:], [cTs], AF.Silu)
        for kc in range(KC):
            cp(cTrep, cTrep[:, kc, :], cTb[:, kc, NS:NS + 1].to_broadcast([128, 128]), [cTb])
        w_ada_v = w_ada.rearrange("(kc p) c -> p kc c", p=128)
        fm_slots = {0: 0, 1: 1, 3: 2, 4: 3}
        for part in range(6):
            for cb in range(4):
                c0 = part * D + cb * 512
                wv, wn, br_ = load_w(w_ada_v[:, :, c0:c0 + 512], [128, KC, 512], b_ada_row[0:1, c0:c0 + 512])
                if part in fm_slots:
                    slot = fm_slots[part]
                    for ct in range(4):
                        pt = pss[ct % 2]
                        for kc in range(KC):
                            mm(pt, pt[:, 0:NS + 1], wv[:, kc, ct * 128:(ct + 1) * 128], cTb[:, kc, :], [wn, cTb],
                               kc == 0, kc == KC - 1)
                        ch = cb * 4 + ct
                        act(modT, modT[:, slot, ch, :], pt[:, 0:NS + 1], [pt, b_adaT_s], AF.Identity,
                            bias=b_adaT_s[:, part * 16 + ch:part * 16 + ch + 1])
                else:
                    gi = 0 if part == 2 else 1
                    for (lh, M, dst, pidx) in [(cTrep, 128, grep, 2), (cTb, NS, gsm, 3)]:
                        pt = pss[pidx]
                        for kc in range(KC):
                            lhs = lh[:, kc, :] if M == 128 else lh[:, kc, 0:NS]
                            mm(pt, pt[0:M, :], lhs, wv[:, kc, :], [wn, lh], kc == 0, False)
                        mm(pt, pt[0:M, :], ones_b[0:1, 0:M], br_[0:1, :], [ones_b, br_], False, True)
                        cp(dst, dst[0:M, gi, cb * 512:(cb + 1) * 512], pt[0:M, :], [pt])
        for (A, slot, nw) in [(A1, 1, n1wT_s), (A2, 3, n2wT_s)]:
            ts(A, A[:], modT[:, slot, :, :], 1.0, None, ALU.add, None, [modT])
            tt(A, A[:], A[:], nw[:].unsqueeze(2).to_broadcast([128, KC, NS + 1]), ALU.mult, [A, nw])
        store('sp', gsm_scr, gsm, gsm[:], extra_writes=["gsm_scr"])
        ph.close()

        def norm_tile(x, x_ap, M, A, shslot, hdst, col0, mcol):
            act(xn, xn[0:M, :], x_ap, [x], AF.Square, accum=st1[0:M, 0:1], extra_writes=[st1])
            rsqrt_mean(st1, st1[0:M, 1:2], st1[0:M, 0:1], [st1], D)
            ts(xn, xn[0:M, :], x_ap, st1[0:M, 1:2], None, ALU.mult, None, [x, st1])
            for q4 in range(4):
                pt = pss[4 + (q4 % 2)]
                ptb = pt[:].bitcast(BF16)
                for i in range(4):
                    kc = q4 * 4 + i
                    tr(pt, ptb[:, i * 128:i * 128 + M], xn, xn[0:M, kc * 128:(kc + 1) * 128], identb, identb[0:M, 0:M])
                for i in range(4):
                    kc = q4 * 4 + i
                    if mcol is not None:
                        act(hdst, hdst[:, kc, col0:col0 + M], ptb[:, i * 128:i * 128 + M], [pt, modT, A], AF.Identity,
                            bias=modT[:, shslot, kc, mcol:mcol + 1], scale=A[:, kc, mcol:mcol + 1])
                    else:
                        tt(hdst, hdst[:, kc, col0:col0 + M], ptb[:, i * 128:i * 128 + M], A[:, kc, 0:NS], ALU.mult, [pt, A])
                        tt(hdst, hdst[:, kc, col0:col0 + M], hdst[:, kc, col0:col0 + M], modT[:, shslot, kc, 0:NS], ALU.add,
                           [hdst, modT])

        def qknorm(kb, k3, M, wt, scr):
            tq = scr[0:M, :].rearrange("p (h d) -> p h d", h=4)
            tt(scr, tq, k3, k3, ALU.mult, [kb])
            tk.op('dve', lambda e: e.tensor_reduce(nrm8[0:M, 0:4], tq, AX.X, ALU.add), reads=[scr.n], writes=[nrm8.n])
            rsqrt_mean(nrm8, nrm8[0:M, 0:4], nrm8[0:M, 0:4], [nrm8], 128)
            tt(kb, k3, k3, nrm8[0:M, 0:4].unsqueeze(2).to_broadcast([M, 4, 128]), ALU.mult, [kb, nrm8])
            tt(kb, k3, k3, wt[0:M, :].unsqueeze(1).to_broadcast([M, 4, 128]), ALU.mult, [kb, wt])

        def proj_tok(hsrc, subs, w_v, b_row, c0, nblk, consume):
            pending = [None]
            for blk in range(nblk):
                cc = c0 + blk * 512
                wv, wn, br_ = load_w(w_v[:, :, cc:cc + 512], [128, KC, 512], None if b_row is None else b_row[0:1, cc:cc + 512])
                for si, (M, col0, idx) in enumerate(subs):
                    pt = pss[si % 4]
                    for kc in range(KC):
                        mm(pt, pt[0:M, :], hsrc[:, kc, col0:col0 + M], wv[:, kc, :], [wn, hsrc], kc == 0,
                           (kc == KC - 1) and b_row is None)
                    if b_row is not None:
                        mm(pt, pt[0:M, :], ones_b[0:1, 0:M], br_[0:1, :], [ones_b, br_], False, True)
                    if pending[0] is not None:
                        pending[0]()
                    pending[0] = consume(M, idx, blk, pt)
            if pending[0] is not None:
                pending[0]()
                pending[0] = None

        def gates(wg, bcol, dst, hsrc, n_):
            pt = pss[6]
            for kc in range(KC):
                mm(pt, pt[0:8, 0:n_], wg[:, kc, :], hsrc[:, kc, 0:n_], [wg, hsrc], kc == 0, kc == KC - 1)
            act(dst, dst[:, 0:n_], pt[0:8, 0:n_], [pt, b_gates_s], AF.Identity, bias=b_gates_s[:, bcol:bcol + 1])

        dbg_hook = [None]

        def dense_tail(subs, ntok, hsrc, ymT_, yfT_, xres, g1src, g2src, sample, y_dst_fn, post_fn=None, halo=False):
            ph_ = Phase()
            mgT = ph_.sb("mgT", [128, KC, ntok], BF16)
            sg = ph_.sb("sg", [128, 4, ntok])
            tA = ph_.sb("tA", [128, 4, ntok])
            w_pa_v = w_pa.rearrange("(kc p) c -> p kc c", p=128)
            w_pb_v = w_pb.rearrange("(kc p) c -> p kc c", p=128)
            for cb in range(4):
                for br_i, (wp_v, ysrc, cg, boff) in enumerate([(w_pa_v, ymT_, C_GA, 0), (w_pb_v, yfT_, C_GB, 16)]):
                    wv, wn, _ = load_w(wp_v[:, :, cb * 512:(cb + 1) * 512], [128, 8, 512])
                    for ct in range(4):
                        pt = pss[ct]
                        for kc in range(8):
                            mm(pt, pt[:, 0:ntok], wv[:, kc, ct * 128:(ct + 1) * 128], ysrc[:, kc, 0:ntok], [wn, ysrc],
                               kc == 0, kc == 7)
                    wv2, wn2, _ = load_w(w_in_v[:, :, cg + cb * 512:cg + (cb + 1) * 512], [128, KC, 512])
                    for ct in range(4):
                        pt = pss[4 + ct]
                        for kc in range(KC):
                            mm(pt, pt[:, 0:ntok], wv2[:, kc, ct * 128:(ct + 1) * 128], hsrc[:, kc, 0:ntok], [wn2, hsrc],
                               kc == 0, kc == KC - 1)
                    for ct in range(4):
                        ch = cb * 4 + ct
                        act(sg, sg[:, ct, :], pss[4 + ct][:, 0:ntok], [pss[4 + ct], b_gT_s], AF.Sigmoid,
                            bias=b_gT_s[:, boff + ch:boff + ch + 1])
                        if br_i == 0:
                            tt(tA, tA[:, ct, :], pss[ct][:, 0:ntok], sg[:, ct, :], ALU.mult, [pss[ct], sg])
                        else:
                            tt(sg, sg[:, ct, :], pss[ct][:, 0:ntok], sg[:, ct, :], ALU.mult, [pss[ct], sg])
                            tt(mgT, mgT[:, ch, :], sg[:, ct, :], tA[:, ct, :], ALU.add, [sg, tA])
            tmp = ph_.sb("tmp", [128, 512])
            w_out_v = w_out.rearrange("(kc p) c -> p kc c", p=128)

            def cons_o(M, idx, blk, pt):
                xb, xap = xres(idx)
                gb_, gap = g1src(idx, blk * 512, 512)
                tt(tmp, tmp[0:M, :], pt[0:M, :], gap, ALU.mult, [pt, gb_])
                tt(xb, xap[:, blk * 512:(blk + 1) * 512], xap[:, blk * 512:(blk + 1) * 512], tmp[0:M, :], ALU.add, [xb, tmp])

            proj_tok(mgT, subs, w_out_v, None, 0, 4, cons_o)
            if dbg_hook[0] is not None:
                dbg_hook[0]()
            ph_.close()
            ph_ = Phase()
            tmp = ph_.sb("tmp", [128, 512])
            for (M, col0, idx) in subs:
                xb, xap = xres(idx)
                norm_tile(xb, xap, M, A2, 2, hsrc, col0, None if sample else NS)
            uT = None if halo else ph_.sb("uT", [128, FC, ntok], BF16)
            ab = ph_.sb("ab", [128, ntok + 2])
            t1 = ph_.sb("t1", [128, ntok])
            t2 = ph_.sb("t2", [128, ntok])
            if sample:
                cvT = ph_.sb("cvT", [128, FC, NS, 2])
                load('sp', cvT, cvT[:], st_convT)
                aS = ph_.sb("aS", [128, FC, NS])
            w_f1_v = w_f1.rearrange("(kc p) c -> p kc c", p=128)
            for fb in range(FC // 4):
                wa, wan, _ = load_w(w_f1_v[:, :, fb * 512:(fb + 1) * 512], [128, KC, 512])
                if not halo:
                    wg_, wgn, _ = load_w(w_f1_v[:, :, DFF + fb * 512:DFF + (fb + 1) * 512], [128, KC, 512])
                for ci in range(4):
                    fc = fb * 4 + ci
                    pa = pss[ci % 2]
                    pg = pss[2 + ci % 2]
                    for kc in range(KC):
                        mm(pa, pa[:, 0:ntok], wa[:, kc, ci * 128:(ci + 1) * 128], hsrc[:, kc, 0:ntok], [wan, hsrc], kc == 0,
                           kc == KC - 1)
                    if halo:
                        cp(aprev, aprev[:, fc, :], pa[:, ntok - 2:ntok], [pa])
                        continue
                    for kc in range(KC):
                        mm(pg, pg[:, 0:ntok], wg_[:, kc, ci * 128:(ci + 1) * 128], hsrc[:, kc, 0:ntok], [wgn, hsrc], kc == 0,
                           kc == KC - 1)
                    w0 = cwT_s[:, fc, 0:1]
                    w1 = cwT_s[:, fc, 1:2]
                    w2 = cwT_s[:, fc, 2:3]
                    if not sample:
                        cp(ab, ab[:, 0:2], aprev[:, fc, :], [aprev])
                        cp(ab, ab[:, 2:2 + ntok], pa[:, 0:ntok], [pa])
                        cp(aprev, aprev[:, fc, :], ab[:, ntok:ntok + 2], [ab])
                        ts(t1, t1[:], ab[:, 0:ntok], w0, cbT_s[:, fc:fc + 1], ALU.mult, ALU.add, [ab, cwT_s, cbT_s])
                        stt(t1, t1[:], ab[:, 1:1 + ntok], w1, t1[:], ALU.mult, ALU.add, [ab, t1, cwT_s])
                        stt(t1, t1[:], ab[:, 2:2 + ntok], w2, t1[:], ALU.mult, ALU.add, [ab, t1, cwT_s])
                    else:
                        cp(aS, aS[:, fc, :], pa[:, 0:ntok], [pa])
                        ts(t1, t1[:], cvT[:, fc, :, 0], w0, cbT_s[:, fc:fc + 1], ALU.mult, ALU.add, [cvT, cwT_s, cbT_s])
                        stt(t1, t1[:], cvT[:, fc, :, 1], w1, t1[:], ALU.mult, ALU.add, [cvT, t1, cwT_s])
                        stt(t1, t1[:], aS[:, fc, :], w2, t1[:], ALU.mult, ALU.add, [aS, t1, cwT_s])
                    tt(t2, t2[:], t1[:], t1[:], ALU.mult, [t1])
                    ts(t2, t2[:], t2[:], 0.044715, 1.0, ALU.mult, ALU.add, [t2])
                    tt(t2, t2[:], t2[:], t1[:], ALU.mult, [t2, t1])
                    act(t2, t2[:], t2[:], [t2], AF.Sigmoid, scale=1.5957691216057308)
                    tt(t2, t2[:], t2[:], t1[:], ALU.mult, [t2, t1])
                    tt(uT, uT[:, fc, :], t2[:], pg[:, 0:ntok], ALU.mult, [t2, pg])
            w_f2_v = w_f2.rearrange("(kc p) c -> p kc c", p=128)
            for oc in range(0 if halo else KC):
                wv, wn, _ = load_w(w_f2_v[:, :, oc * 128:(oc + 1) * 128], [128, FC, 128])
                for si, (M, col0, idx) in enumerate(subs):
                    pt = pss[si % 4]
                    for kc in range(FC):
                        mm(pt, pt[0:M, 0:128], uT[:, kc, col0:col0 + M], wv[:, kc, :], [wn, uT], kc == 0, kc == FC - 1)
                    xb, xap = xres(idx)
                    gb_, gap = g2src(idx, oc * 128, 128)
                    tt(tmp, tmp[0:M, 0:128], pt[0:M, 0:128], gap, ALU.mult, [pt, gb_])
                    tt(xb, xap[:, oc * 128:(oc + 1) * 128], xap[:, oc * 128:(oc + 1) * 128], tmp[0:M, 0:128], ALU.add,
                       [xb, tmp])
            y_dst_fn()
            if post_fn is not None:
                post_fn(aS if sample else None, ph_)
            ph_.close()

        php = Phase()
        xt = php.sb("xt", [128, 4, D])
        hT = php.sb("hT", [128, KC, GT], BF16)
        ymT = php.sb("ymT", [128, 8, GT], BF16)
        yfT = php.sb("yfT", [128, 8, GT], BF16)
        Cst = php.sb("Cst", [128, 128])
        xw_v = xw.rearrange("(g t p) d -> g p t d", p=128, t=4)
        yp_v = y_p.rearrange("(g t p) d -> g p t d", p=128, t=4)
        kp_v = k_p.rearrange("(g t p) d -> g p t d", p=128, t=4)
        vp_v = v_p.rearrange("(g t p) d -> g p t d", p=128, t=4)
        ngroups = ng
        psubs = [(128, t4 * 128, t4) for t4 in range(4)]
        for g in range(ngroups):
            mode = 'prefix' if g < ngroups - 3 else ('halo' if g == ngroups - 3 else 'own')
            out_tiles = [] if mode == 'prefix' else ([3] if mode == 'halo' else [0, 1, 2, 3])
            osubs = [(128, t4 * 128, t4) for t4 in out_tiles]
            lo = g - (ngroups - 2)
            load('sp', xt, xt[:], xw_v[g])
            for t4 in range(4):
                norm_tile(xt, xt[:, t4, :], 128, A1, 0, hT, t4 * 128, NS)
            phr = Phase()
            gm = phr.sb("gm", [8, GT])
            gf = phr.sb("gf", [8, GT])
            Ff = phr.sb("Ff", [8, GT])
            Fr = phr.sb("Fr", [8, GT])
            F3 = phr.sb("F3", [8, 3, GT], BF16)
            ig4 = phr.sb("ig4", [4, GT])
            mf4 = phr.sb("mf4", [4, GT])
            Bm = phr.sb("Bm", [4, GT])
            Um = phr.sb("Um", [4, GT])
            Am = phr.sb("Am", [4, GT])
            zrow = phr.sb("zrow", [8, GT])
            memset(zrow, zrow[:], 0.0)
            gates(wg_m, 0, gm, hT, GT)
            gates(wg_f, 1, gf, hT, GT)
            logsig(gf, gf[:], gf[:], [gf])
            ts(gf, gf[:], gf[:], vgate[:, g, 0:1], None, ALU.mult, None, [gf, vgate])
            if mode == 'own':
                store('sp', lf_p[lo * GT:(lo + 1) * GT, :].rearrange("t h -> h t"), gf, gf[:])
            tk.op('dve', lambda e: e.tensor_tensor_scan(Ff[:], gf[:], zrow[:], Fc[:], ALU.add, ALU.add),
                  reads=[gf.n, zrow.n, Fc.n], writes=[Ff.n])
            cp(Fc, Fc[:], Ff[:, GT - 1:GT], [Ff])
            store('sp', F_scr[:, g * GT:(g + 1) * GT], Ff, Ff[:], extra_writes=["F_scr"])
            cp(F3, F3[:, 0, :], Ff[:], [Ff])
            tt(Fr, Fr[:], Ff[:], F3[:, 0, :], ALU.subtract, [Ff, F3])
            cp(F3, F3[:, 1, :], Fr[:], [Fr])
            tt(Fr, Fr[:], Fr[:], F3[:, 1, :], ALU.subtract, [Fr, F3])
            cp(F3, F3[:, 2, :], Fr[:], [Fr])
            store('sp', F3_scr[:, :, g * GT:(g + 1) * GT].rearrange("j h t -> h j t"), F3, F3[:], extra_writes=["F3_scr"])
            for h in range(8):
                tk.dma('sp', nFk.k, nFk[:, h, 4 * g:4 * g + 4],
                       F_scr[h, g * GT:(g + 1) * GT].rearrange("(t p) -> p t", p=128), reads=["F_scr", nFk.n],
                       writes=["nFk_part%d" % h])
            ts(nFk, nFk[:, :, 4 * g:4 * g + 4], nFk[:, :, 4 * g:4 * g + 4], -1.0, None, ALU.mult, None,
               [nFk] + ["nFk_part%d" % h for h in range(8)])
            tt(nFk, nFk[:, :, 4 * g:4 * g + 4], nFk[:, :, 4 * g:4 * g + 4],
               kmask[:, 4 * g:4 * g + 4].unsqueeze(1).to_broadcast([128, 8, 4]), ALU.add, [nFk, kmask])
            store('sp', G_scr[:, 0:GT], gm, gm[:, 0:GT], extra_writes=["G_scr"])
            tk.dma('sp', ig4.k, ig4[:], G_scr[0:4, 0:GT], reads=["G_scr"], writes=[ig4.n])
            tk.dma('sp', mf4.k, mf4[:], G_scr[4:8, 0:GT], reads=["G_scr"], writes=[mf4.n])
            logsig(mf4, mf4[:], mf4[:], [mf4])
            ts(mf4, mf4[:], mf4[:], vgate[0:4, g, 0:1], None, ALU.mult, None, [mf4, vgate])
            ts(ig4, ig4[:], ig4[:], vgate[0:4, g, 0:1], vgate[0:4, g, 1:2], ALU.mult, ALU.add, [ig4, vgate])
            tk.op('dve', lambda e: e.tensor_tensor_scan(Bm[:], mf4[:], zrow[0:4, :], Bc[:], ALU.add, ALU.add),
                  reads=[mf4.n, zrow.n, Bc.n], writes=[Bm.n])
            tt(Um, Um[:], ig4[:], Bm[:], ALU.subtract, [ig4, Bm])
            tk.op('dve', lambda e: e.tensor_tensor_scan(Am[:], Um[:], Um[:], Ac[:], ALU.max, ALU.max),
                  reads=[Um.n, Ac.n], writes=[Am.n])
            cp(Bc, Bc[:], Bm[:, GT - 1:GT], [Bm])
            cp(Ac, Ac[:], Am[:, GT - 1:GT], [Am])
            store('sp', U_scr[:, g * GT:(g + 1) * GT], Um, Um[:], extra_writes=["U_scr"])
            store('sp', B_scr[:, g * GT:(g + 1) * GT], Bm, Bm[:], extra_writes=["B_scr"])
            store('sp', Ae_scr[:, 4 * g + 1:4 * g + 5], Am, Am[:].rearrange("h (t p) -> h t p", p=128)[:, :, 127],
                  extra_writes=["Ae_scr"])
            for h in range(4):
                tk.dma('sp', Ucol.k, Ucol[:, h, :], U_scr[h, g * GT:(g + 1) * GT].rearrange("(t p) -> p t", p=128),
                       reads=["U_scr", Ucol.n], writes=["Ucol_part%d" % h])
                tk.dma('sp', Bcol.k, Bcol[:, h, :], B_scr[h, g * GT:(g + 1) * GT].rearrange("(t p) -> p t", p=128),
                       reads=["B_scr", Bcol.n], writes=["Bcol_part%d" % h])
            tk.dma('sp', ALb.k, ALb[:], Ae_scr[:, 4 * g:4 * g + 5].unsqueeze(0).to_broadcast([128, 4, 5]),
                   reads=["Ae_scr"], writes=[ALb.n])
            tt(wcol, wcol[:], Ucol[:], ALb[:, :, 1:5], ALU.subtract, [Ucol, ALb] + ["Ucol_part%d" % h for h in range(4)])
            act(wcol, wcol[:], wcol[:], [wcol], AF.Exp)
            tt(sccol, sccol[:], ALb[:, :, 0:4], ALb[:, :, 1:5], ALU.subtract, [ALb])
            act(sccol, sccol[:], sccol[:], [sccol], AF.Exp)
            tt(thrcol, thrcol[:], Bcol[:], ALb[:, :, 1:5], ALU.add, [Bcol, ALb] + ["Bcol_part%d" % h for h in range(4)])
            act(thrcol, thrcol[:], thrcol[:], [thrcol], AF.Exp, scale=-1.0)

            phr.close()
            ph = Phase()
            kmt = ph.sb("kmt", [128, 4, 512], BF16)
            vm = ph.sb("vm", [128, 4, 1024], BF16)
            qm = ph.sb("qm", [128, 4, 512], BF16)
            smo = ph.sb("smo", [128, 4, 1024], BF16)
            qT = ph.sb("qT", [128, 4, GT], BF16)
            kT = ph.sb("kT", [128, 4, GT], BF16)
            Vp = ph.sb("Vp", [128, 257], BF16)
            Stb = ph.sb("Stb", [128, 257], BF16)
            Sm = ph.sb("Sm", [128, 128], BF16)
            hm = ph.sb("hm", [128, 256])
            hj = ph.sb("hj", [128, 256], BF16)
            ymt = ph.sb("ymt", [128, 256], BF16)
            dn = ph.sb("dn", [128, 4])

            proj_tok(hT, psubs, w_in_v, b_in_row, C_MK, 1,
                     lambda M, idx, blk, pt: act(kmt, kmt[:, idx, :], pt[0:M, :], [pt], AF.Identity, scale=128 ** -0.5))
            proj_tok(hT, psubs, w_in_v, b_in_row, C_MV, 2,
                     lambda M, idx, blk, pt: cp(vm, vm[:, idx, blk * 512:(blk + 1) * 512], pt[0:M, :], [pt]))
            if osubs:
                proj_tok(hT, osubs, w_in_v, b_in_row, C_MQ, 1,
                         lambda M, idx, blk, pt: cp(qm, qm[:, idx, :], pt[0:M, :], [pt]))
                proj_tok(hT, osubs, w_in_v, b_in_row, C_MO, 2,
                         lambda M, idx, blk, pt: act(smo, smo[:, idx, blk * 512:(blk + 1) * 512], pt[0:M, :], [pt], AF.Sigmoid))
            for t4 in out_tiles:
                for (src, dst, pi) in [(qm, qT, 4), (kmt, kT, 5)]:
                    pt = pss[pi]
                    ptb = pt[:].bitcast(BF16)
                    for h in range(4):
                        tr(pt, ptb[:, h * 128:(h + 1) * 128], src, src[:, t4, h * 128:(h + 1) * 128], identb, identb[:])
                    cp(dst, dst[:, :, t4 * 128:(t4 + 1) * 128], ptb[:, 0:512].rearrange("p (h t) -> p h t", h=4), [pt])
            for t4 in range(4):
                tc0 = t4 * 128
                for h in range(4):
                    ts(Vp, Vp[:, 0:256], vm[:, t4, h * 256:(h + 1) * 256], wcol[:, h, t4:t4 + 1], None, ALU.mult, None,
                       [vm, wcol])
                    cp(Vp, Vp[:, 256:257], wcol[:, h, t4:t4 + 1], [wcol])
                    ts(St, St[:, h, :], St[:, h, :], sccol[:, h, t4:t4 + 1], None, ALU.mult, None, [St, sccol])
                    if t4 not in out_tiles:
                        p2 = pss[2]
                        mm(p2, p2[:, 0:257], kmt[:, t4, h * 128:(h + 1) * 128], Vp[:], [kmt, Vp], True, True)
                        tt(St, St[:, h, :], St[:, h, :], p2[:, 0:257], ALU.add, [St, p2])
                        continue
                    cp(Stb, Stb[:], St[:, h, :], [St])
                    p0 = pss[0]
                    mm(p0, p0[:, 0:128], kT[:, h, tc0:tc0 + 128], qT[:, h, tc0:tc0 + 128], [kT, qT], True, True)
                    tt(Sm, Sm[:], p0[:, 0:128], maskT[:], ALU.mult, [p0, maskT])
                    p1 = pss[1]
                    mm(p1, p1[:, 0:257], Sm[:], Vp[:], [Sm, Vp], True, False)
                    mm(p1, p1[:, 0:257], qT[:, h, tc0:tc0 + 128], Stb[:], [qT, Stb], False, True)
                    p2 = pss[2]
                    mm(p2, p2[:, 0:257], kmt[:, t4, h * 128:(h + 1) * 128], Vp[:], [kmt, Vp], True, True)
                    tt(St, St[:, h, :], St[:, h, :], p2[:, 0:257], ALU.add, [St, p2])
                    act(dn, dn[:, 0:1], p1[:, 256:257], [p1], AF.Abs)
                    ts(dn, dn[:, 0:1], dn[:, 0:1], thrcol[:, h, t4:t4 + 1], None, ALU.max, None, [dn, thrcol])
                    tk.op('dve', lambda e: e.reciprocal(dn[:, 1:2], dn[:, 0:1]), reads=[dn.n], writes=[dn.n])
                    ts(hm, hm[:], p1[:, 0:256], dn[:, 1:2], None, ALU.mult, None, [p1, dn])
                    act(hj, hj[:], hm[:], [hm], AF.Square, accum=dn[:, 2:3], extra_writes=[dn])
                    rsqrt_mean(dn, dn[:, 3:4], dn[:, 2:3], [dn], 256)
                    stt(hm, hm[:], hm[:], dn[:, 3:4], hnw_s[:, h * 256:(h + 1) * 256], ALU.mult, ALU.mult, [hm, dn, hnw_s])
                    tt(ymt, ymt[:], hm[:], smo[:, t4, h * 256:(h + 1) * 256], ALU.mult, [hm, smo])
                    p3 = pss[3]
                    p3b = p3[:].bitcast(BF16)
                    for a in range(2):
                        tr(p3, p3b[:, a * 128:(a + 1) * 128], ymt, ymt[:, a * 128:(a + 1) * 128], identb, identb[:])
                    cp(ymT, ymT[:, 2 * h:2 * h + 2, tc0:tc0 + 128], p3b[:, 0:256].rearrange("p (a t) -> p a t", a=2), [p3])
            ph.close()

            ph = Phase()
            kst = [ph.sb("kst0", [128, 512]), ph.sb("kst1", [128, 512])]
            scr = ph.sb("scr", [128, 512])
            kb16s = [ph.sb("kb16a", [128, 512], BF16), ph.sb("kb16b", [128, 512], BF16)]
            kTst = ph.sb("kTst", [128, 4, 8, 128], BF16)
            vaug = ph.sb("vaug", [128, 4, 8, 130], BF16)
            QT = ph.sb("QT", [128, 8, GT], BF16)
            KTh = ph.sb("KTh", [128, T], BF16)
            Vh = ph.sb("Vh", [128, 32, 130], BF16)
            fq3 = ph.sb("fq3", [3, GT], BF16)
            eT = [ph.sb("eT0", [128, 512], BF16), ph.sb("eT1", [128, 512], BF16)]
            yft = ph.sb("yft", [128, 128], BF16)
            rd = ph.sb("rd", [128, 1])
            memset(vaug, vaug[:, :, :, 128:130], 1.0)

            def cons_k(M, idx, blk, pt):
                i = (idx + blk) % 2
                cp(kst[i], kst[i][:], pt[0:M, :], [pt])
                qknorm(kst[i], kst[i][:].rearrange("p (h d) -> p h d", h=4), 128, knw_s, scr)
                if mode == 'own':
                    store('sp', kp_v[lo][:, idx, blk * 512:(blk + 1) * 512], kst[i], kst[i][:])
                kb16 = kb16s[i]
                cp(kb16, kb16[:], kst[i][:], [kst[i]])

                def later():
                    p5 = pss[5]
                    p5b = p5[:].bitcast(BF16)
                    for hh in range(4):
                        tr(p5, p5b[:, hh * 128:(hh + 1) * 128], kb16, kb16[:, hh * 128:(hh + 1) * 128], identb, identb[:])
                    cp(kTst, kTst[:, idx, blk * 4:(blk + 1) * 4, :], p5b[:, 0:512].rearrange("p (h t) -> p h t", h=4), [p5])
                return later

            def cons_v(M, idx, blk, pt):
                i = (idx + blk) % 2
                cp(kst[i], kst[i][:], pt[0:M, :], [pt])
                if mode == 'own':
                    store('sp', vp_v[lo][:, idx, blk * 512:(blk + 1) * 512], kst[i], kst[i][:])
                cp(vaug, vaug[:, idx, blk * 4:(blk + 1) * 4, 0:128], kst[i][:].rearrange("p (h d) -> p h d", h=4), [kst[i]])

            def cons_q(M, idx, blk, pt):
                i = (idx + blk) % 2
                cp(kst[i], kst[i][:], pt[0:M, :], [pt])
                qknorm(kst[i], kst[i][:].rearrange("p (h d) -> p h d", h=4), 128, qnw_s, scr)
                kb16 = kb16s[i]
                act(kb16, kb16[:], kst[i][:], [kst[i]], AF.Identity, scale=128 ** -0.5)

                def later():
                    p5 = pss[5]
                    p5b = p5[:].bitcast(BF16)
                    for hh in range(4):
                        tr(p5, p5b[:, hh * 128:(hh + 1) * 128], kb16, kb16[:, hh * 128:(hh + 1) * 128], identb, identb[:])
                    cp(QT, QT[:, blk * 4:(blk + 1) * 4, idx * 128:(idx + 1) * 128],
                       p5b[:, 0:512].rearrange("p (h t) -> p h t", h=4), [p5])
                return later

            proj_tok(hT, psubs, w_in_v, b_in_row, C_FK, 2, cons_k)
            proj_tok(hT, psubs, w_in_v, b_in_row, C_FV, 2, cons_v)
            if osubs:
                proj_tok(hT, osubs, w_in_v, b_in_row, C_FQ, 2, cons_q)
            for t4 in range(4):
                c0_ = g * GT + t4 * 128
                store('sp', kT_scr[:, :, c0_:c0_ + 128].rearrange("h p t -> p h t"), kTst, kTst[:, t4, :, :],
                      extra_writes=["kT_scr"])
                store('sp', v_scr[:, :, 4 * g + t4, :].rearrange("h p c -> p h c"), vaug, vaug[:, t4, :, :],
                      extra_writes=["v_scr"])
            nkt = 4 * (g + 1)
            qis = out_tiles
            for h in (range(8) if qis else []):
                tk.dma('sp', KTh.k, KTh[:, 0:nkt * 128], kT_scr[h, :, 0:nkt * 128], reads=["kT_scr"], writes=[KTh.n])
                tk.dma('sp', Vh.k, Vh[:, 0:nkt, :], v_scr[h, :, 0:nkt, :], reads=["v_scr"], writes=[Vh.n])
                tk.dma('sp', fq3.k, fq3[:], F3_scr[:, h, g * GT:(g + 1) * GT], reads=["F3_scr"], writes=[fq3.n])
                def att_geom(kt):
                    d = kt - 4 * g
                    ql = [qi for qi in qis if qi >= d]
                    c0_ = max(0, d, min(qis)) * 128
                    return d, ql, c0_, GT - c0_

                def emit_qk(kt):
                    d, ql, c0_, n_ = att_geom(kt)
                    psn = pss[kt % 2]
                    mm(psn, psn[:, 0:n_], KTh[:, kt * 128:(kt + 1) * 128], QT[:, h, c0_:GT], [KTh, QT], True, False)
                    mm(psn, psn[:, 0:n_], ones_b[0:3, 0:128], fq3[0:3, c0_:GT], [ones_b, fq3], False, True)

                def emit_pv(kt):
                    d, ql, c0_, n_ = att_geom(kt)
                    psn = pss[kt % 2]
                    e_ = eT[kt % 2]
                    act(e_, e_[:, 0:n_], psn[:, 0:n_], [psn, nFk], AF.Exp, bias=nFk[:, h, kt:kt + 1])
                    if d >= 0 and d in ql:
                        tt(e_, e_[:, 0:128], e_[:, 0:128], maskTb[:], ALU.mult, [e_, maskTb], eng='pool')
                    for qi in ql:
                        po = pss[2 + qi]
                        mm(po, po[:, 0:130], e_[:, qi * 128 - c0_:(qi + 1) * 128 - c0_], Vh[:, kt, :], [e_, Vh],
                           kt == 0, kt == 4 * g + qi)

                kts = [kt for kt in range(nkt) if att_geom(kt)[1]]
                emit_qk(kts[0])
                for ki, kt in enumerate(kts):
                    if ki + 1 < len(kts):
                        emit_qk(kts[ki + 1])
                    emit_pv(kt)
                for qi in qis:
                    po = pss[2 + qi]
                    tk.op('dve', lambda e, po=po: e.reciprocal(rd[:], po[:, 128:129]), reads=[po.n], writes=[rd.n])
                    ts(yft, yft[:], po[:, 0:128], rd[:, 0:1], None, ALU.mult, None, [po, rd])
                    p7 = pss[7]
                    p7b = p7[:].bitcast(BF16)
                    tr(p7, p7b[:, 0:128], yft, yft[:], identb, identb[:])
                    cp(yfT, yfT[:, h, qi * 128:(qi + 1) * 128], p7b[:, 0:128], [p7])
            ph.close()

            if mode == 'own' and lo == 0:
                store('sp', dbg_ymT, ymT, ymT[:])
                store('sp', dbg_yfT, yfT, yfT[:])
            if mode == 'own':
                def ydst():
                    store('sp', yp_v[lo], xt, xt[:])

                dbg_hook[0] = (lambda: store('sp', dbg_x1.rearrange("(t p) d -> p t d", p=128), xt, xt[:])) if lo == 0 else None

                dense_tail(psubs, GT, hT, ymT, yfT, lambda idx: (xt, xt[:, idx, :]),
                           lambda idx, c, n: (grep, grep[:, 0, c:c + n]), lambda idx, c, n: (grep, grep[:, 1, c:c + n]),
                           False, ydst)
            elif mode == 'halo':
                phh = Phase()
                hTh = phh.sb("hTh", [128, KC, 128], BF16)
                ymTh = phh.sb("ymTh", [128, 8, 128], BF16)
                yfTh = phh.sb("yfTh", [128, 8, 128], BF16)
                cp(hTh, hTh[:], hT[:, :, 384:512], [hT])
                cp(ymTh, ymTh[:], ymT[:, :, 384:512], [ymT])
                cp(yfTh, yfTh[:], yfT[:, :, 384:512], [yfT])
                dbg_hook[0] = None
                dense_tail([(128, 0, 3)], 128, hTh, ymTh, yfTh, lambda idx: (xt, xt[:, idx, :]),
                           lambda idx, c, n: (grep, grep[:, 0, c:c + n]), lambda idx, c, n: (grep, grep[:, 1, c:c + n]),
                           False, lambda: None, None, True)
                ts(aprev, aprev[:], aprev[:], -1.0e30, 1.0e30, ALU.max, ALU.min, [aprev])
                ts(aprev, aprev[:], aprev[:], halo_valid[:, 0:1], None, ALU.mult, None, [aprev, halo_valid])
                phh.close()

        if stage >= 1:
            for h in range(4):
                for half in range(2):
                    pt = pss[7]
                    tr(pt, pt[:, 0:128], St, St[:, h, half * 128:(half + 1) * 128], ident, ident[:])
                    cp(Cst, Cst[:], pt[:, 0:128], [pt])
                    store('sp', C_p[h, half * 128:(half + 1) * 128, :], Cst, Cst[:])
                store('sp', n_p[h:h + 1, :].rearrange("o d -> d o"), St, St[:, h, 256:257])
            tt(Bc, Bc[:], Bc[:], Ac[:], ALU.add, [Bc, Ac])
            store('sp', m_p, Bc, Bc[:])
            if stage >= 3:
                for r in range(2):
                    store('sp', cv_p[r].rearrange("(c p) -> p c", p=128), aprev, aprev[:, :, r])

        php.close()
        if stage >= 4:
            phs = Phase()
            xst = phs.sb("xst", [NS, D])
            hTs = phs.sb("hTs", [128, KC, NS], BF16)
            ymTs = phs.sb("ymTs", [128, 8, NS], BF16)
            yfTs = phs.sb("yfTs", [128, 8, NS], BF16)
            gsm = phs.sb("gsm", [NS, 2, D], BF16)
            gms = phs.sb("gms", [8, NS])
            gfs = phs.sb("gfs", [8, NS])
            ksm = phs.sb("ksm", [NS, 1024])
            vsm = phs.sb("vsm", [NS, 1024])
            qsm = phs.sb("qsm", [NS, 1024])
            qsb = phs.sb("qsb", [NS, 1024], BF16)
            kmf = phs.sb("kmf", [NS, 512])
            qmf = phs.sb("qmf", [NS, 512])
            kms = phs.sb("kms", [NS, 512], BF16)
            qms = phs.sb("qms", [NS, 512], BF16)
            vms = phs.sb("vms", [NS, 1024], BF16)
            smoT = phs.sb("smoT", [128, 8, NS])
            scr16 = phs.sb("scr16", [NS, 512])
            b_moT = phs.sb("b_moT", [128, 8])
            hnwT = phs.sb("hnwT", [128, 8])
            sel_f = phs.sb("sel_f", [NS, NS, 128])
            sel_b = phs.sb("sel_b", [NS, NS, 128], BF16)
            ones_f = phs.sb("ones_f", [128, 128])
            load('sp', xst, xst[:], xs)
            tk.dma('sp', gsm.k, gsm[:], gsm_scr, reads=["gsm_scr"], writes=[gsm.n])
            load('sp', b_moT, b_moT[:], b_moT_in)
            load('sp', hnwT, hnwT[:], hnwT_in)
            load('sp', sel_f, sel_f[:], sel_in)
            cp(sel_b, sel_b[:], sel_f[:], [sel_f])
            memset(ones_f, ones_f[:], 1.0)
            norm_tile(xst, xst[:], NS, A1, 0, hTs, 0, None)
            ssubs = [(NS, 0, 0)]
            gates(wg_m, 0, gms, hTs, NS)
            gates(wg_f, 1, gfs, hTs, NS)
            logsig(gfs, gfs[:], gfs[:], [gfs])
            store('sp', lf_s.rearrange("i h -> h i"), gfs, gfs[:])

            def s_k(M, idx, blk, pt):
                cp(ksm, ksm[:, blk * 512:(blk + 1) * 512], pt[0:M, :], [pt])
                qknorm(ksm, ksm[:, blk * 512:(blk + 1) * 512].rearrange("p (h d) -> p h d", h=4), NS, knw_s, scr16)

            def s_q(M, idx, blk, pt):
                cp(qsm, qsm[:, blk * 512:(blk + 1) * 512], pt[0:M, :], [pt])
                qknorm(qsm, qsm[:, blk * 512:(blk + 1) * 512].rearrange("p (h d) -> p h d", h=4), NS, qnw_s, scr16)
                ts(qsm, qsm[:, blk * 512:(blk + 1) * 512], qsm[:, blk * 512:(blk + 1) * 512], 128 ** -0.5, None, ALU.mult,
                   None, [qsm])
                cp(qsb, qsb[:, blk * 512:(blk + 1) * 512], qsm[:, blk * 512:(blk + 1) * 512], [qsm])

            def s_mk(M, idx, blk, pt):
                act(kmf, kmf[:], pt[0:M, :], [pt], AF.Identity, scale=128 ** -0.5)
                cp(kms, kms[:], kmf[:], [kmf])

            def s_mq(M, idx, blk, pt):
                cp(qmf, qmf[:], pt[0:M, :], [pt])
                cp(qms, qms[:], qmf[:], [qmf])

            proj_tok(hTs, ssubs, w_in_v, b_in_row, C_FK, 2, s_k)
            store('sp', k_s, ksm, ksm[:])
            proj_tok(hTs, ssubs, w_in_v, b_in_row, C_FV, 2,
                     lambda M, idx, blk, pt: cp(vsm, vsm[:, blk * 512:(blk + 1) * 512], pt[0:M, :], [pt]))
            store('sp', v_s, vsm, vsm[:])
            proj_tok(hTs, ssubs, w_in_v, b_in_row, C_FQ, 2, s_q)
            proj_tok(hTs, ssubs, w_in_v, b_in_row, C_MK, 1, s_mk)
            proj_tok(hTs, ssubs, w_in_v, b_in_row, C_MQ, 1, s_mq)
            proj_tok(hTs, ssubs, w_in_v, b_in_row, C_MV, 2,
                     lambda M, idx, blk, pt: cp(vms, vms[:, blk * 512:(blk + 1) * 512], pt[0:M, :], [pt]))
            for blk in range(2):
                wv, wn, _ = load_w(w_in_v[:, :, C_MO + blk * 512:C_MO + (blk + 1) * 512], [128, KC, 512])
                for ct in range(4):
                    pt = pss[ct]
                    for kc in range(KC):
                        mm(pt, pt[:, 0:NS], wv[:, kc, ct * 128:(ct + 1) * 128], hTs[:, kc, :], [wn, hTs], kc == 0, kc == KC - 1)
                    ch = blk * 4 + ct
                    act(smoT, smoT[:, ch, :], pt[:, 0:NS], [pt, b_moT], AF.Sigmoid, bias=b_moT[:, ch:ch + 1])

            phm = Phase()
            m0s = phm.sb("m0s", [NS, 4])
            n0s = phm.sb("n0s", [NS, 512])
            gts = phm.sb("gts", [NS, 8])
            sm = phm.sb("sm", [NS, 8, 4])
            bcs = phm.sb("bcs", [NS, 16])
            t16 = phm.sb("t16", [NS, 512])
            vTs = phm.sb("vTs", [128, 8, NS], BF16)
            Ct = phm.sb("Ct", [128, 8, 128])
            Cu = phm.sb("Cu", [128, 8, 128])
            scw = phm.sb("scw", [128, 16])
            wv8 = phm.sb("wv8", [128, 8])
            c0q = phm.sb("c0q", [128, 8])
            t8 = phm.sb("t8", [128, 8])
            hAll = phm.sb("hAll", [128, 8, NS])
            hsq = phm.sb("hsq", [128, 8, NS])
            ssum = phm.sb("ssum", [128, 4, NS])
            load('sp', m0s, m0s[:], st_m)
            load('sp', n0s, n0s[:], st_n)
            pt = pss[6]
            tr(pt, pt[0:NS, 0:8], gms, gms[:], ident, ident[0:8, 0:8])
            cp(gts, gts[:], pt[0:NS, 0:8], [pt])
            logsig(gts, gts[:, 4:8], gts[:, 4:8], [gts])
            tt(sm, sm[:, 0, :], m0s[:], gts[:, 4:8], ALU.add, [m0s, gts])
            tt(sm, sm[:, 1, :], sm[:, 0, :], gts[:, 0:4], ALU.max, [sm, gts])
            store('sp', m_s, sm, sm[:, 1, :])
            tt(sm, sm[:, 2, :], sm[:, 0, :], sm[:, 1, :], ALU.subtract, [sm])
            act(sm, sm[:, 2, :], sm[:, 2, :], [sm], AF.Exp)
            tt(sm, sm[:, 3, :], gts[:, 0:4], sm[:, 1, :], ALU.subtract, [sm, gts])
            act(sm, sm[:, 3, :], sm[:, 3, :], [sm], AF.Exp)
            act(sm, sm[:, 4, :], sm[:, 1, :], [sm], AF.Exp, scale=-1.0)
            q3 = qmf[:].rearrange("p (h d) -> p h d", h=4)
            k3 = kmf[:].rearrange("p (h d) -> p h d", h=4)
            n3 = n0s[:].rearrange("p (h d) -> p h d", h=4)
            t3 = t16[:].rearrange("p (h d) -> p h d", h=4)
            tt(t16, t3, q3, k3, ALU.mult, [qmf, kmf])
            tk.op('dve', lambda e: e.tensor_reduce(sm[:, 5, :], t3, AX.X, ALU.add), reads=[t16.n], writes=[sm.n])
            tt(t16, t3, q3, n3, ALU.mult, [qmf, n0s])
            tk.op('dve', lambda e: e.tensor_reduce(sm[:, 6, :], t3, AX.X, ALU.add), reads=[t16.n], writes=[sm.n])
            tt(sm, sm[:, 7, :], sm[:, 3, :], sm[:, 5, :], ALU.mult, [sm])
            tt(sm, sm[:, 6, :], sm[:, 6, :], sm[:, 2, :], ALU.mult, [sm])
            tt(sm, sm[:, 6, :], sm[:, 6, :], sm[:, 7, :], ALU.add, [sm])
            act(sm, sm[:, 6, :], sm[:, 6, :], [sm], AF.Abs)
            tt(sm, sm[:, 6, :], sm[:, 6, :], sm[:, 4, :], ALU.max, [sm])
            tk.op('dve', lambda e: e.reciprocal(sm[:, 6, :], sm[:, 6, :]), reads=[sm.n], writes=[sm.n])
            cp(bcs, bcs[:, 0:4], sm[:, 2, :], [sm])
            cp(bcs, bcs[:, 4:8], sm[:, 3, :], [sm])
            cp(bcs, bcs[:, 8:12], sm[:, 6, :], [sm])
            cp(bcs, bcs[:, 12:16], sm[:, 7, :], [sm])
            tt(n0s, n3, n3, sm[:, 2, :].unsqueeze(2).to_broadcast([NS, 4, 128]), ALU.mult, [n0s, sm])
            tt(t16, t3, k3, sm[:, 3, :].unsqueeze(2).to_broadcast([NS, 4, 128]), ALU.mult, [kmf, sm])
            tt(n0s, n0s[:], n0s[:], t16[:], ALU.add, [n0s, t16])
            store('sp', n_s, n0s, n0s[:])
            ptb = pss[6][:].bitcast(BF16)
            for c8 in range(8):
                tr(pss[6], ptb[:, c8 * NS:(c8 + 1) * NS], vms, vms[:, c8 * 128:(c8 + 1) * 128], identb, identb[0:NS, 0:NS])
            cp(vTs, vTs[:].rearrange("p c i -> p (c i)"), ptb[:, 0:8 * NS], [pss[6]])
            for i in range(NS):
                tk.dma('sp', Ct.k, Ct[:], st_C[i].rearrange("h (a p) d -> p (h a) d", p=128), writes=[Ct.n])
                mm(pss[0], pss[0][:, :], sel_b[:, i, :], kms[:, :], [sel_b, kms], True, True)
                mm(pss[2], pss[2][:, :], sel_b[:, i, :], qms[:, :], [sel_b, qms], True, True)
                mm(pss[1], pss[1][:, 0:16], sel_f[:, i, :], bcs[:, :], [sel_f, bcs], True, True)
                cp(scw, scw[:], pss[1][:, 0:16], [pss[1]])
                C4 = Ct[:].rearrange("p (h a) d -> p h a d", h=4)
                U4 = Cu[:].rearrange("p (h a) d -> p h a d", h=4)
                tt(Cu, U4, C4, pss[2][:, :].rearrange("p (h d) -> p h d", h=4).unsqueeze(2).to_broadcast([128, 4, 2, 128]),
                   ALU.mult, [Ct, pss[2]])
                tk.op('dve', lambda e: e.tensor_reduce(c0q[:], Cu[:], AX.X, ALU.add), reads=[Cu.n], writes=[c0q.n])
                v3 = vTs[:, :, i].rearrange("p (h a) -> p h a", h=4)
                tt(wv8, wv8[:].rearrange("p (h a) -> p h a", h=4), v3, scw[:, 4:8].unsqueeze(2).to_broadcast([128, 4, 2]),
                   ALU.mult, [vTs, scw])
                tt(t8, t8[:].rearrange("p (h a) -> p h a", h=4), v3, scw[:, 12:16].unsqueeze(2).to_broadcast([128, 4, 2]),
                   ALU.mult, [vTs, scw])
                tt(c0q, c0q[:].rearrange("p (h a) -> p h a", h=4), c0q[:].rearrange("p (h a) -> p h a", h=4),
                   scw[:, 0:4].unsqueeze(2).to_broadcast([128, 4, 2]), ALU.mult, [c0q, scw])
                tt(c0q, c0q[:], c0q[:], t8[:], ALU.add, [c0q, t8])
                tt(hAll, hAll[:, :, i].rearrange("p (h a) -> p h a", h=4), c0q[:].rearrange("p (h a) -> p h a", h=4),
                   scw[:, 8:12].unsqueeze(2).to_broadcast([128, 4, 2]), ALU.mult, [c0q, scw])
                tt(Ct, C4, C4, scw[:, 0:4].unsqueeze(2).unsqueeze(3).to_broadcast([128, 4, 2, 128]), ALU.mult, [Ct, scw])
                tt(Cu, U4, pss[0][:, :].rearrange("p (h d) -> p h d", h=4).unsqueeze(2).to_broadcast([128, 4, 2, 128]),
                   wv8[:].rearrange("p (h a) -> p h a", h=4).unsqueeze(3).to_broadcast([128, 4, 2, 128]), ALU.mult,
                   [pss[0], wv8, Cu])
                tt(Ct, Ct[:], Ct[:], Cu[:], ALU.add, [Ct, Cu], eng='pool')
                store('sp', C_s[i].rearrange("h (a p) d -> p (h a) d", p=128), Ct, Ct[:])
            tt(hsq, hsq[:], hAll[:], hAll[:], ALU.mult, [hAll])
            mm(pss[3], pss[3][:, 0:8 * NS], ones_f[:], hsq[:].rearrange("p c i -> p (c i)"), [ones_f, hsq], True, True)
            p3v = pss[3][:, 0:8 * NS].rearrange("p (h a i) -> p h a i", h=4, a=2)
            cp(ssum, ssum[:], p3v[:, :, 0, :], [pss[3]])
            tt(ssum, ssum[:], ssum[:], p3v[:, :, 1, :], ALU.add, [ssum, pss[3]])
            rsqrt_mean(ssum, ssum[:], ssum[:], [ssum], 256)
            h4 = hAll[:].rearrange("p (h a) i -> p h a i", h=4)
            tt(hAll, h4, h4, ssum[:].unsqueeze(2).to_broadcast([128, 4, 2, NS]), ALU.mult, [hAll, ssum])
            tt(hAll, hAll[:], hAll[:], hnwT[:].unsqueeze(2).to_broadcast([128, 8, NS]), ALU.mult, [hAll, hnwT])
            tt(ymTs, ymTs[:], hAll[:], smoT[:], ALU.mult, [hAll, smoT])
            phm.close()

            pha = Phase()
            NP = NS * 16
            ptb_i = pha.sb("ptb_i", [128, NP], I32)
            ptb_f = pha.sb("ptb_f", [128, NP])
            pidx = pha.sb("pidx", [128, 1])
            idx_all = pha.sb("idx_all", [128, NP], I32)
            pcol = pha.sb("pcol", [128, 2], I32)
            LF = pha.sb("LF", [128, 2, 1024])
            tot = pha.sb("tot", [128, 2, 8])
            later = pha.sb("later", [128, 2, 8])
            Lmat = pha.sb("Lmat", [128, 128])
            Emat = pha.sb("Emat", [NS, 2, 128])
            gft = pha.sb("gft", [NS, 8])
            DT = pha.sb("DT", [128, NP, 8])
            sc_all = pha.sb("sc_all", [128, NP, 8])
            e_all = pha.sb("e_all", [128, NP, 8], BF16)
            qbf = pha.sb("qbf", [128, 1024], BF16)
            NPB = 4
            Kpg = [pha.sb("Kpg%d" % q_, [128, 1024], BF16) for q_ in range(NPB)]
            Vpg = [pha.sb("Vpg%d" % q_, [128, 1024], BF16) for q_ in range(NPB)]
            prod = pha.sb("prod", [128, 1024], BF16)
            bdm = pha.sb("bdm", [8, 1024])
            hsel = pha.sb("hsel", [8, NS, NS])
            msk = pha.sb("msk", [8, 1024])
            pdd = pha.sb("pdd", [8, 8])
            pdc = pha.sb("pdc", [8, 1])
            enew = pha.sb("enew", [NS, 8])
            yacc = pha.sb("yacc", [NS, 1024])
            dacc = pha.sb("dacc", [NS, 8])
            yfb = pha.sb("yfb", [NS, 1024], BF16)
            zscan = pha.sb("zscan", [128, 128])
            memset(zscan, zscan[:], 0.0)
            load('sp', ptb_i, ptb_i[:], ptab[0:1, :].to_broadcast([128, NP]))
            load('sp', pidx, pidx[:], pidx_in)
            load('sp', pcol, pcol[:], ptab_col)
            load('sp', Lmat, Lmat[:], Lmat_in)
            load('sp', Emat, Emat[:], Emat_in)
            load('sp', bdm, bdm[:], bdm_in)
            load('sp', hsel, hsel[:], hsel_in)
            cp(ptb_f, ptb_f[:], ptb_i[:], [ptb_i])
            ts(ptb_f, ptb_f[:], ptb_f[:], 128.0, pidx[:, 0:1], ALU.mult, ALU.add, [ptb_f, pidx])
            cp(idx_all, idx_all[:], ptb_f[:], [ptb_f])

            def gather(dst, dst_ap, src_ap, idx_buf, idx_ap):
                tk._wait('pool', tk._deps([idx_buf.n], [dst.n]))
                key = dst.k
                if key not in tk.dsem:
                    tk.dsem[key] = es.enter_context(nc.semaphore('d_' + key))
                    tk.dcnt[key] = 0
                ins = nc.gpsimd.indirect_dma_start(out=dst_ap, out_offset=None, in_=src_ap,
                                                   in_offset=bass.IndirectOffsetOnAxis(ap=idx_ap, axis=0))
                tk.dcnt[key] += 16
                ins.then_inc(tk.dsem[key], 16)
                tk._commit((key, tk.dcnt[key]), [idx_buf.n], [dst.n])

            for half in range(2):
                gather(LF, LF[:, half, :], cache_lf[:, :], pcol, pcol[:, half:half + 1])
            for half in range(2):
                for h in range(8):
                    v2 = LF[:, half, :].rearrange("p (k h) -> p h k", h=8)[:, h, :]
                    tk.op('dve', lambda e, v2=v2: e.tensor_tensor_scan(v2, v2, zscan[:], 0.0, ALU.add, ALU.add),
                          reads=[LF.n, zscan.n], writes=[LF.n])
            LF4 = LF[:].rearrange("p a (k h) -> p a k h", h=8)
            cp(tot, tot[:], LF4[:, :, 127, :], [LF])
            tt(LF, LF4, tot[:].unsqueeze(2).to_broadcast([128, 2, 128, 8]), LF4, ALU.subtract, [tot, LF])
            mm(pss[0], pss[0][:, 0:16], Lmat[:], tot[:].rearrange("p a h -> p (a h)"), [Lmat, tot], True, True)
            cp(later, later[:].rearrange("p a h -> p (a h)"), pss[0][:, 0:16], [pss[0]])
            tr(pss[1], pss[1][0:NS, 0:8], gfs, gfs[:], ident, ident[0:8, 0:8])
            cp(gft, gft[:], pss[1][0:NS, 0:8], [pss[1]])
            for half in range(2):
                mm(pss[2], pss[2][:, half * 8:(half + 1) * 8], Emat[:, half, :], gft[:], [Emat, gft], True, True)
            tt(later, later[:].rearrange("p a h -> p (a h)"), later[:].rearrange("p a h -> p (a h)"), pss[2][:, 0:16], ALU.add,
               [later, pss[2]])
            tt(LF, LF4, LF4, later[:].unsqueeze(2).to_broadcast([128, 2, 128, 8]), ALU.add, [LF, later])
            for half in range(2):
                for h in range(8):
                    pt = pss[4 + (h % 2)]
                    tr(pt, pt[:, 0:128], LF, LF4[:, half, :, h], ident, ident[:])
                    cp(DT, DT[:, half * 128:(half + 1) * 128, h], pt[:, 0:128], [pt])
            tt(scr16, scr16[:], qsm[:, 0:512], ksm[:, 0:512], ALU.mult, [qsm, ksm])
            tk.op('dve', lambda e: e.tensor_reduce(enew[:, 0:4], scr16[:].rearrange("p (h d) -> p h d", h=4), AX.X, ALU.add),
                  reads=[scr16.n], writes=[enew.n])
            tt(scr16, scr16[:], qsm[:, 512:1024], ksm[:, 512:1024], ALU.mult, [qsm, ksm])
            tk.op('dve', lambda e: e.tensor_reduce(enew[:, 4:8], scr16[:].rearrange("p (h d) -> p h d", h=4), AX.X, ALU.add),
                  reads=[scr16.n], writes=[enew.n])
            act(enew, enew[:], enew[:], [enew], AF.Exp)
            for i in range(NS):
                for hb in range(2):
                    mm(pss[hb], pss[hb][:, :], sel_b[:, i, :], qsb[:, hb * 512:(hb + 1) * 512], [sel_b, qsb], True, True)
                    cp(qbf, qbf[:, hb * 512:(hb + 1) * 512], pss[hb][:, :], [pss[hb]])
                for pg in range(16):
                    j = i * 16 + pg
                    gather(Kpg[j % NPB], Kpg[j % NPB][:, :], cache_k[:, :], idx_all, idx_all[:, j:j + 1])
                    gather(Vpg[j % NPB], Vpg[j % NPB][:, :], cache_v[:, :], idx_all, idx_all[:, j:j + 1])
                    tt(prod, prod[:], Kpg[j % NPB][:], qbf[:], ALU.mult, [Kpg[j % NPB], qbf])
                    tk.op('dve', lambda e, j=j: e.tensor_reduce(sc_all[:, j, :], prod[:].rearrange("p (h d) -> p h d", h=8),
                                                                AX.X, ALU.add), reads=[prod.n], writes=[sc_all.n])
                    tt(sc_all, sc_all[:, j, :], sc_all[:, j, :], DT[:, j, :], ALU.add, [sc_all, DT])
                    act(e_all, e_all[:, j, :], sc_all[:, j, :], [sc_all], AF.Exp)
                    for hb in range(2):
                        mm(pss[2 + hb], pss[2 + hb][0:8, :], e_all[:, j, :], Vpg[j % NPB][:, hb * 512:(hb + 1) * 512],
                           [e_all, Vpg[j % NPB]], pg == 0, pg == 15)
                    mm(pss[4], pss[4][0:8, 0:1], e_all[:, j, :], ones_b[:, 0:1], [e_all, ones_b], pg == 0, pg == 15)
                for hb in range(2):
                    tt(msk, msk[:, hb * 512:(hb + 1) * 512], pss[2 + hb][0:8, :], bdm[:, hb * 512:(hb + 1) * 512], ALU.mult,
                       [pss[2 + hb], bdm])
                cp(pdc, pdc[:], pss[4][0:8, 0:1], [pss[4]])
                ts(pdd, pdd[:], ident[0:8, 0:8], pdc[:, 0:1], None, ALU.mult, None, [ident, pdc])
                for hb in range(2):
                    mm(pss[5 + hb], pss[5 + hb][0:NS, :], hsel[:, i, :], msk[:, hb * 512:(hb + 1) * 512], [hsel, msk],
                       i == 0, i == NS - 1)
                mm(pss[7], pss[7][0:NS, 0:8], hsel[:, i, :], pdd[:], [hsel, pdd], i == 0, i == NS - 1)
            for hb in range(2):
                cp(yacc, yacc[:, hb * 512:(hb + 1) * 512], pss[5 + hb][0:NS, :], [pss[5 + hb]])
            tt(dacc, dacc[:], pss[7][0:NS, 0:8], enew[:], ALU.add, [pss[7], enew])
            tk.op('dve', lambda e: e.reciprocal(dacc[:], dacc[:]), reads=[dacc.n], writes=[dacc.n])
            v3s = vsm[:].rearrange("p (h d) -> p h d", h=8)
            y3s = yacc[:].rearrange("p (h d) -> p h d", h=8)
            tt(vsm, v3s, v3s, enew[:].unsqueeze(2).to_broadcast([NS, 8, 128]), ALU.mult, [vsm, enew])
            tt(yacc, yacc[:], yacc[:], vsm[:], ALU.add, [yacc, vsm])
            tt(yacc, y3s, y3s, dacc[:].unsqueeze(2).to_broadcast([NS, 8, 128]), ALU.mult, [yacc, dacc])
            cp(yfb, yfb[:], yacc[:], [yacc])
            ptb = pss[0][:].bitcast(BF16)
            for c8 in range(8):
                tr(pss[0], ptb[:, c8 * NS:(c8 + 1) * NS], yfb, yfb[:, c8 * 128:(c8 + 1) * 128], identb, identb[0:NS, 0:NS])
            cp(yfTs, yfTs[:].rearrange("p c i -> p (c i)"), ptb[:, 0:8 * NS], [pss[0]])
            pha.close()

            def ydst_s():
                store('sp', y_s, xst, xst[:])

            def post_s(aS, ph_):
                cvst = ph_.sb("cvst", [NS, 512])
                for fb in range(FC // 4):
                    pt = pss[4 + fb % 2]
                    for ci in range(4):
                        tr(pt, pt[0:NS, ci * 128:(ci + 1) * 128], aS, aS[:, fb * 4 + ci, :], ident, ident[:])
                    cp(cvst, cvst[:], pt[0:NS, :], [pt])
                    store('sp', cv_s[:, 1, fb * 512:(fb + 1) * 512], cvst, cvst[:])

            dbg_hook[0] = None
            dense_tail(ssubs, NS, hTs, ymTs, yfTs, lambda idx: (xst, xst[:]),
                       lambda idx, c, n: (gsm, gsm[:, 0, c:c + n]), lambda idx, c, n: (gsm, gsm[:, 1, c:c + n]),
                       True, ydst_s, post_s)
            tk.dma('sp', "cvs0", cv_s[:, 0, :], st_conv[:, 1, :])
            phs.close()

        tk.finish()
    return nc


_CACHE = {}


def _prep_inputs(inp):
    f = np.float32
    w = {k: np.asarray(v) for k, v in inp.items()}
    ident = np.eye(128, dtype=f)
    maskT = np.triu(np.ones((128, 128), f))
    sel = np.zeros((NS, NS, 128), f)
    for i in range(NS):
        sel[i, i, :] = 1.0
    hsel = np.zeros((8, NS, NS), f)
    for i in range(NS):
        hsel[:, i, i] = 1.0
    bdm = np.repeat(np.eye(8, dtype=f), 128, axis=1)
    pidx = np.arange(128, dtype=f)[:, None].copy()
    Lmat = np.zeros((128, 128), f)
    for a in range(128):
        for b_ in range(128):
            if a // 16 == b_ // 16 and a > b_:
                Lmat[a, b_] = 1.0
    Emat = np.zeros((NS, 2, 128), f)
    for half in range(2):
        for j in range(128):
            Emat[half * 8 + j // 16, half, j] = 1.0
    b_in = w['b_in'][0]
    b_gT = np.concatenate([b_in[C_GA:C_GA + D].reshape(16, 128).T, b_in[C_GB:C_GB + D].reshape(16, 128).T], axis=1)
    b_gates = np.stack([b_in[C_MI:C_MI + 8], b_in[C_FF:C_FF + 8]], axis=1)
    common = dict(
        w_ada=w['w_ada'][0], w_in=w['w_in'][0], w_pa=w['w_proj_a'][0], w_pb=w['w_proj_b'][0], w_out=w['w_out'][0],
        w_f1=w['w_ffn_in'][0], w_f2=w['w_ffn_out'][0],
        b_adaT=np.ascontiguousarray(w['b_ada'][0].reshape(96, 128).T), b_ada_row=w['b_ada'][0][None, :].copy(),
        n1wT=np.ascontiguousarray(w['norm1_w'][0].reshape(16, 128).T), n2wT=np.ascontiguousarray(w['norm2_w'][0].reshape(16, 128).T),
        b_in_row=b_in[None, :].copy(), b_gT=np.ascontiguousarray(b_gT), b_gates=np.ascontiguousarray(b_gates),
        hnw_rep=np.ascontiguousarray(np.broadcast_to(w['m_hnorm_w'][0][None, :], (128, 1024))),
        qnw_rep=np.ascontiguousarray(np.broadcast_to(w['f_qnorm_w'][0][None, :], (128, 128))),
        knw_rep=np.ascontiguousarray(np.broadcast_to(w['f_knorm_w'][0][None, :], (128, 128))),
        cwT=np.ascontiguousarray(w['conv_w'][0].reshape(3, FC, 128).transpose(2, 1, 0)),
        cbT=np.ascontiguousarray(w['conv_b'][0].reshape(FC, 128).T),
        ident=ident, maskT=maskT, sel=sel, hsel=hsel, bdm=bdm, pidx=pidx, Lmat=Lmat, Emat=Emat,
        b_moT=np.ascontiguousarray(b_in[C_MO:C_MO + 1024].reshape(8, 128).T),
        hnwT=np.ascontiguousarray(w['m_hnorm_w'][0].reshape(8, 128).T),
        cache_k=w['cache_k'][0].reshape(NPOOL * 128, 1024), cache_v=w['cache_v'][0].reshape(NPOOL * 128, 1024),
        cache_lf=w['cache_logf'][0].reshape(NPOOL, 1024),
    )
    maps = []
    for c in range(8):
        sl = slice(c * NS, (c + 1) * NS)
        b, j = c // 4, c % 4
        nvalid = 1024 * (j + 1)
        xwin = np.zeros((T, D), f)
        xwin[T - nvalid:] = w['x_prompt'][b, :nvalid]
        vt = (np.arange(32) >= 32 - 8 * (j + 1)).astype(f)
        kmask = np.ascontiguousarray(np.broadcast_to(((vt - 1.0) * 30000.0)[None, :], (128, 32)))
        vg = vt.reshape(NG, 4)[:, 0]
        vgate = np.ascontiguousarray(np.broadcast_to(np.stack([vg, (vg - 1.0) * 1.0e4], axis=1)[None], (8, NG, 2)))
        halo_valid = np.full((128, 1), 1.0 if j > 0 else 0.0, f)
        cc = np.concatenate([w['c_sample'][sl], w['c_prompt'][b:b + 1]], axis=0)
        cT = np.ascontiguousarray(cc.reshape(NS + 1, KC, 128).transpose(2, 1, 0))
        pt = w['page_table'][sl].astype(np.int32)
        m = dict(common)
        m.update(
            xw=xwin, kmask=kmask, vgate=vgate.astype(f), halo_valid=halo_valid,
            xs=np.ascontiguousarray(w['x_sample'][sl, 0, :]), cT=cT,
            ptab=pt.reshape(1, NS * 16).copy(),
            ptab_col=np.ascontiguousarray(pt.reshape(2, 128).T),
            st_C=np.ascontiguousarray(w['state_C'][0, sl]), st_n=np.ascontiguousarray(w['state_n'][0, sl].reshape(NS, 512)),
            st_m=np.ascontiguousarray(w['state_m'][0, sl]),
            st_convT=np.ascontiguousarray(w['state_conv'][0, sl].reshape(NS, 2, FC, 128).transpose(3, 2, 0, 1)),
            st_conv=np.ascontiguousarray(w['state_conv'][0, sl]),
        )
        maps.append(m)
    return maps


def kernel(**inputs):
    stage = inputs.pop('_stage', 99)
    ng = inputs.pop('_ng', NG)
    if (stage, ng) not in _CACHE:
        _CACHE[(stage, ng)] = build_program(stage, ng)
    nc = _CACHE[(stage, ng)]
    maps = _prep_inputs(inputs)
    res = run_bass_kernel_spmd(nc, maps, core_ids=list(range(8)))
    r = res.results
    f = np.float32
    global _LAST
    _LAST = r

    def cat(name):
        return np.concatenate([np.asarray(r[c][name]) for c in range(8)], axis=0)

    def catp(name, shape):
        return np.stack([np.concatenate([np.asarray(r[b * 4 + j][name]) for j in range(4)], axis=0) for b in range(2)]).reshape(shape).astype(f)

    y_prompt = catp('y_p', (2, T, D))
    y_sample = cat('y_s').reshape(128, 1, D).astype(f)
    k_prompt = catp('k_p', (1, 2, T, 8, 128))
    v_prompt = catp('v_p', (1, 2, T, 8, 128))
    lf_prompt = catp('lf_p', (1, 2, T, 8))
    C_prompt = np.stack([r[4 * b + 3]['C_p'] for b in range(2)]).reshape(1, 2, 4, 256, 128).astype(f)
    n_prompt = np.stack([r[4 * b + 3]['n_p'] for b in range(2)]).reshape(1, 2, 4, 128).astype(f)
    m_prompt = np.stack([r[4 * b + 3]['m_p'] for b in range(2)]).reshape(1, 2, 4).astype(f)
    cv_prompt = np.stack([r[4 * b + 3]['cv_p'] for b in range(2)]).reshape(1, 2, 2, DFF).astype(f)
    k_sample = cat('k_s').reshape(1, 128, 1, 8, 128).astype(f)
    v_sample = cat('v_s').reshape(1, 128, 1, 8, 128).astype(f)
    lf_sample = cat('lf_s').reshape(1, 128, 1, 8).astype(f)
    C_sample = cat('C_s').reshape(1, 128, 4, 256, 128).astype(f)
    n_sample = cat('n_s').reshape(1, 128, 4, 128).astype(f)
    m_sample = cat('m_s').reshape(1, 128, 4).astype(f)
    cv_sample = cat('cv_s').reshape(1, 128, 2, DFF).astype(f)
    return (y_prompt, y_sample, k_prompt, v_prompt, lf_prompt, C_prompt, n_prompt, m_prompt, cv_prompt,
            k_sample, v_sample, lf_sample, C_sample, n_sample, m_sample, cv_sample)
```

```python
import contextlib
import numpy as np
import concourse.bass as bass
import concourse.mybir as mybir
from concourse.bass_utils import run_bass_kernel_spmd

F32 = mybir.dt.float32
BF16 = mybir.dt.bfloat16
I32 = mybir.dt.int32
AF = mybir.ActivationFunctionType
ALU = mybir.AluOpType
AX = mybir.AxisListType

D = 2048
KC = 16
T = 4096
NG = 8
GT = 512
NS = 16
DFF = 5632
FC = 44
PIN = 10256
EPS = 1e-6
NPOOL = 2560
C_MQ, C_MK, C_MV, C_MO, C_MI, C_MF, C_FQ, C_FK, C_FV, C_FF, C_GA, C_GB = (
    0, 512, 1024, 2048, 3072, 3076, 3080, 4104, 5128, 6152, 6160, 8208)


class Trk:
    def __init__(self, nc, es):
        self.nc = nc
        self.es = es
        self.eng = {'pe': nc.tensor, 'act': nc.scalar, 'dve': nc.vector, 'pool': nc.gpsimd, 'sp': nc.sync}
        self.sem = {}
        self.cnt = {}
        for e in self.eng:
            self.sem[e] = es.enter_context(nc.semaphore('s_' + e))
            self.cnt[e] = 0
        self.waited = {e: {} for e in self.eng}
        self.bw = {}
        self.br = {}
        self.dsem = {}
        self.dcnt = {}
        self.ninst = 0

    def _deps(self, reads, writes):
        deps = []
        for r in reads:
            if r in self.bw:
                deps.append(self.bw[r])
        for w in writes:
            if w in self.bw:
                deps.append(self.bw[w])
            deps.extend(self.br.get(w, []))
        return deps

    def _wait(self, e, deps):
        need = {}
        for (k, v) in deps:
            if k == e and e == 'pe':
                continue
            if v > need.get(k, 0):
                need[k] = v
        for k, v in need.items():
            if self.waited[e].get(k, 0) >= v:
                continue
            s = self.sem[k] if k in self.sem else self.dsem[k]
            self.eng[e].wait_ge(s, v)
            self.waited[e][k] = v

    def _commit(self, tok, reads, writes):
        for r in reads:
            self.br.setdefault(r, []).append(tok)
        for w in writes:
            self.bw[w] = tok
            self.br[w] = []

    def op(self, e, fn, reads=(), writes=(), signal=True):
        self._wait(e, self._deps(reads, writes))
        ins = fn(self.eng[e])
        if signal:
            self.cnt[e] += 1
            ins.then_inc(self.sem[e], 1)
            tok = (e, self.cnt[e])
        else:
            tok = (e, self.cnt[e] + 1)
        self._commit(tok, reads, writes)
        self.ninst += 1

    def dma(self, q, key, out, in_, reads=(), writes=(), **kw):
        self._wait(q, self._deps(reads, writes))
        if key not in self.dsem:
            self.dsem[key] = self.es.enter_context(self.nc.semaphore('d_' + key))
            self.dcnt[key] = 0
        ins = self.eng[q].dma_start(out=out, in_=in_, **kw)
        self.dcnt[key] += 16
        ins.then_inc(self.dsem[key], 16)
        self._commit((key, self.dcnt[key]), reads, writes)
        self.ninst += 1

    def barrier(self):
        for e in self.eng:
            for k in self.eng:
                if k != e and self.cnt[k] > self.waited[e].get(k, 0):
                    self.eng[e].wait_ge(self.sem[k], self.cnt[k])
                    self.waited[e][k] = self.cnt[k]
            for k, sm_ in self.dsem.items():
                if self.dcnt[k] > self.waited[e].get(k, 0):
                    self.eng[e].wait_ge(sm_, self.dcnt[k])
                    self.waited[e][k] = self.dcnt[k]

    def finish(self):
        for k, s in self.dsem.items():
            self.eng['sp'].wait_ge(s, self.dcnt[k])
        for e in self.eng:
            if e != 'sp' and self.cnt[e] > 0:
                self.eng['sp'].wait_ge(self.sem[e], self.cnt[e])


def build_program(stage=99, ng=NG):
    nc = bass.Bass("TRN2", target_bir_lowering=False)

    def din(name, shape, dt=F32):
        return nc.dram_tensor(name, list(shape), dt, kind="ExternalInput").ap()

    def dout(name, shape, dt=F32):
        return nc.dram_tensor(name, list(shape), dt, kind="ExternalOutput").ap()

    def dscr(name, shape, dt=F32):
        return nc.dram_tensor(name, list(shape), dt, kind="Internal").ap()

    xw = din("xw", [T, D])
    xs = din("xs", [NS, D])
    cT_in = din("cT", [128, KC, NS + 1])
    w_ada = din("w_ada", [D, 6 * D])
    w_in = din("w_in", [D, PIN])
    w_pa = din("w_pa", [1024, D])
    w_pb = din("w_pb", [1024, D])
    w_out = din("w_out", [D, D])
    w_f1 = din("w_f1", [D, 2 * DFF])
    w_f2 = din("w_f2", [DFF, D])
    b_adaT = din("b_adaT", [128, 96])
    b_ada_row = din("b_ada_row", [1, 6 * D])
    n1wT = din("n1wT", [128, KC])
    n2wT = din("n2wT", [128, KC])
    b_in_row = din("b_in_row", [1, PIN])
    b_gT = din("b_gT", [128, 32])
    b_gates = din("b_gates", [8, 2])
    hnw_rep = din("hnw_rep", [128, 1024])
    qnw_rep = din("qnw_rep", [128, 128])
    knw_rep = din("knw_rep", [128, 128])
    cwT = din("cwT", [128, FC, 3])
    cbT = din("cbT", [128, FC])
    ident_in = din("ident", [128, 128])
    maskT_in = din("maskT", [128, 128])
    cache_k = din("cache_k", [NPOOL * 128, 1024])
    cache_v = din("cache_v", [NPOOL * 128, 1024])
    cache_lf = din("cache_lf", [NPOOL, 1024])
    ptab = din("ptab", [1, NS * 16], I32)
    ptab_col = din("ptab_col", [128, 2], I32)
    st_C = din("st_C", [NS, 4, 256, 128])
    st_n = din("st_n", [NS, 512])
    st_m = din("st_m", [NS, 4])
    st_convT = din("st_convT", [128, FC, NS, 2])
    st_conv = din("st_conv", [NS, 2, DFF])
    sel_in = din("sel", [NS, NS, 128])
    hsel_in = din("hsel", [8, NS, NS])
    kmask_in = din("kmask", [128, 32])
    vgate_in = din("vgate", [8, NG, 2])
    halo_valid_in = din("halo_valid", [128, 1])
    bdm_in = din("bdm", [8, 1024])
    pidx_in = din("pidx", [128, 1])
    Lmat_in = din("Lmat", [128, 128])
    Emat_in = din("Emat", [NS, 2, 128])
    b_moT_in = din("b_moT", [128, 8])
    hnwT_in = din("hnwT", [128, 8])

    y_p = dout("y_p", [1024, D])
    y_s = dout("y_s", [NS, D])
    k_p = dout("k_p", [1024, 1024])
    v_p = dout("v_p", [1024, 1024])
    lf_p = dout("lf_p", [1024, 8])
    C_p = dout("C_p", [4, 256, 128])
    n_p = dout("n_p", [4, 128])
    m_p = dout("m_p", [4, 1])
    cv_p = dout("cv_p", [2, DFF])
    k_s = dout("k_s", [NS, 1024])
    v_s = dout("v_s", [NS, 1024])
    lf_s = dout("lf_s", [NS, 8])
    C_s = dout("C_s", [NS, 4, 256, 128])
    n_s = dout("n_s", [NS, 512])
    m_s = dout("m_s", [NS, 4])
    cv_s = dout("cv_s", [NS, 2, DFF])
    dbg_ymT = dout("dbg_ymT", [128, 8, GT], BF16)
    dbg_yfT = dout("dbg_yfT", [128, 8, GT], BF16)
    dbg_x1 = dout("dbg_x1", [GT, D])

    kT_scr = dscr("kT_scr", [8, 128, T], BF16)
    v_scr = dscr("v_scr", [8, 128, 32, 130], BF16)
    F_scr = dscr("F_scr", [8, T])
    F3_scr = dscr("F3_scr", [3, 8, T], BF16)
    U_scr = dscr("U_scr", [4, T])
    B_scr = dscr("B_scr", [4, T])
    Ae_scr = dscr("Ae_scr", [4, 33])
    G_scr = dscr("G_scr", [8, GT])
    gsm_scr = dscr("gsm_scr", [NS, 2, D], BF16)

    es = contextlib.ExitStack()
    with es:
        tk = Trk(nc, es)

        es_loop = contextlib.ExitStack()
        es_samp = contextlib.ExitStack()

        def sb(name, shape, dt=F32, st=None):
            return (st or es).enter_context(nc.sbuf_tensor("sb_" + name, list(shape), dt))

        def ps(name, shape, dt=F32):
            return es.enter_context(nc.psum_tensor("pp_" + name, list(shape), dt))

        uid = [0]

        class Buf:
            def __init__(self, t, n, k=None):
                self.t = t
                self.n = n
                self.k = k or n

            def __getitem__(self, k):
                return self.t[k]

        def sbuf(name, shape, dt=F32, st=None):
            uid[0] += 1
            t = (st or es).enter_context(nc.sbuf_tensor("sb%d_%s" % (uid[0], name), list(shape), dt))
            return Buf(t, "%s_%d" % (name, uid[0]), name)

        class Phase:
            def __init__(self):
                self.st = contextlib.ExitStack()

            def sb(self, name, shape, dt=F32):
                return sbuf(name, shape, dt, st=self.st)

            def close(self):
                tk.barrier()
                self.st.close()

        ident = sbuf("ident", [128, 128])
        identb = sbuf("identb", [128, 128], BF16)
        maskT = sbuf("maskT", [128, 128])
        maskTb = sbuf("maskTb", [128, 128], BF16)
        ones_b = sbuf("ones_b", [128, 128], BF16)
        cTs = sbuf("cTs", [128, KC, NS + 1])
        cTb = sbuf("cTb", [128, KC, NS + 1], BF16)
        modT = sbuf("modT", [128, 4, KC, NS + 1])
        grep = sbuf("grep", [128, 2, D], BF16)
        b_adaT_s = sbuf("b_adaT_s", [128, 96])
        brow = [sbuf("brow0", [1, 512], BF16), sbuf("brow1", [1, 512], BF16)]
        n1wT_s = sbuf("n1wT_s", [128, KC])
        n2wT_s = sbuf("n2wT_s", [128, KC])
        A1 = sbuf("A1", [128, KC, NS + 1])
        A2 = sbuf("A2", [128, KC, NS + 1])
        b_gT_s = sbuf("b_gT_s", [128, 32])
        b_gates_s = sbuf("b_gates_s", [8, 2])
        hnw_s = sbuf("hnw_s", [128, 1024])
        qnw_s = sbuf("qnw_s", [128, 128])
        knw_s = sbuf("knw_s", [128, 128])
        cwT_s = sbuf("cwT_s", [128, FC, 3])
        cbT_s = sbuf("cbT_s", [128, FC])
        wg_m = sbuf("wg_m", [128, KC, 8], BF16)
        wg_f = sbuf("wg_f", [128, KC, 8], BF16)
        st1 = sbuf("st1", [128, 8])
        nrm8 = sbuf("nrm8", [128, 8])
        St = sbuf("St", [128, 4, 257])
        aprev = sbuf("aprev", [128, FC, 2])
        nFk = sbuf("nFk", [128, 8, 32])
        zcol = sbuf("zcol", [8, 1])
        kmask = sbuf("kmask", [128, 32])
        vgate = sbuf("vgate", [8, NG, 2])
        halo_valid = sbuf("halo_valid", [128, 1])
        Bc = sbuf("Bc", [4, 1])
        Ac = sbuf("Ac", [4, 1])
        Fc = sbuf("Fc", [8, 1])
        Ucol = sbuf("Ucol", [128, 4, 4])
        Bcol = sbuf("Bcol", [128, 4, 4])
        ALb = sbuf("ALb", [128, 4, 5])
        wcol = sbuf("wcol", [128, 4, 4])
        sccol = sbuf("sccol", [128, 4, 4])
        thrcol = sbuf("thrcol", [128, 4, 4])
        xn = sbuf("xn", [128, D], BF16)
        wb = [sbuf("wb0", [128, 8192], BF16), sbuf("wb1", [128, 8192], BF16)]
        pss = [Buf(ps("ps%d" % i, [128, 512]), "ps%d" % i) for i in range(8)]

        def names(bufs):
            return [b if isinstance(b, str) else b.n for b in bufs]

        def load(q, dst, dst_ap, src):
            tk.dma(q, dst.k, dst_ap, src, writes=[dst.n])

        def store(q, dst, src, src_ap, extra_writes=(), extra_reads=()):
            tk.dma(q, "st_" + src.k, dst, src_ap, reads=[src.n] + list(extra_reads), writes=list(extra_writes))

        wslot = [0]

        def load_w(src_ap, shape_view, bias_ap=None):
            i = wslot[0] % 2
            wslot[0] += 1
            if bias_ap is not None:
                tk.dma('pool', brow[i].k, brow[i][0:1, 0:bias_ap.shape[-1]], bias_ap, writes=[brow[i].n])
            n = 1
            for s_ in shape_view[1:]:
                n *= s_
            flat = wb[i][:, 0:n]
            view = flat.rearrange("p (a b) -> p a b", a=shape_view[1]) if len(shape_view) == 3 else flat
            tk.dma('pool', wb[i].k, view, src_ap, writes=[wb[i].n])
            return view, wb[i], brow[i]

        def mm(out, out_ap, lhsT, rhs, reads, start, stop, sig=None):
            tk.op('pe', lambda e: e.matmul(out_ap, lhsT, rhs, start=start, stop=stop), reads=names(reads), writes=[out.n],
                  signal=True)

        def tr(out, out_ap, src, in_ap, idt, idt_ap):
            tk.op('pe', lambda e: e.transpose(out_ap, in_ap, idt_ap), reads=[src.n, idt.n], writes=[out.n])

        def act(out, out_ap, in_ap, reads, func, bias=None, scale=None, accum=None, extra_writes=()):
            kw = {}
            if bias is not None:
                kw['bias'] = bias
            if scale is not None:
                kw['scale'] = scale
            if accum is not None:
                kw['accum_out'] = accum
            tk.op('act', lambda e: e.activation(out_ap, in_ap, func, **kw), reads=names(reads),
                  writes=[out.n] + names(extra_writes))

        def tt(out, out_ap, a, b, op, reads, eng='dve'):
            tk.op(eng, lambda e: e.tensor_tensor(out_ap, a, b, op), reads=names(reads), writes=[out.n])

        def ts(out, out_ap, a, s1, s2, op0, op1, reads, eng='dve'):
            if op1 is None:
                tk.op(eng, lambda e: e.tensor_scalar(out_ap, a, s1, None, op0), reads=names(reads), writes=[out.n])
            else:
                tk.op(eng, lambda e: e.tensor_scalar(out_ap, a, s1, s2, op0, op1), reads=names(reads), writes=[out.n])

        def stt(out, out_ap, in0, scalar, in1, op0, op1, reads):
            tk.op('dve', lambda e: e.scalar_tensor_tensor(out_ap, in0, scalar, in1, op0, op1), reads=names(reads),
                  writes=[out.n])

        def cp(out, out_ap, a, reads, eng='dve'):
            tk.op(eng, lambda e: e.tensor_copy(out_ap, a), reads=names(reads), writes=[out.n])

        def memset(out, out_ap, v):
            tk.op('dve', lambda e: e.memset(out_ap, v), writes=[out.n])

        def logsig(dst, dst_ap, src_ap, reads):
            act(dst, dst_ap, src_ap, reads, AF.Exp, scale=-1.0)
            act(dst, dst_ap, dst_ap, [dst], AF.Ln, bias=1.0)
            ts(dst, dst_ap, dst_ap, -1.0, None, ALU.mult, None, [dst])

        def rsqrt_mean(dst, dst_ap, src_ap, reads, n):
            ts(dst, dst_ap, src_ap, 1.0 / n, EPS, ALU.mult, ALU.add, reads)
            act(dst, dst_ap, dst_ap, [dst], AF.Ln)
            act(dst, dst_ap, dst_ap, [dst], AF.Exp, scale=-0.5)

        NCA = nc.allow_non_contiguous_dma(reason="tiny re-layout DMAs")
        es.enter_context(NCA)

        for (dst, src) in [(ident, ident_in), (maskT, maskT_in), (cTs, cT_in), (b_adaT_s, b_adaT), (n1wT_s, n1wT),
                           (n2wT_s, n2wT), (b_gT_s, b_gT), (b_gates_s, b_gates), (hnw_s, hnw_rep), (qnw_s, qnw_rep),
                           (knw_s, knw_rep), (cwT_s, cwT), (cbT_s, cbT), (kmask, kmask_in), (vgate, vgate_in),
                           (halo_valid, halo_valid_in)]:
            load('sp', dst, dst[:], src)
        w_in_v = w_in.rearrange("(kc p) c -> p kc c", p=128)
        tk.dma('pool', wg_m.k, wg_m[:], w_in_v[:, :, C_MI:C_MI + 8], writes=[wg_m.n])
        tk.dma('pool', wg_f.k, wg_f[:], w_in_v[:, :, C_FF:C_FF + 8], writes=[wg_f.n])
        cp(identb, identb[:], ident[:], [ident])
        cp(maskTb, maskTb[:], maskT[:], [maskT])
        memset(ones_b, ones_b[:], 1.0)
        memset(zcol, zcol[:], 0.0)
        memset(Bc, Bc[:], 0.0)
        memset(Ac, Ac[:], 0.0)
        memset(Fc, Fc[:], 0.0)
        memset(St, St[:], 0.0)
        memset(aprev, aprev[:], 0.0)
        store('sp', Ae_scr[:, 0:1], zcol, zcol[0:4, 0:1], extra_writes=["Ae_scr"])

        ph = Phase()
        cTrep = ph.sb("cTrep", [128, KC, 128], BF16)
        gsm = ph.sb("gsm", [NS, 2, D], BF16)
        act(cTb, cTb[:], cTs[:], [cTs], AF.Silu)
        for kc in range(KC):
            cp(cTrep, cTrep[:, kc, :], cTb[:, kc, NS:NS + 1].to_broadcast([128, 128]), [cTb])
        w_ada_v = w_ada.rearrange("(kc p) c -> p kc c", p=128)
        fm_slots = {0: 0, 1: 1, 3: 2, 4: 3}
        for part in range(6):
            for cb in range(4):
                c0 = part * D + cb * 512
                wv, wn, br_ = load_w(w_ada_v[:, :, c0:c0 + 512], [128, KC, 512], b_ada_row[0:1, c0:c0 + 512])
                if part in fm_slots:
                    slot = fm_slots[part]
                    for ct in range(4):
                        pt = pss[ct % 2]
                        for kc in range(KC):
                            mm(pt, pt[:, 0:NS + 1], wv[:, kc, ct * 128:(ct + 1) * 128], cTb[:, kc, :], [wn, cTb],
                               kc == 0, kc == KC - 1)
                        ch = cb * 4 + ct
                        act(modT, modT[:, slot, ch, :], pt[:, 0:NS + 1], [pt, b_adaT_s], AF.Identity,
                            bias=b_adaT_s[:, part * 16 + ch:part * 16 + ch + 1])
                else:
                    gi = 0 if part == 2 else 1
                    for (lh, M, dst, pidx) in [(cTrep, 128, grep, 2), (cTb, NS, gsm, 3)]:
                        pt = pss[pidx]
                        for kc in range(KC):
                            lhs = lh[:, kc, :] if M == 128 else lh[:, kc, 0:NS]
                            mm(pt, pt[0:M, :], lhs, wv[:, kc, :], [wn, lh], kc == 0, False)
                        mm(pt, pt[0:M, :], ones_b[0:1, 0:M], br_[0:1, :], [ones_b, br_], False, True)
                        cp(dst, dst[0:M, gi, cb * 512:(cb + 1) * 512], pt[0:M, :], [pt])
        for (A, slot, nw) in [(A1, 1, n1wT_s), (A2, 3, n2wT_s)]:
            ts(A, A[:], modT[:, slot, :, :], 1.0, None, ALU.add, None, [modT])
            tt(A, A[:], A[:], nw[:].unsqueeze(2).to_broadcast([128, KC, NS + 1]), ALU.mult, [A, nw])
        store('sp', gsm_scr, gsm, gsm[:], extra_writes=["gsm_scr"])
        ph.close()

        def norm_tile(x, x_ap, M, A, shslot, hdst, col0, mcol):
            act(xn, xn[0:M, :], x_ap, [x], AF.Square, accum=st1[0:M, 0:1], extra_writes=[st1])
            rsqrt_mean(st1, st1[0:M, 1:2], st1[0:M, 0:1], [st1], D)
            ts(xn, xn[0:M, :], x_ap, st1[0:M, 1:2], None, ALU.mult, None, [x, st1])
            for q4 in range(4):
                pt = pss[4 + (q4 % 2)]
                ptb = pt[:].bitcast(BF16)
                for i in range(4):
                    kc = q4 * 4 + i
                    tr(pt, ptb[:, i * 128:i * 128 + M], xn, xn[0:M, kc * 128:(kc + 1) * 128], identb, identb[0:M, 0:M])
                for i in range(4):
                    kc = q4 * 4 + i
                    if mcol is not None:
                        act(hdst, hdst[:, kc, col0:col0 + M], ptb[:, i * 128:i * 128 + M], [pt, modT, A], AF.Identity,
                            bias=modT[:, shslot, kc, mcol:mcol + 1], scale=A[:, kc, mcol:mcol + 1])
                    else:
                        tt(hdst, hdst[:, kc, col0:col0 + M], ptb[:, i * 128:i * 128 + M], A[:, kc, 0:NS], ALU.mult, [pt, A])
                        tt(hdst, hdst[:, kc, col0:col0 + M], hdst[:, kc, col0:col0 + M], modT[:, shslot, kc, 0:NS], ALU.add,
                           [hdst, modT])

        def qknorm(kb, k3, M, wt, scr):
            tq = scr[0:M, :].rearrange("p (h d) -> p h d", h=4)
            tt(scr, tq, k3, k3, ALU.mult, [kb])
            tk.op('dve', lambda e: e.tensor_reduce(nrm8[0:M, 0:4], tq, AX.X, ALU.add), reads=[scr.n], writes=[nrm8.n])
            rsqrt_mean(nrm8, nrm8[0:M, 0:4], nrm8[0:M, 0:4], [nrm8], 128)
            tt(kb, k3, k3, nrm8[0:M, 0:4].unsqueeze(2).to_broadcast([M, 4, 128]), ALU.mult, [kb, nrm8])
            tt(kb, k3, k3, wt[0:M, :].unsqueeze(1).to_broadcast([M, 4, 128]), ALU.mult, [kb, wt])

        def proj_tok(hsrc, subs, w_v, b_row, c0, nblk, consume):
            pending = [None]
            for blk in range(nblk):
                cc = c0 + blk * 512
                wv, wn, br_ = load_w(w_v[:, :, cc:cc + 512], [128, KC, 512], None if b_row is None else b_row[0:1, cc:cc + 512])
                for si, (M, col0, idx) in enumerate(subs):
                    pt = pss[si % 4]
                    for kc in range(KC):
                        mm(pt, pt[0:M, :], hsrc[:, kc, col0:col0 + M], wv[:, kc, :], [wn, hsrc], kc == 0,
                           (kc == KC - 1) and b_row is None)
                    if b_row is not None:
                        mm(pt, pt[0:M, :], ones_b[0:1, 0:M], br_[0:1, :], [ones_b, br_], False, True)
                    if pending[0] is not None:
                        pending[0]()
                    pending[0] = consume(M, idx, blk, pt)
            if pending[0] is not None:
                pending[0]()
                pending[0] = None

        def gates(wg, bcol, dst, hsrc, n_):
            pt = pss[6]
            for kc in range(KC):
                mm(pt, pt[0:8, 0:n_], wg[:, kc, :], hsrc[:, kc, 0:n_], [wg, hsrc], kc == 0, kc == KC - 1)
            act(dst, dst[:, 0:n_], pt[0:8, 0:n_], [pt, b_gates_s], AF.Identity, bias=b_gates_s[:, bcol:bcol + 1])

        dbg_hook = [None]

        def dense_tail(subs, ntok, hsrc, ymT_, yfT_, xres, g1src, g2src, sample, y_dst_fn, post_fn=None, halo=False):
            ph_ = Phase()
            mgT = ph_.sb("mgT", [128, KC, ntok], BF16)
            sg = ph_.sb("sg", [128, 4, ntok])
            tA = ph_.sb("tA", [128, 4, ntok])
            w_pa_v = w_pa.rearrange("(kc p) c -> p kc c", p=128)
            w_pb_v = w_pb.rearrange("(kc p) c -> p kc c", p=128)
            for cb in range(4):
                for br_i, (wp_v, ysrc, cg, boff) in enumerate([(w_pa_v, ymT_, C_GA, 0), (w_pb_v, yfT_, C_GB, 16)]):
                    wv, wn, _ = load_w(wp_v[:, :, cb * 512:(cb + 1) * 512], [128, 8, 512])
                    for ct in range(4):
                        pt = pss[ct]
                        for kc in range(8):
                            mm(pt, pt[:, 0:ntok], wv[:, kc, ct * 128:(ct + 1) * 128], ysrc[:, kc, 0:ntok], [wn, ysrc],
                               kc == 0, kc == 7)
                    wv2, wn2, _ = load_w(w_in_v[:, :, cg + cb * 512:cg + (cb + 1) * 512], [128, KC, 512])
                    for ct in range(4):
                        pt = pss[4 + ct]
                        for kc in range(KC):
                            mm(pt, pt[:, 0:ntok], wv2[:, kc, ct * 128:(ct + 1) * 128], hsrc[:, kc, 0:ntok], [wn2, hsrc],
                               kc == 0, kc == KC - 1)
                    for ct in range(4):
                        ch = cb * 4 + ct
                        act(sg, sg[:, ct, :], pss[4 + ct][:, 0:ntok], [pss[4 + ct], b_gT_s], AF.Sigmoid,
                            bias=b_gT_s[:, boff + ch:boff + ch + 1])
                        if br_i == 0:
                            tt(tA, tA[:, ct, :], pss[ct][:, 0:ntok], sg[:, ct, :], ALU.mult, [pss[ct], sg])
                        else:
                            tt(sg, sg[:, ct, :], pss[ct][:, 0:ntok], sg[:, ct, :], ALU.mult, [pss[ct], sg])
                            tt(mgT, mgT[:, ch, :], sg[:, ct, :], tA[:, ct, :], ALU.add, [sg, tA])
            tmp = ph_.sb("tmp", [128, 512])
            w_out_v = w_out.rearrange("(kc p) c -> p kc c", p=128)

            def cons_o(M, idx, blk, pt):
                xb, xap = xres(idx)
                gb_, gap = g1src(idx, blk * 512, 512)
                tt(tmp, tmp[0:M, :], pt[0:M, :], gap, ALU.mult, [pt, gb_])
                tt(xb, xap[:, blk * 512:(blk + 1) * 512], xap[:, blk * 512:(blk + 1) * 512], tmp[0:M, :], ALU.add, [xb, tmp])

            proj_tok(mgT, subs, w_out_v, None, 0, 4, cons_o)
            if dbg_hook[0] is not None:
                dbg_hook[0]()
            ph_.close()
            ph_ = Phase()
            tmp = ph_.sb("tmp", [128, 512])
            for (M, col0, idx) in subs:
                xb, xap = xres(idx)
                norm_tile(xb, xap, M, A2, 2, hsrc, col0, None if sample else NS)
            uT = None if halo else ph_.sb("uT", [128, FC, ntok], BF16)
            ab = ph_.sb("ab", [128, ntok + 2])
            t1 = ph_.sb("t1", [128, ntok])
            t2 = ph_.sb("t2", [128, ntok])
            if sample:
                cvT = ph_.sb("cvT", [128, FC, NS, 2])
                load('sp', cvT, cvT[:], st_convT)
                aS = ph_.sb("aS", [128, FC, NS])
            w_f1_v = w_f1.rearrange("(kc p) c -> p kc c", p=128)
            for fb in range(FC // 4):
                wa, wan, _ = load_w(w_f1_v[:, :, fb * 512:(fb + 1) * 512], [128, KC, 512])
                if not halo:
                    wg_, wgn, _ = load_w(w_f1_v[:, :, DFF + fb * 512:DFF + (fb + 1) * 512], [128, KC, 512])
                for ci in range(4):
                    fc = fb * 4 + ci
                    pa = pss[ci % 2]
                    pg = pss[2 + ci % 2]
                    for kc in range(KC):
                        mm(pa, pa[:, 0:ntok], wa[:, kc, ci * 128:(ci + 1) * 128], hsrc[:, kc, 0:ntok], [wan, hsrc], kc == 0,
                           kc == KC - 1)
                    if halo:
                        cp(aprev, aprev[:, fc, :], pa[:, ntok - 2:ntok], [pa])
                        continue
                    for kc in range(KC):
                        mm(pg, pg[:, 0:ntok], wg_[:, kc, ci * 128:(ci + 1) * 128], hsrc[:, kc, 0:ntok], [wgn, hsrc], kc == 0,
                           kc == KC - 1)
                    w0 = cwT_s[:, fc, 0:1]
                    w1 = cwT_s[:, fc, 1:2]
                    w2 = cwT_s[:, fc, 2:3]
                    if not sample:
                        cp(ab, ab[:, 0:2], aprev[:, fc, :], [aprev])
                        cp(ab, ab[:, 2:2 + ntok], pa[:, 0:ntok], [pa])
                        cp(aprev, aprev[:, fc, :], ab[:, ntok:ntok + 2], [ab])
                        ts(t1, t1[:], ab[:, 0:ntok], w0, cbT_s[:, fc:fc + 1], ALU.mult, ALU.add, [ab, cwT_s, cbT_s])
                        stt(t1, t1[:], ab[:, 1:1 + ntok], w1, t1[:], ALU.mult, ALU.add, [ab, t1, cwT_s])
                        stt(t1, t1[:], ab[:, 2:2 + ntok], w2, t1[:], ALU.mult, ALU.add, [ab, t1, cwT_s])
                    else:
                        cp(aS, aS[:, fc, :], pa[:, 0:ntok], [pa])
                        ts(t1, t1[:], cvT[:, fc, :, 0], w0, cbT_s[:, fc:fc + 1], ALU.mult, ALU.add, [cvT, cwT_s, cbT_s])
                        stt(t1, t1[:], cvT[:, fc, :, 1], w1, t1[:], ALU.mult, ALU.add, [cvT, t1, cwT_s])
                        stt(t1, t1[:], aS[:, fc, :], w2, t1[:], ALU.mult, ALU.add, [aS, t1, cwT_s])
                    tt(t2, t2[:], t1[:], t1[:], ALU.mult, [t1])
                    ts(t2, t2[:], t2[:], 0.044715, 1.0, ALU.mult, ALU.add, [t2])
                    tt(t2, t2[:], t2[:], t1[:], ALU.mult, [t2, t1])
                    act(t2, t2[:], t2[:], [t2], AF.Sigmoid, scale=1.5957691216057308)
                    tt(t2, t2[:], t2[:], t1[:], ALU.mult, [t2, t1])
                    tt(uT, uT[:, fc, :], t2[:], pg[:, 0:ntok], ALU.mult, [t2, pg])
            w_f2_v = w_f2.rearrange("(kc p) c -> p kc c", p=128)
            for oc in range(0 if halo else KC):
                wv, wn, _ = load_w(w_f2_v[:, :, oc * 128:(oc + 1) * 128], [128, FC, 128])
                for si, (M, col0, idx) in enumerate(subs):
                    pt = pss[si % 4]
                    for kc in range(FC):
                        mm(pt, pt[0:M, 0:128], uT[:, kc, col0:col0 + M], wv[:, kc, :], [wn, uT], kc == 0, kc == FC - 1)
                    xb, xap = xres(idx)
                    gb_, gap = g2src(idx, oc * 128, 128)
                    tt(tmp, tmp[0:M, 0:128], pt[0:M, 0:128], gap, ALU.mult, [pt, gb_])
                    tt(xb, xap[:, oc * 128:(oc + 1) * 128], xap[:, oc * 128:(oc + 1) * 128], tmp[0:M, 0:128], ALU.add,
                       [xb, tmp])
            y_dst_fn()
            if post_fn is not None:
                post_fn(aS if sample else None, ph_)
            ph_.close()

        php = Phase()
        xt = php.sb("xt", [128, 4, D])
        hT = php.sb("hT", [128, KC, GT], BF16)
        ymT = php.sb("ymT", [128, 8, GT], BF16)
        yfT = php.sb("yfT", [128, 8, GT], BF16)
        Cst = php.sb("Cst", [128, 128])
        xw_v = xw.rearrange("(g t p) d -> g p t d", p=128, t=4)
        yp_v = y_p.rearrange("(g t p) d -> g p t d", p=128, t=4)
        kp_v = k_p.rearrange("(g t p) d -> g p t d", p=128, t=4)
        vp_v = v_p.rearrange("(g t p) d -> g p t d", p=128, t=4)
        ngroups = ng
        xt_loaded = [False]
        psubs = [(128, t4 * 128, t4) for t4 in range(4)]
        for g in range(ngroups):
            mode = 'prefix' if g < ngroups - 3 else ('halo' if g == ngroups - 3 else 'own')
            out_tiles = [] if mode == 'prefix' else ([3] if mode == 'halo' else [0, 1, 2, 3])
            osubs = [(128, t4 * 128, t4) for t4 in out_tiles]
            lo = g - (ngroups - 2)
            if not xt_loaded[0]:
                load('sp', xt, xt[:], xw_v[g])
            xt_loaded[0] = False
            for t4 in range(4):
                norm_tile(xt, xt[:, t4, :], 128, A1, 0, hT, t4 * 128, NS)
            if mode == 'prefix' and g + 1 < ngroups:
                load('sp', xt, xt[:], xw_v[g + 1])
                xt_loaded[0] = True
            phr = Phase()
            gm = phr.sb("gm", [8, GT])
            gf = phr.sb("gf", [8, GT])
            Ff = phr.sb("Ff", [8, GT])
            Fr = phr.sb("Fr", [8, GT])
            F3 = phr.sb("F3", [8, 3, GT], BF16)
            ig4 = phr.sb("ig4", [4, GT])
            mf4 = phr.sb("mf4", [4, GT])
            Bm = phr.sb("Bm", [4, GT])
            Um = phr.sb("Um", [4, GT])
            Am = phr.sb("Am", [4, GT])
            zrow = phr.sb("zrow", [8, GT])
            memset(zrow, zrow[:], 0.0)
            gates(wg_m, 0, gm, hT, GT)
            gates(wg_f, 1, gf, hT, GT)
            logsig(gf, gf[:], gf[:], [gf])
            ts(gf, gf[:], gf[:], vgate[:, g, 0:1], None, ALU.mult, None, [gf, vgate])
            if mode == 'own':
                store('sp', lf_p[lo * GT:(lo + 1) * GT, :].rearrange("t h -> h t"), gf, gf[:])
            tk.op('dve', lambda e: e.tensor_tensor_scan(Ff[:], gf[:], zrow[:], Fc[:], ALU.add, ALU.add),
                  reads=[gf.n, zrow.n, Fc.n], writes=[Ff.n])
            cp(Fc, Fc[:], Ff[:, GT - 1:GT], [Ff])
            store('sp', F_scr[:, g * GT:(g + 1) * GT], Ff, Ff[:], extra_writes=["F_scr"])
            cp(F3, F3[:, 0, :], Ff[:], [Ff])
            tt(Fr, Fr[:], Ff[:], F3[:, 0, :], ALU.subtract, [Ff, F3])
            cp(F3, F3[:, 1, :], Fr[:], [Fr])
            tt(Fr, Fr[:], Fr[:], F3[:, 1, :], ALU.subtract, [Fr, F3])
            cp(F3, F3[:, 2, :], Fr[:], [Fr])
            store('sp', F3_scr[:, :, g * GT:(g + 1) * GT].rearrange("j h t -> h j t"), F3, F3[:], extra_writes=["F3_scr"])
            for h in range(8):
                tk.dma('sp', nFk.k, nFk[:, h, 4 * g:4 * g + 4],
                       F_scr[h, g * GT:(g + 1) * GT].rearrange("(t p) -> p t", p=128), reads=["F_scr", nFk.n],
                       writes=["nFk_part%d" % h])
            ts(nFk, nFk[:, :, 4 * g:4 * g + 4], nFk[:, :, 4 * g:4 * g + 4], -1.0, None, ALU.mult, None,
               [nFk] + ["nFk_part%d" % h for h in range(8)])
            tt(nFk, nFk[:, :, 4 * g:4 * g + 4], nFk[:, :, 4 * g:4 * g + 4],
               kmask[:, 4 * g:4 * g + 4].unsqueeze(1).to_broadcast([128, 8, 4]), ALU.add, [nFk, kmask])
            store('sp', G_scr[:, 0:GT], gm, gm[:, 0:GT], extra_writes=["G_scr"])
            tk.dma('sp', ig4.k, ig4[:], G_scr[0:4, 0:GT], reads=["G_scr"], writes=[ig4.n])
            tk.dma('sp', mf4.k, mf4[:], G_scr[4:8, 0:GT], reads=["G_scr"], writes=[mf4.n])
            logsig(mf4, mf4[:], mf4[:], [mf4])
            ts(mf4, mf4[:], mf4[:], vgate[0:4, g, 0:1], None, ALU.mult, None, [mf4, vgate])
            ts(ig4, ig4[:], ig4[:], vgate[0:4, g, 0:1], vgate[0:4, g, 1:2], ALU.mult, ALU.add, [ig4, vgate])
            tk.op('dve', lambda e: e.tensor_tensor_scan(Bm[:], mf4[:], zrow[0:4, :], Bc[:], ALU.add, ALU.add),
                  reads=[mf4.n, zrow.n, Bc.n], writes=[Bm.n])
            tt(Um, Um[:], ig4[:], Bm[:], ALU.subtract, [ig4, Bm])
            tk.op('dve', lambda e: e.tensor_tensor_scan(Am[:], Um[:], Um[:], Ac[:], ALU.max, ALU.max),
                  reads=[Um.n, Ac.n], writes=[Am.n])
            cp(Bc, Bc[:], Bm[:, GT - 1:GT], [Bm])
            cp(Ac, Ac[:], Am[:, GT - 1:GT], [Am])
            store('sp', U_scr[:, g * GT:(g + 1) * GT], Um, Um[:], extra_writes=["U_scr"])
            store('sp', B_scr[:, g * GT:(g + 1) * GT], Bm, Bm[:], extra_writes=["B_scr"])
            store('sp', Ae_scr[:, 4 * g + 1:4 * g + 5], Am, Am[:].rearrange("h (t p) -> h t p", p=128)[:, :, 127],
                  extra_writes=["Ae_scr"])
            for h in range(4):
                tk.dma('sp', Ucol.k, Ucol[:, h, :], U_scr[h, g * GT:(g + 1) * GT].rearrange("(t p) -> p t", p=128),
                       reads=["U_scr", Ucol.n], writes=["Ucol_part%d" % h])
                tk.dma('sp', Bcol.k, Bcol[:, h, :], B_scr[h, g * GT:(g + 1) * GT].rearrange("(t p) -> p t", p=128),
                       reads=["B_scr", Bcol.n], writes=["Bcol_part%d" % h])
            tk.dma('sp', ALb.k, ALb[:], Ae_scr[:, 4 * g:4 * g + 5].unsqueeze(0).to_broadcast([128, 4, 5]),
                   reads=["Ae_scr"], writes=[ALb.n])
            tt(wcol, wcol[:], Ucol[:], ALb[:, :, 1:5], ALU.subtract, [Ucol, ALb] + ["Ucol_part%d" % h for h in range(4)])
            act(wcol, wcol[:], wcol[:], [wcol], AF.Exp)
            tt(sccol, sccol[:], ALb[:, :, 0:4], ALb[:, :, 1:5], ALU.subtract, [ALb])
            act(sccol, sccol[:], sccol[:], [sccol], AF.Exp)
            tt(thrcol, thrcol[:], Bcol[:], ALb[:, :, 1:5], ALU.add, [Bcol, ALb] + ["Bcol_part%d" % h for h in range(4)])
            act(thrcol, thrcol[:], thrcol[:], [thrcol], AF.Exp, scale=-1.0)

            ph = phr
            kmt = ph.sb("kmt", [128, 4, 512], BF16)
            vm = ph.sb("vm", [128, 4, 1024], BF16)
            qm = ph.sb("qm", [128, 4, 512], BF16)
            smo = ph.sb("smo", [128, 4, 1024], BF16)
            qT = ph.sb("qT", [128, 4, GT], BF16)
            kT = ph.sb("kT", [128, 4, GT], BF16)
            Vp = ph.sb("Vp", [128, 257], BF16)
            Stb = ph.sb("Stb", [128, 257], BF16)
            Sm = ph.sb("Sm", [128, 128], BF16)
            hm = ph.sb("hm", [128, 256])
            hj = ph.sb("hj", [128, 256], BF16)
            ymt = ph.sb("ymt", [128, 256], BF16)
            dn = ph.sb("dn", [128, 4])

            proj_tok(hT, psubs, w_in_v, b_in_row, C_MK, 1,
                     lambda M, idx, blk, pt: act(kmt, kmt[:, idx, :], pt[0:M, :], [pt], AF.Identity, scale=128 ** -0.5))
            proj_tok(hT, psubs, w_in_v, b_in_row, C_MV, 2,
                     lambda M, idx, blk, pt: cp(vm, vm[:, idx, blk * 512:(blk + 1) * 512], pt[0:M, :], [pt]))
            if osubs:
                proj_tok(hT, osubs, w_in_v, b_in_row, C_MQ, 1,
                         lambda M, idx, blk, pt: cp(qm, qm[:, idx, :], pt[0:M, :], [pt]))
                proj_tok(hT, osubs, w_in_v, b_in_row, C_MO, 2,
                         lambda M, idx, blk, pt: act(smo, smo[:, idx, blk * 512:(blk + 1) * 512], pt[0:M, :], [pt], AF.Sigmoid))
            for t4 in out_tiles:
                for (src, dst, pi) in [(qm, qT, 4), (kmt, kT, 5)]:
                    pt = pss[pi]
                    ptb = pt[:].bitcast(BF16)
                    for h in range(4):
                        tr(pt, ptb[:, h * 128:(h + 1) * 128], src, src[:, t4, h * 128:(h + 1) * 128], identb, identb[:])
                    cp(dst, dst[:, :, t4 * 128:(t4 + 1) * 128], ptb[:, 0:512].rearrange("p (h t) -> p h t", h=4), [pt])
            for t4 in range(4):
                tc0 = t4 * 128
                for h in range(4):
                    ts(Vp, Vp[:, 0:256], vm[:, t4, h * 256:(h + 1) * 256], wcol[:, h, t4:t4 + 1], None, ALU.mult, None,
                       [vm, wcol])
                    cp(Vp, Vp[:, 256:257], wcol[:, h, t4:t4 + 1], [wcol])
                    ts(St, St[:, h, :], St[:, h, :], sccol[:, h, t4:t4 + 1], None, ALU.mult, None, [St, sccol])
                    if t4 not in out_tiles:
                        p2 = pss[2]
                        mm(p2, p2[:, 0:257], kmt[:, t4, h * 128:(h + 1) * 128], Vp[:], [kmt, Vp], True, True)
                        tt(St, St[:, h, :], St[:, h, :], p2[:, 0:257], ALU.add, [St, p2])
                        continue
                    cp(Stb, Stb[:], St[:, h, :], [St])
                    p0 = pss[0]
                    mm(p0, p0[:, 0:128], kT[:, h, tc0:tc0 + 128], qT[:, h, tc0:tc0 + 128], [kT, qT], True, True)
                    tt(Sm, Sm[:], p0[:, 0:128], maskT[:], ALU.mult, [p0, maskT])
                    p1 = pss[1]
                    mm(p1, p1[:, 0:257], Sm[:], Vp[:], [Sm, Vp], True, False)
                    mm(p1, p1[:, 0:257], qT[:, h, tc0:tc0 + 128], Stb[:], [qT, Stb], False, True)
                    p2 = pss[2]
                    mm(p2, p2[:, 0:257], kmt[:, t4, h * 128:(h + 1) * 128], Vp[:], [kmt, Vp], True, True)
                    tt(St, St[:, h, :], St[:, h, :], p2[:, 0:257], ALU.add, [St, p2])
                    act(dn, dn[:, 0:1], p1[:, 256:257], [p1], AF.Abs)
                    ts(dn, dn[:, 0:1], dn[:, 0:1], thrcol[:, h, t4:t4 + 1], None, ALU.max, None, [dn, thrcol])
                    tk.op('dve', lambda e: e.reciprocal(dn[:, 1:2], dn[:, 0:1]), reads=[dn.n], writes=[dn.n])
                    ts(hm, hm[:], p1[:, 0:256], dn[:, 1:2], None, ALU.mult, None, [p1, dn])
                    act(hj, hj[:], hm[:], [hm], AF.Square, accum=dn[:, 2:3], extra_writes=[dn])
                    rsqrt_mean(dn, dn[:, 3:4], dn[:, 2:3], [dn], 256)
                    stt(hm, hm[:], hm[:], dn[:, 3:4], hnw_s[:, h * 256:(h + 1) * 256], ALU.mult, ALU.mult, [hm, dn, hnw_s])
                    tt(ymt, ymt[:], hm[:], smo[:, t4, h * 256:(h + 1) * 256], ALU.mult, [hm, smo])
                    p3 = pss[3]
                    p3b = p3[:].bitcast(BF16)
                    for a in range(2):
                        tr(p3, p3b[:, a * 128:(a + 1) * 128], ymt, ymt[:, a * 128:(a + 1) * 128], identb, identb[:])
                    cp(ymT, ymT[:, 2 * h:2 * h + 2, tc0:tc0 + 128], p3b[:, 0:256].rearrange("p (a t) -> p a t", a=2), [p3])
            ph.close()

            ph = Phase()
            kst = [ph.sb("kst0", [128, 512]), ph.sb("kst1", [128, 512])]
            scr = ph.sb("scr", [128, 512])
            kb16s = [ph.sb("kb16a", [128, 512], BF16), ph.sb("kb16b", [128, 512], BF16)]
            kTst = ph.sb("kTst", [128, 4, 8, 128], BF16)
            vaug = ph.sb("vaug", [128, 4, 8, 130], BF16)
            QT = ph.sb("QT", [128, 8, GT], BF16)
            KTh = ph.sb("KTh", [128, T], BF16)
            Vh = ph.sb("Vh", [128, 32, 130], BF16)
            fq3 = ph.sb("fq3", [3, GT], BF16)
            eT = [ph.sb("eT0", [128, 512], BF16), ph.sb("eT1", [128, 512], BF16)]
            yft = ph.sb("yft", [128, 128], BF16)
            rd = ph.sb("rd", [128, 1])
            memset(vaug, vaug[:, :, :, 128:130], 1.0)

            def cons_k(M, idx, blk, pt):
                i = (idx + blk) % 2
                cp(kst[i], kst[i][:], pt[0:M, :], [pt])
                qknorm(kst[i], kst[i][:].rearrange("p (h d) -> p h d", h=4), 128, knw_s, scr)
                if mode == 'own':
                    store('sp', kp_v[lo][:, idx, blk * 512:(blk + 1) * 512], kst[i], kst[i][:])
                kb16 = kb16s[i]
                cp(kb16, kb16[:], kst[i][:], [kst[i]])

                def later():
                    p5 = pss[5]
                    p5b = p5[:].bitcast(BF16)
                    for hh in range(4):
                        tr(p5, p5b[:, hh * 128:(hh + 1) * 128], kb16, kb16[:, hh * 128:(hh + 1) * 128], identb, identb[:])
                    cp(kTst, kTst[:, idx, blk * 4:(blk + 1) * 4, :], p5b[:, 0:512].rearrange("p (h t) -> p h t", h=4), [p5])
                return later

            def cons_v(M, idx, blk, pt):
                i = (idx + blk) % 2
                cp(kst[i], kst[i][:], pt[0:M, :], [pt])
                if mode == 'own':
                    store('sp', vp_v[lo][:, idx, blk * 512:(blk + 1) * 512], kst[i], kst[i][:])
                cp(vaug, vaug[:, idx, blk * 4:(blk + 1) * 4, 0:128], kst[i][:].rearrange("p (h d) -> p h d", h=4), [kst[i]])

            def cons_q(M, idx, blk, pt):
                i = (idx + blk) % 2
                cp(kst[i], kst[i][:], pt[0:M, :], [pt])
                qknorm(kst[i], kst[i][:].rearrange("p (h d) -> p h d", h=4), 128, qnw_s, scr)
                kb16 = kb16s[i]
                act(kb16, kb16[:], kst[i][:], [kst[i]], AF.Identity, scale=128 ** -0.5)

                def later():
                    p5 = pss[5]
                    p5b = p5[:].bitcast(BF16)
                    for hh in range(4):
                        tr(p5, p5b[:, hh * 128:(hh + 1) * 128], kb16, kb16[:, hh * 128:(hh + 1) * 128], identb, identb[:])
                    cp(QT, QT[:, blk * 4:(blk + 1) * 4, idx * 128:(idx + 1) * 128],
                       p5b[:, 0:512].rearrange("p (h t) -> p h t", h=4), [p5])
                return later

            proj_tok(hT, psubs, w_in_v, b_in_row, C_FK, 2, cons_k)
            proj_tok(hT, psubs, w_in_v, b_in_row, C_FV, 2, cons_v)
            if osubs:
                proj_tok(hT, osubs, w_in_v, b_in_row, C_FQ, 2, cons_q)
            for t4 in range(4):
                c0_ = g * GT + t4 * 128
                store('sp', kT_scr[:, :, c0_:c0_ + 128].rearrange("h p t -> p h t"), kTst, kTst[:, t4, :, :],
                      extra_writes=["kT_scr"])
                store('sp', v_scr[:, :, 4 * g + t4, :].rearrange("h p c -> p h c"), vaug, vaug[:, t4, :, :],
                      extra_writes=["v_scr"])
            nkt = 4 * (g + 1)
            qis = out_tiles
            for h in (range(8) if qis else []):
                tk.dma('sp', KTh.k, KTh[:, 0:nkt * 128], kT_scr[h, :, 0:nkt * 128], reads=["kT_scr"], writes=[KTh.n])
                tk.dma('sp', Vh.k, Vh[:, 0:nkt, :], v_scr[h, :, 0:nkt, :], reads=["v_scr"], writes=[Vh.n])
                tk.dma('sp', fq3.k, fq3[:], F3_scr[:, h, g * GT:(g + 1) * GT], reads=["F3_scr"], writes=[fq3.n])
                def att_geom(kt):
                    d = kt - 4 * g
                    ql = [qi for qi in qis if qi >= d]
                    c0_ = max(0, d, min(qis)) * 128
                    return d, ql, c0_, GT - c0_

                def emit_qk(kt):
                    d, ql, c0_, n_ = att_geom(kt)
                    psn = pss[kt % 2]
                    mm(psn, psn[:, 0:n_], KTh[:, kt * 128:(kt + 1) * 128], QT[:, h, c0_:GT], [KTh, QT], True, False)
                    mm(psn, psn[:, 0:n_], ones_b[0:3, 0:128], fq3[0:3, c0_:GT], [ones_b, fq3], False, True)

                def emit_pv(kt):
                    d, ql, c0_, n_ = att_geom(kt)
                    psn = pss[kt % 2]
                    e_ = eT[kt % 2]
                    act(e_, e_[:, 0:n_], psn[:, 0:n_], [psn, nFk], AF.Exp, bias=nFk[:, h, kt:kt + 1])
                    if d >= 0 and d in ql:
                        tt(e_, e_[:, 0:128], e_[:, 0:128], maskTb[:], ALU.mult, [e_, maskTb], eng='pool')
                    for qi in ql:
                        po = pss[2 + qi]
                        mm(po, po[:, 0:130], e_[:, qi * 128 - c0_:(qi + 1) * 128 - c0_], Vh[:, kt, :], [e_, Vh],
                           kt == 0, kt == 4 * g + qi, sig=True)

                kts = [kt for kt in range(nkt) if att_geom(kt)[1]]
                emit_qk(kts[0])
                for ki, kt in enumerate(kts):
                    if ki + 1 < len(kts):
                        emit_qk(kts[ki + 1])
                    emit_pv(kt)
                for qi in qis:
                    po = pss[2 + qi]
                    tk.op('dve', lambda e, po=po: e.reciprocal(rd[:], po[:, 128:129]), reads=[po.n], writes=[rd.n])
                    ts(yft, yft[:], po[:, 0:128], rd[:, 0:1], None, ALU.mult, None, [po, rd])
                    p7 = pss[7]
                    p7b = p7[:].bitcast(BF16)
                    tr(p7, p7b[:, 0:128], yft, yft[:], identb, identb[:])
                    cp(yfT, yfT[:, h, qi * 128:(qi + 1) * 128], p7b[:, 0:128], [p7])
            ph.close()

            if mode == 'own' and lo == 0:
                store('sp', dbg_ymT, ymT, ymT[:])
                store('sp', dbg_yfT, yfT, yfT[:])
            if mode == 'own':
                def ydst():
                    store('sp', yp_v[lo], xt, xt[:])

                dbg_hook[0] = (lambda: store('sp', dbg_x1.rearrange("(t p) d -> p t d", p=128), xt, xt[:])) if lo == 0 else None

                dense_tail(psubs, GT, hT, ymT, yfT, lambda idx: (xt, xt[:, idx, :]),
                           lambda idx, c, n: (grep, grep[:, 0, c:c + n]), lambda idx, c, n: (grep, grep[:, 1, c:c + n]),
                           False, ydst)
            elif mode == 'halo':
                phh = Phase()
                hTh = phh.sb("hTh", [128, KC, 128], BF16)
                ymTh = phh.sb("ymTh", [128, 8, 128], BF16)
                yfTh = phh.sb("yfTh", [128, 8, 128], BF16)
                cp(hTh, hTh[:], hT[:, :, 384:512], [hT])
                cp(ymTh, ymTh[:], ymT[:, :, 384:512], [ymT])
                cp(yfTh, yfTh[:], yfT[:, :, 384:512], [yfT])
                dbg_hook[0] = None
                dense_tail([(128, 0, 3)], 128, hTh, ymTh, yfTh, lambda idx: (xt, xt[:, idx, :]),
                           lambda idx, c, n: (grep, grep[:, 0, c:c + n]), lambda idx, c, n: (grep, grep[:, 1, c:c + n]),
                           False, lambda: None, None, True)
                ts(aprev, aprev[:], aprev[:], -1.0e30, 1.0e30, ALU.max, ALU.min, [aprev])
                ts(aprev, aprev[:], aprev[:], halo_valid[:, 0:1], None, ALU.mult, None, [aprev, halo_valid])
                phh.close()

        if stage >= 1:
            for h in range(4):
                for half in range(2):
                    pt = pss[7]
                    tr(pt, pt[:, 0:128], St, St[:, h, half * 128:(half + 1) * 128], ident, ident[:])
                    cp(Cst, Cst[:], pt[:, 0:128], [pt])
                    store('sp', C_p[h, half * 128:(half + 1) * 128, :], Cst, Cst[:])
                store('sp', n_p[h:h + 1, :].rearrange("o d -> d o"), St, St[:, h, 256:257])
            tt(Bc, Bc[:], Bc[:], Ac[:], ALU.add, [Bc, Ac])
            store('sp', m_p, Bc, Bc[:])
            if stage >= 3:
                for r in range(2):
                    store('sp', cv_p[r].rearrange("(c p) -> p c", p=128), aprev, aprev[:, :, r])

        php.close()
        if stage >= 4:
            phs = Phase()
            xst = phs.sb("xst", [NS, D])
            hTs = phs.sb("hTs", [128, KC, NS], BF16)
            ymTs = phs.sb("ymTs", [128, 8, NS], BF16)
            yfTs = phs.sb("yfTs", [128, 8, NS], BF16)
            gsm = phs.sb("gsm", [NS, 2, D], BF16)
            gms = phs.sb("gms", [8, NS])
            gfs = phs.sb("gfs", [8, NS])
            ksm = phs.sb("ksm", [NS, 1024])
            vsm = phs.sb("vsm", [NS, 1024])
            qsm = phs.sb("qsm", [NS, 1024])
            qsb = phs.sb("qsb", [NS, 1024], BF16)
            kmf = phs.sb("kmf", [NS, 512])
            qmf = phs.sb("qmf", [NS, 512])
            kms = phs.sb("kms", [NS, 512], BF16)
            qms = phs.sb("qms", [NS, 512], BF16)
            vms = phs.sb("vms", [NS, 1024], BF16)
            smoT = phs.sb("smoT", [128, 8, NS])
            scr16 = phs.sb("scr16", [NS, 512])
            b_moT = phs.sb("b_moT", [128, 8])
            hnwT = phs.sb("hnwT", [128, 8])
            sel_f = phs.sb("sel_f", [NS, NS, 128])
            sel_b = phs.sb("sel_b", [NS, NS, 128], BF16)
            ones_f = phs.sb("ones_f", [128, 128])
            load('sp', xst, xst[:], xs)
            tk.dma('sp', gsm.k, gsm[:], gsm_scr, reads=["gsm_scr"], writes=[gsm.n])
            load('sp', b_moT, b_moT[:], b_moT_in)
            load('sp', hnwT, hnwT[:], hnwT_in)
            load('sp', sel_f, sel_f[:], sel_in)
            cp(sel_b, sel_b[:], sel_f[:], [sel_f])
            memset(ones_f, ones_f[:], 1.0)
            norm_tile(xst, xst[:], NS, A1, 0, hTs, 0, None)
            ssubs = [(NS, 0, 0)]
            gates(wg_m, 0, gms, hTs, NS)
            gates(wg_f, 1, gfs, hTs, NS)
            logsig(gfs, gfs[:], gfs[:], [gfs])
            store('sp', lf_s.rearrange("i h -> h i"), gfs, gfs[:])

            def s_k(M, idx, blk, pt):
                cp(ksm, ksm[:, blk * 512:(blk + 1) * 512], pt[0:M, :], [pt])
                qknorm(ksm, ksm[:, blk * 512:(blk + 1) * 512].rearrange("p (h d) -> p h d", h=4), NS, knw_s, scr16)

            def s_q(M, idx, blk, pt):
                cp(qsm, qsm[:, blk * 512:(blk + 1) * 512], pt[0:M, :], [pt])
                qknorm(qsm, qsm[:, blk * 512:(blk + 1) * 512].rearrange("p (h d) -> p h d", h=4), NS, qnw_s, scr16)
                ts(qsm, qsm[:, blk * 512:(blk + 1) * 512], qsm[:, blk * 512:(blk + 1) * 512], 128 ** -0.5, None, ALU.mult,
                   None, [qsm])
                cp(qsb, qsb[:, blk * 512:(blk + 1) * 512], qsm[:, blk * 512:(blk + 1) * 512], [qsm])

            def s_mk(M, idx, blk, pt):
                act(kmf, kmf[:], pt[0:M, :], [pt], AF.Identity, scale=128 ** -0.5)
                cp(kms, kms[:], kmf[:], [kmf])

            def s_mq(M, idx, blk, pt):
                cp(qmf, qmf[:], pt[0:M, :], [pt])
                cp(qms, qms[:], qmf[:], [qmf])

            proj_tok(hTs, ssubs, w_in_v, b_in_row, C_FK, 2, s_k)
            store('sp', k_s, ksm, ksm[:])
            proj_tok(hTs, ssubs, w_in_v, b_in_row, C_FV, 2,
                     lambda M, idx, blk, pt: cp(vsm, vsm[:, blk * 512:(blk + 1) * 512], pt[0:M, :], [pt]))
            store('sp', v_s, vsm, vsm[:])
            proj_tok(hTs, ssubs, w_in_v, b_in_row, C_FQ, 2, s_q)
            proj_tok(hTs, ssubs, w_in_v, b_in_row, C_MK, 1, s_mk)
            proj_tok(hTs, ssubs, w_in_v, b_in_row, C_MQ, 1, s_mq)
            proj_tok(hTs, ssubs, w_in_v, b_in_row, C_MV, 2,
                     lambda M, idx, blk, pt: cp(vms, vms[:, blk * 512:(blk + 1) * 512], pt[0:M, :], [pt]))
            for blk in range(2):
                wv, wn, _ = load_w(w_in_v[:, :, C_MO + blk * 512:C_MO + (blk + 1) * 512], [128, KC, 512])
                for ct in range(4):
                    pt = pss[ct]
                    for kc in range(KC):
                        mm(pt, pt[:, 0:NS], wv[:, kc, ct * 128:(ct + 1) * 128], hTs[:, kc, :], [wn, hTs], kc == 0, kc == KC - 1)
                    ch = blk * 4 + ct
                    act(smoT, smoT[:, ch, :], pt[:, 0:NS], [pt, b_moT], AF.Sigmoid, bias=b_moT[:, ch:ch + 1])

            phm = Phase()
            m0s = phm.sb("m0s", [NS, 4])
            n0s = phm.sb("n0s", [NS, 512])
            gts = phm.sb("gts", [NS, 8])
            sm = phm.sb("sm", [NS, 8, 4])
            bcs = phm.sb("bcs", [NS, 16])
            t16 = phm.sb("t16", [NS, 512])
            vTs = phm.sb("vTs", [128, 8, NS], BF16)
            Ct = phm.sb("Ct", [128, 8, 128])
            Cu = phm.sb("Cu", [128, 8, 128])
            scw = phm.sb("scw", [128, 16])
            wv8 = phm.sb("wv8", [128, 8])
            c0q = phm.sb("c0q", [128, 8])
            t8 = phm.sb("t8", [128, 8])
            hAll = phm.sb("hAll", [128, 8, NS])
            hsq = phm.sb("hsq", [128, 8, NS])
            ssum = phm.sb("ssum", [128, 4, NS])
            load('sp', m0s, m0s[:], st_m)
            load('sp', n0s, n0s[:], st_n)
            pt = pss[6]
            tr(pt, pt[0:NS, 0:8], gms, gms[:], ident, ident[0:8, 0:8])
            cp(gts, gts[:], pt[0:NS, 0:8], [pt])
            logsig(gts, gts[:, 4:8], gts[:, 4:8], [gts])
            tt(sm, sm[:, 0, :], m0s[:], gts[:, 4:8], ALU.add, [m0s, gts])
            tt(sm, sm[:, 1, :], sm[:, 0, :], gts[:, 0:4], ALU.max, [sm, gts])
            store('sp', m_s, sm, sm[:, 1, :])
            tt(sm, sm[:, 2, :], sm[:, 0, :], sm[:, 1, :], ALU.subtract, [sm])
            act(sm, sm[:, 2, :], sm[:, 2, :], [sm], AF.Exp)
            tt(sm, sm[:, 3, :], gts[:, 0:4], sm[:, 1, :], ALU.subtract, [sm, gts])
            act(sm, sm[:, 3, :], sm[:, 3, :], [sm], AF.Exp)
            act(sm, sm[:, 4, :], sm[:, 1, :], [sm], AF.Exp, scale=-1.0)
            q3 = qmf[:].rearrange("p (h d) -> p h d", h=4)
            k3 = kmf[:].rearrange("p (h d) -> p h d", h=4)
            n3 = n0s[:].rearrange("p (h d) -> p h d", h=4)
            t3 = t16[:].rearrange("p (h d) -> p h d", h=4)
            tt(t16, t3, q3, k3, ALU.mult, [qmf, kmf])
            tk.op('dve', lambda e: e.tensor_reduce(sm[:, 5, :], t3, AX.X, ALU.add), reads=[t16.n], writes=[sm.n])
            tt(t16, t3, q3, n3, ALU.mult, [qmf, n0s])
            tk.op('dve', lambda e: e.tensor_reduce(sm[:, 6, :], t3, AX.X, ALU.add), reads=[t16.n], writes=[sm.n])
            tt(sm, sm[:, 7, :], sm[:, 3, :], sm[:, 5, :], ALU.mult, [sm])
            tt(sm, sm[:, 6, :], sm[:, 6, :], sm[:, 2, :], ALU.mult, [sm])
            tt(sm, sm[:, 6, :], sm[:, 6, :], sm[:, 7, :], ALU.add, [sm])
            act(sm, sm[:, 6, :], sm[:, 6, :], [sm], AF.Abs)
            tt(sm, sm[:, 6, :], sm[:, 6, :], sm[:, 4, :], ALU.max, [sm])
            tk.op('dve', lambda e: e.reciprocal(sm[:, 6, :], sm[:, 6, :]), reads=[sm.n], writes=[sm.n])
            cp(bcs, bcs[:, 0:4], sm[:, 2, :], [sm])
            cp(bcs, bcs[:, 4:8], sm[:, 3, :], [sm])
            cp(bcs, bcs[:, 8:12], sm[:, 6, :], [sm])
            cp(bcs, bcs[:, 12:16], sm[:, 7, :], [sm])
            tt(n0s, n3, n3, sm[:, 2, :].unsqueeze(2).to_broadcast([NS, 4, 128]), ALU.mult, [n0s, sm])
            tt(t16, t3, k3, sm[:, 3, :].unsqueeze(2).to_broadcast([NS, 4, 128]), ALU.mult, [kmf, sm])
            tt(n0s, n0s[:], n0s[:], t16[:], ALU.add, [n0s, t16])
            store('sp', n_s, n0s, n0s[:])
            ptb = pss[6][:].bitcast(BF16)
            for c8 in range(8):
                tr(pss[6], ptb[:, c8 * NS:(c8 + 1) * NS], vms, vms[:, c8 * 128:(c8 + 1) * 128], identb, identb[0:NS, 0:NS])
            cp(vTs, vTs[:].rearrange("p c i -> p (c i)"), ptb[:, 0:8 * NS], [pss[6]])
            for i in range(NS):
                tk.dma('sp', Ct.k, Ct[:], st_C[i].rearrange("h (a p) d -> p (h a) d", p=128), writes=[Ct.n])
                mm(pss[0], pss[0][:, :], sel_b[:, i, :], kms[:, :], [sel_b, kms], True, True)
                mm(pss[2], pss[2][:, :], sel_b[:, i, :], qms[:, :], [sel_b, qms], True, True)
                mm(pss[1], pss[1][:, 0:16], sel_f[:, i, :], bcs[:, :], [sel_f, bcs], True, True)
                cp(scw, scw[:], pss[1][:, 0:16], [pss[1]])
                C4 = Ct[:].rearrange("p (h a) d -> p h a d", h=4)
                U4 = Cu[:].rearrange("p (h a) d -> p h a d", h=4)
                tt(Cu, U4, C4, pss[2][:, :].rearrange("p (h d) -> p h d", h=4).unsqueeze(2).to_broadcast([128, 4, 2, 128]),
                   ALU.mult, [Ct, pss[2]])
                tk.op('dve', lambda e: e.tensor_reduce(c0q[:], Cu[:], AX.X, ALU.add), reads=[Cu.n], writes=[c0q.n])
                v3 = vTs[:, :, i].rearrange("p (h a) -> p h a", h=4)
                tt(wv8, wv8[:].rearrange("p (h a) -> p h a", h=4), v3, scw[:, 4:8].unsqueeze(2).to_broadcast([128, 4, 2]),
                   ALU.mult, [vTs, scw])
                tt(t8, t8[:].rearrange("p (h a) -> p h a", h=4), v3, scw[:, 12:16].unsqueeze(2).to_broadcast([128, 4, 2]),
                   ALU.mult, [vTs, scw])
                tt(c0q, c0q[:].rearrange("p (h a) -> p h a", h=4), c0q[:].rearrange("p (h a) -> p h a", h=4),
                   scw[:, 0:4].unsqueeze(2).to_broadcast([128, 4, 2]), ALU.mult, [c0q, scw])
                tt(c0q, c0q[:], c0q[:], t8[:], ALU.add, [c0q, t8])
                tt(hAll, hAll[:, :, i].rearrange("p (h a) -> p h a", h=4), c0q[:].rearrange("p (h a) -> p h a", h=4),
                   scw[:, 8:12].unsqueeze(2).to_broadcast([128, 4, 2]), ALU.mult, [c0q, scw])
                tt(Ct, C4, C4, scw[:, 0:4].unsqueeze(2).unsqueeze(3).to_broadcast([128, 4, 2, 128]), ALU.mult, [Ct, scw])
                tt(Cu, U4, pss[0][:, :].rearrange("p (h d) -> p h d", h=4).unsqueeze(2).to_broadcast([128, 4, 2, 128]),
                   wv8[:].rearrange("p (h a) -> p h a", h=4).unsqueeze(3).to_broadcast([128, 4, 2, 128]), ALU.mult,
                   [pss[0], wv8, Cu])
                tt(Ct, Ct[:], Ct[:], Cu[:], ALU.add, [Ct, Cu], eng='pool')
                store('sp', C_s[i].rearrange("h (a p) d -> p (h a) d", p=128), Ct, Ct[:])
            tt(hsq, hsq[:], hAll[:], hAll[:], ALU.mult, [hAll])
            mm(pss[3], pss[3][:, 0:8 * NS], ones_f[:], hsq[:].rearrange("p c i -> p (c i)"), [ones_f, hsq], True, True)
            p3v = pss[3][:, 0:8 * NS].rearrange("p (h a i) -> p h a i", h=4, a=2)
            cp(ssum, ssum[:], p3v[:, :, 0, :], [pss[3]])
            tt(ssum, ssum[:], ssum[:], p3v[:, :, 1, :], ALU.add, [ssum, pss[3]])
            rsqrt_mean(ssum, ssum[:], ssum[:], [ssum], 256)
            h4 = hAll[:].rearrange("p (h a) i -> p h a i", h=4)
            tt(hAll, h4, h4, ssum[:].unsqueeze(2).to_broadcast([128, 4, 2, NS]), ALU.mult, [hAll, ssum])
            tt(hAll, hAll[:], hAll[:], hnwT[:].unsqueeze(2).to_broadcast([128, 8, NS]), ALU.mult, [hAll, hnwT])
            tt(ymTs, ymTs[:], hAll[:], smoT[:], ALU.mult, [hAll, smoT])
            phm.close()

            pha = Phase()
            NP = NS * 16
            ptb_i = pha.sb("ptb_i", [128, NP], I32)
            ptb_f = pha.sb("ptb_f", [128, NP])
            pidx = pha.sb("pidx", [128, 1])
            idx_all = pha.sb("idx_all", [128, NP], I32)
            pcol = pha.sb("pcol", [128, 2], I32)
            LF = pha.sb("LF", [128, 2, 1024])
            tot = pha.sb("tot", [128, 2, 8])
            later = pha.sb("later", [128, 2, 8])
            Lmat = pha.sb("Lmat", [128, 128])
            Emat = pha.sb("Emat", [NS, 2, 128])
            gft = pha.sb("gft", [NS, 8])
            DT = pha.sb("DT", [128, NP, 8])
            sc_all = pha.sb("sc_all", [128, NP, 8])
            e_all = pha.sb("e_all", [128, NP, 8], BF16)
            qbf = pha.sb("qbf", [128, 1024], BF16)
            NPB = 4
            Kpg = [pha.sb("Kpg%d" % q_, [128, 1024], BF16) for q_ in range(NPB)]
            Vpg = [pha.sb("Vpg%d" % q_, [128, 1024], BF16) for q_ in range(NPB)]
            prod = pha.sb("prod", [128, 1024], BF16)
            bdm = pha.sb("bdm", [8, 1024])
            hsel = pha.sb("hsel", [8, NS, NS])
            msk = pha.sb("msk", [8, 1024])
            pdd = pha.sb("pdd", [8, 8])
            pdc = pha.sb("pdc", [8, 1])
            enew = pha.sb("enew", [NS, 8])
            yacc = pha.sb("yacc", [NS, 1024])
            dacc = pha.sb("dacc", [NS, 8])
            yfb = pha.sb("yfb", [NS, 1024], BF16)
            zscan = pha.sb("zscan", [128, 128])
            memset(zscan, zscan[:], 0.0)
            load('sp', ptb_i, ptb_i[:], ptab[0:1, :].to_broadcast([128, NP]))
            load('sp', pidx, pidx[:], pidx_in)
            load('sp', pcol, pcol[:], ptab_col)
            load('sp', Lmat, Lmat[:], Lmat_in)
            load('sp', Emat, Emat[:], Emat_in)
            load('sp', bdm, bdm[:], bdm_in)
            load('sp', hsel, hsel[:], hsel_in)
            cp(ptb_f, ptb_f[:], ptb_i[:], [ptb_i])
            ts(ptb_f, ptb_f[:], ptb_f[:], 128.0, pidx[:, 0:1], ALU.mult, ALU.add, [ptb_f, pidx])
            cp(idx_all, idx_all[:], ptb_f[:], [ptb_f])

            def gather(dst, dst_ap, src_ap, idx_buf, idx_ap):
                tk._wait('pool', tk._deps([idx_buf.n], [dst.n]))
                key = dst.k
                if key not in tk.dsem:
                    tk.dsem[key] = es.enter_context(nc.semaphore('d_' + key))
                    tk.dcnt[key] = 0
                ins = nc.gpsimd.indirect_dma_start(out=dst_ap, out_offset=None, in_=src_ap,
                                                   in_offset=bass.IndirectOffsetOnAxis(ap=idx_ap, axis=0))
                tk.dcnt[key] += 16
                ins.then_inc(tk.dsem[key], 16)
                tk._commit((key, tk.dcnt[key]), [idx_buf.n], [dst.n])

            for half in range(2):
                gather(LF, LF[:, half, :], cache_lf[:, :], pcol, pcol[:, half:half + 1])
            for half in range(2):
                for h in range(8):
                    v2 = LF[:, half, :].rearrange("p (k h) -> p h k", h=8)[:, h, :]
                    tk.op('dve', lambda e, v2=v2: e.tensor_tensor_scan(v2, v2, zscan[:], 0.0, ALU.add, ALU.add),
                          reads=[LF.n, zscan.n], writes=[LF.n])
            LF4 = LF[:].rearrange("p a (k h) -> p a k h", h=8)
            cp(tot, tot[:], LF4[:, :, 127, :], [LF])
            tt(LF, LF4, tot[:].unsqueeze(2).to_broadcast([128, 2, 128, 8]), LF4, ALU.subtract, [tot, LF])
            mm(pss[0], pss[0][:, 0:16], Lmat[:], tot[:].rearrange("p a h -> p (a h)"), [Lmat, tot], True, True)
            cp(later, later[:].rearrange("p a h -> p (a h)"), pss[0][:, 0:16], [pss[0]])
            tr(pss[1], pss[1][0:NS, 0:8], gfs, gfs[:], ident, ident[0:8, 0:8])
            cp(gft, gft[:], pss[1][0:NS, 0:8], [pss[1]])
            for half in range(2):
                mm(pss[2], pss[2][:, half * 8:(half + 1) * 8], Emat[:, half, :], gft[:], [Emat, gft], True, True)
            tt(later, later[:].rearrange("p a h -> p (a h)"), later[:].rearrange("p a h -> p (a h)"), pss[2][:, 0:16], ALU.add,
               [later, pss[2]])
            tt(LF, LF4, LF4, later[:].unsqueeze(2).to_broadcast([128, 2, 128, 8]), ALU.add, [LF, later])
            for half in range(2):
                for h in range(8):
                    pt = pss[4 + (h % 2)]
                    tr(pt, pt[:, 0:128], LF, LF4[:, half, :, h], ident, ident[:])
                    cp(DT, DT[:, half * 128:(half + 1) * 128, h], pt[:, 0:128], [pt])
            tt(scr16, scr16[:], qsm[:, 0:512], ksm[:, 0:512], ALU.mult, [qsm, ksm])
            tk.op('dve', lambda e: e.tensor_reduce(enew[:, 0:4], scr16[:].rearrange("p (h d) -> p h d", h=4), AX.X, ALU.add),
                  reads=[scr16.n], writes=[enew.n])
            tt(scr16, scr16[:], qsm[:, 512:1024], ksm[:, 512:1024], ALU.mult, [qsm, ksm])
            tk.op('dve', lambda e: e.tensor_reduce(enew[:, 4:8], scr16[:].rearrange("p (h d) -> p h d", h=4), AX.X, ALU.add),
                  reads=[scr16.n], writes=[enew.n])
            act(enew, enew[:], enew[:], [enew], AF.Exp)
            for i in range(NS):
                for hb in range(2):
                    mm(pss[hb], pss[hb][:, :], sel_b[:, i, :], qsb[:, hb * 512:(hb + 1) * 512], [sel_b, qsb], True, True)
                    cp(qbf, qbf[:, hb * 512:(hb + 1) * 512], pss[hb][:, :], [pss[hb]])
                for pg in range(16):
                    j = i * 16 + pg
                    gather(Kpg[j % NPB], Kpg[j % NPB][:, :], cache_k[:, :], idx_all, idx_all[:, j:j + 1])
                    gather(Vpg[j % NPB], Vpg[j % NPB][:, :], cache_v[:, :], idx_all, idx_all[:, j:j + 1])
                    tt(prod, prod[:], Kpg[j % NPB][:], qbf[:], ALU.mult, [Kpg[j % NPB], qbf])
                    tk.op('dve', lambda e, j=j: e.tensor_reduce(sc_all[:, j, :], prod[:].rearrange("p (h d) -> p h d", h=8),
                                                                AX.X, ALU.add), reads=[prod.n], writes=[sc_all.n])
                    tt(sc_all, sc_all[:, j, :], sc_all[:, j, :], DT[:, j, :], ALU.add, [sc_all, DT])
                    act(e_all, e_all[:, j, :], sc_all[:, j, :], [sc_all], AF.Exp)
                    for hb in range(2):
                        mm(pss[2 + hb], pss[2 + hb][0:8, :], e_all[:, j, :], Vpg[j % NPB][:, hb * 512:(hb + 1) * 512],
                           [e_all, Vpg[j % NPB]], pg == 0, pg == 15, sig=True)
                    mm(pss[4], pss[4][0:8, 0:1], e_all[:, j, :], ones_b[:, 0:1], [e_all, ones_b], pg == 0, pg == 15, sig=True)
                for hb in range(2):
                    tt(msk, msk[:, hb * 512:(hb + 1) * 512], pss[2 + hb][0:8, :], bdm[:, hb * 512:(hb + 1) * 512], ALU.mult,
                       [pss[2 + hb], bdm])
                cp(pdc, pdc[:], pss[4][0:8, 0:1], [pss[4]])
                ts(pdd, pdd[:], ident[0:8, 0:8], pdc[:, 0:1], None, ALU.mult, None, [ident, pdc])
                for hb in range(2):
                    mm(pss[5 + hb], pss[5 + hb][0:NS, :], hsel[:, i, :], msk[:, hb * 512:(hb + 1) * 512], [hsel, msk],
                       i == 0, i == NS - 1, sig=True)
                mm(pss[7], pss[7][0:NS, 0:8], hsel[:, i, :], pdd[:], [hsel, pdd], i == 0, i == NS - 1, sig=True)
            for hb in range(2):
                cp(yacc, yacc[:, hb * 512:(hb + 1) * 512], pss[5 + hb][0:NS, :], [pss[5 + hb]])
            tt(dacc, dacc[:], pss[7][0:NS, 0:8], enew[:], ALU.add, [pss[7], enew])
            tk.op('dve', lambda e: e.reciprocal(dacc[:], dacc[:]), reads=[dacc.n], writes=[dacc.n])
            v3s = vsm[:].rearrange("p (h d) -> p h d", h=8)
            y3s = yacc[:].rearrange("p (h d) -> p h d", h=8)
            tt(vsm, v3s, v3s, enew[:].unsqueeze(2).to_broadcast([NS, 8, 128]), ALU.mult, [vsm, enew])
            tt(yacc, yacc[:], yacc[:], vsm[:], ALU.add, [yacc, vsm])
            tt(yacc, y3s, y3s, dacc[:].unsqueeze(2).to_broadcast([NS, 8, 128]), ALU.mult, [yacc, dacc])
            cp(yfb, yfb[:], yacc[:], [yacc])
            ptb = pss[0][:].bitcast(BF16)
            for c8 in range(8):
                tr(pss[0], ptb[:, c8 * NS:(c8 + 1) * NS], yfb, yfb[:, c8 * 128:(c8 + 1) * 128], identb, identb[0:NS, 0:NS])
            cp(yfTs, yfTs[:].rearrange("p c i -> p (c i)"), ptb[:, 0:8 * NS], [pss[0]])
            pha.close()

            def ydst_s():
                store('sp', y_s, xst, xst[:])

            def post_s(aS, ph_):
                cvst = ph_.sb("cvst", [NS, 512])
                for fb in range(FC // 4):
                    pt = pss[4 + fb % 2]
                    for ci in range(4):
                        tr(pt, pt[0:NS, ci * 128:(ci + 1) * 128], aS, aS[:, fb * 4 + ci, :], ident, ident[:])
                    cp(cvst, cvst[:], pt[0:NS, :], [pt])
                    store('sp', cv_s[:, 1, fb * 512:(fb + 1) * 512], cvst, cvst[:])

            dbg_hook[0] = None
            dense_tail(ssubs, NS, hTs, ymTs, yfTs, lambda idx: (xst, xst[:]),
                       lambda idx, c, n: (gsm, gsm[:, 0, c:c + n]), lambda idx, c, n: (gsm, gsm[:, 1, c:c + n]),
                       True, ydst_s, post_s)
            tk.dma('sp', "cvs0", cv_s[:, 0, :], st_conv[:, 1, :])
            phs.close()

        tk.finish()
    return nc


_CACHE = {}


def _prep_inputs(inp):
    f = np.float32
    w = {k: np.asarray(v) for k, v in inp.items()}
    ident = np.eye(128, dtype=f)
    maskT = np.triu(np.ones((128, 128), f))
    sel = np.zeros((NS, NS, 128), f)
    for i in range(NS):
        sel[i, i, :] = 1.0
    hsel = np.zeros((8, NS, NS), f)
    for i in range(NS):
        hsel[:, i, i] = 1.0
    bdm = np.repeat(np.eye(8, dtype=f), 128, axis=1)
    pidx = np.arange(128, dtype=f)[:, None].copy()
    Lmat = np.zeros((128, 128), f)
    for a in range(128):
        for b_ in range(128):
            if a // 16 == b_ // 16 and a > b_:
                Lmat[a, b_] = 1.0
    Emat = np.zeros((NS, 2, 128), f)
    for half in range(2):
        for j in range(128):
            Emat[half * 8 + j // 16, half, j] = 1.0
    b_in = w['b_in'][0]
    b_gT = np.concatenate([b_in[C_GA:C_GA + D].reshape(16, 128).T, b_in[C_GB:C_GB + D].reshape(16, 128).T], axis=1)
    b_gates = np.stack([b_in[C_MI:C_MI + 8], b_in[C_FF:C_FF + 8]], axis=1)
    common = dict(
        w_ada=w['w_ada'][0], w_in=w['w_in'][0], w_pa=w['w_proj_a'][0], w_pb=w['w_proj_b'][0], w_out=w['w_out'][0],
        w_f1=w['w_ffn_in'][0], w_f2=w['w_ffn_out'][0],
        b_adaT=np.ascontiguousarray(w['b_ada'][0].reshape(96, 128).T), b_ada_row=w['b_ada'][0][None, :].copy(),
        n1wT=np.ascontiguousarray(w['norm1_w'][0].reshape(16, 128).T), n2wT=np.ascontiguousarray(w['norm2_w'][0].reshape(16, 128).T),
        b_in_row=b_in[None, :].copy(), b_gT=np.ascontiguousarray(b_gT), b_gates=np.ascontiguousarray(b_gates),
        hnw_rep=np.ascontiguousarray(np.broadcast_to(w['m_hnorm_w'][0][None, :], (128, 1024))),
        qnw_rep=np.ascontiguousarray(np.broadcast_to(w['f_qnorm_w'][0][None, :], (128, 128))),
        knw_rep=np.ascontiguousarray(np.broadcast_to(w['f_knorm_w'][0][None, :], (128, 128))),
        cwT=np.ascontiguousarray(w['conv_w'][0].reshape(3, FC, 128).transpose(2, 1, 0)),
        cbT=np.ascontiguousarray(w['conv_b'][0].reshape(FC, 128).T),
        ident=ident, maskT=maskT, sel=sel, hsel=hsel, bdm=bdm, pidx=pidx, Lmat=Lmat, Emat=Emat,
        b_moT=np.ascontiguousarray(b_in[C_MO:C_MO + 1024].reshape(8, 128).T),
        hnwT=np.ascontiguousarray(w['m_hnorm_w'][0].reshape(8, 128).T),
        cache_k=w['cache_k'][0].reshape(NPOOL * 128, 1024), cache_v=w['cache_v'][0].reshape(NPOOL * 128, 1024),
        cache_lf=w['cache_logf'][0].reshape(NPOOL, 1024),
    )
    maps = []
    for c in range(8):
        sl = slice(c * NS, (c + 1) * NS)
        b, j = c // 4, c % 4
        nvalid = 1024 * (j + 1)
        xwin = np.zeros((T, D), f)
        xwin[T - nvalid:] = w['x_prompt'][b, :nvalid]
        vt = (np.arange(32) >= 32 - 8 * (j + 1)).astype(f)
        kmask = np.ascontiguousarray(np.broadcast_to(((vt - 1.0) * 30000.0)[None, :], (128, 32)))
        vg = vt.reshape(NG, 4)[:, 0]
        vgate = np.ascontiguousarray(np.broadcast_to(np.stack([vg, (vg - 1.0) * 1.0e4], axis=1)[None], (8, NG, 2)))
        halo_valid = np.full((128, 1), 1.0 if j > 0 else 0.0, f)
        cc = np.concatenate([w['c_sample'][sl], w['c_prompt'][b:b + 1]], axis=0)
        cT = np.ascontiguousarray(cc.reshape(NS + 1, KC, 128).transpose(2, 1, 0))
        pt = w['page_table'][sl].astype(np.int32)
        m = dict(common)
        m.update(
            xw=xwin, kmask=kmask, vgate=vgate.astype(f), halo_valid=halo_valid,
            xs=np.ascontiguousarray(w['x_sample'][sl, 0, :]), cT=cT,
            ptab=pt.reshape(1, NS * 16).copy(),
            ptab_col=np.ascontiguousarray(pt.reshape(2, 128).T),
            st_C=np.ascontiguousarray(w['state_C'][0, sl]), st_n=np.ascontiguousarray(w['state_n'][0, sl].reshape(NS, 512)),
            st_m=np.ascontiguousarray(w['state_m'][0, sl]),
            st_convT=np.ascontiguousarray(w['state_conv'][0, sl].reshape(NS, 2, FC, 128).transpose(3, 2, 0, 1)),
            st_conv=np.ascontiguousarray(w['state_conv'][0, sl]),
        )
        maps.append(m)
    return maps


def kernel(**inputs):
    stage = inputs.pop('_stage', 99)
    ng = inputs.pop('_ng', NG)
    if (stage, ng) not in _CACHE:
        _CACHE[(stage, ng)] = build_program(stage, ng)
    nc = _CACHE[(stage, ng)]
    maps = _prep_inputs(inputs)
    res = run_bass_kernel_spmd(nc, maps, core_ids=list(range(8)))
    r = res.results
    f = np.float32
    global _LAST
    _LAST = r

    def cat(name):
        return np.concatenate([np.asarray(r[c][name]) for c in range(8)], axis=0)

    def catp(name, shape):
        return np.stack([np.concatenate([np.asarray(r[b * 4 + j][name]) for j in range(4)], axis=0) for b in range(2)]).reshape(shape).astype(f)

    y_prompt = catp('y_p', (2, T, D))
    y_sample = cat('y_s').reshape(128, 1, D).astype(f)
    k_prompt = catp('k_p', (1, 2, T, 8, 128))
    v_prompt = catp('v_p', (1, 2, T, 8, 128))
    lf_prompt = catp('lf_p', (1, 2, T, 8))
    C_prompt = np.stack([r[4 * b + 3]['C_p'] for b in range(2)]).reshape(1, 2, 4, 256, 128).astype(f)
    n_prompt = np.stack([r[4 * b + 3]['n_p'] for b in range(2)]).reshape(1, 2, 4, 128).astype(f)
    m_prompt = np.stack([r[4 * b + 3]['m_p'] for b in range(2)]).reshape(1, 2, 4).astype(f)
    cv_prompt = np.stack([r[4 * b + 3]['cv_p'] for b in range(2)]).reshape(1, 2, 2, DFF).astype(f)
    k_sample = cat('k_s').reshape(1, 128, 1, 8, 128).astype(f)
    v_sample = cat('v_s').reshape(1, 128, 1, 8, 128).astype(f)
    lf_sample = cat('lf_s').reshape(1, 128, 1, 8).astype(f)
    C_sample = cat('C_s').reshape(1, 128, 4, 256, 128).astype(f)
    n_sample = cat('n_s').reshape(1, 128, 4, 128).astype(f)
    m_sample = cat('m_s').reshape(1, 128, 4).astype(f)
    cv_sample = cat('cv_s').reshape(1, 128, 2, DFF).astype(f)
    return (y_prompt, y_sample, k_prompt, v_prompt, lf_prompt, C_prompt, n_prompt, m_prompt, cv_prompt,
            k_sample, v_sample, lf_sample, C_sample, n_sample, m_sample, cv_sample)
```

```python
import contextlib
import numpy as np
import concourse.bass as bass
import concourse.mybir as mybir
from concourse.bass_utils import run_bass_kernel_spmd

F32 = mybir.dt.float32
BF16 = mybir.dt.bfloat16
I32 = mybir.dt.int32
AF = mybir.ActivationFunctionType
ALU = mybir.AluOpType
AX = mybir.AxisListType

D = 2048
KC = 16
T = 4096
NG = 8
GT = 512
NS = 16
DFF = 5632
FC = 44
PIN = 10256
EPS = 1e-6
NPOOL = 2560
C_MQ, C_MK, C_MV, C_MO, C_MI, C_MF, C_FQ, C_FK, C_FV, C_FF, C_GA, C_GB = (
    0, 512, 1024, 2048, 3072, 3076, 3080, 4104, 5128, 6152, 6160, 8208)


class Trk:
    def __init__(self, nc, es):
        self.nc = nc
        self.es = es
        self.eng = {'pe': nc.tensor, 'act': nc.scalar, 'dve': nc.vector, 'pool': nc.gpsimd, 'sp': nc.sync}
        self.sem = {}
        self.cnt = {}
        for e in self.eng:
            self.sem[e] = es.enter_context(nc.semaphore('s_' + e))
            self.cnt[e] = 0
        self.waited = {e: {} for e in self.eng}
        self.bw = {}
        self.br = {}
        self.dsem = {}
        self.dcnt = {}
        self.ninst = 0

    def _deps(self, reads, writes):
        deps = []
        for r in reads:
            if r in self.bw:
                deps.append(self.bw[r])
        for w in writes:
            if w in self.bw:
                deps.append(self.bw[w])
            deps.extend(self.br.get(w, []))
        return deps

    def _wait(self, e, deps):
        need = {}
        for (k, v) in deps:
            if k == e and e == 'pe':
                continue
            if v > need.get(k, 0):
                need[k] = v
        for k, v in need.items():
            if self.waited[e].get(k, 0) >= v:
                continue
            s = self.sem[k] if k in self.sem else self.dsem[k]
            self.eng[e].wait_ge(s, v)
            self.waited[e][k] = v

    def _commit(self, tok, reads, writes):
        for r in reads:
            self.br.setdefault(r, []).append(tok)
        for w in writes:
            self.bw[w] = tok
            self.br[w] = []

    def op(self, e, fn, reads=(), writes=(), signal=True):
        self._wait(e, self._deps(reads, writes))
        ins = fn(self.eng[e])
        if signal:
            self.cnt[e] += 1
            ins.then_inc(self.sem[e], 1)
            tok = (e, self.cnt[e])
        else:
            tok = (e, self.cnt[e] + 1)
        self._commit(tok, reads, writes)
        self.ninst += 1

    def dma(self, q, key, out, in_, reads=(), writes=(), **kw):
        self._wait(q, self._deps(reads, writes))
        if key not in self.dsem:
            self.dsem[key] = self.es.enter_context(self.nc.semaphore('d_' + key))
            self.dcnt[key] = 0
        ins = self.eng[q].dma_start(out=out, in_=in_, **kw)
        self.dcnt[key] += 16
        ins.then_inc(self.dsem[key], 16)
        self._commit((key, self.dcnt[key]), reads, writes)
        self.ninst += 1

    def barrier(self):
        for e in self.eng:
            for k in self.eng:
                if k != e and self.cnt[k] > self.waited[e].get(k, 0):
                    self.eng[e].wait_ge(self.sem[k], self.cnt[k])
                    self.waited[e][k] = self.cnt[k]
            for k, sm_ in self.dsem.items():
                if self.dcnt[k] > self.waited[e].get(k, 0):
                    self.eng[e].wait_ge(sm_, self.dcnt[k])
                    self.waited[e][k] = self.dcnt[k]

    def finish(self):
        for k, s in self.dsem.items():
            self.eng['sp'].wait_ge(s, self.dcnt[k])
        for e in self.eng:
            if e != 'sp' and self.cnt[e] > 0:
                self.eng['sp'].wait_ge(self.sem[e], self.cnt[e])


def build_program(stage=99, ng=NG):
    nc = bass.Bass("TRN2", target_bir_lowering=False)

    def din(name, shape, dt=F32):
        return nc.dram_tensor(name, list(shape), dt, kind="ExternalInput").ap()

    def dout(name, shape, dt=F32):
        return nc.dram_tensor(name, list(shape), dt, kind="ExternalOutput").ap()

    def dscr(name, shape, dt=F32):
        return nc.dram_tensor(name, list(shape), dt, kind="Internal").ap()

    xw = din("xw", [T, D])
    xs = din("xs", [NS, D])
    cT_in = din("cT", [128, KC, NS + 1])
    w_ada = din("w_ada", [D, 6 * D])
    w_in = din("w_in", [D, PIN])
    w_pa = din("w_pa", [1024, D])
    w_pb = din("w_pb", [1024, D])
    w_out = din("w_out", [D, D])
    w_f1 = din("w_f1", [D, 2 * DFF])
    w_f2 = din("w_f2", [DFF, D])
    b_adaT = din("b_adaT", [128, 96])
    b_ada_row = din("b_ada_row", [1, 6 * D])
    n1wT = din("n1wT", [128, KC])
    n2wT = din("n2wT", [128, KC])
    b_in_row = din("b_in_row", [1, PIN])
    b_gT = din("b_gT", [128, 32])
    b_gates = din("b_gates", [8, 2])
    hnw_rep = din("hnw_rep", [128, 1024])
    qnw_rep = din("qnw_rep", [128, 128])
    knw_rep = din("knw_rep", [128, 128])
    cwT = din("cwT", [128, FC, 3])
    cbT = din("cbT", [128, FC])
    ident_in = din("ident", [128, 128])
    maskT_in = din("maskT", [128, 128])
    cache_k = din("cache_k", [NPOOL * 128, 1024])
    cache_v = din("cache_v", [NPOOL * 128, 1024])
    cache_lf = din("cache_lf", [NPOOL, 1024])
    ptab = din("ptab", [1, NS * 16], I32)
    ptab_col = din("ptab_col", [128, 2], I32)
    st_C = din("st_C", [NS, 4, 256, 128])
    st_n = din("st_n", [NS, 512])
    st_m = din("st_m", [NS, 4])
    st_convT = din("st_convT", [128, FC, NS, 2])
    st_conv = din("st_conv", [NS, 2, DFF])
    sel_in = din("sel", [NS, NS, 128])
    hsel_in = din("hsel", [8, NS, NS])
    kmask_in = din("kmask", [128, 32])
    vgate_in = din("vgate", [8, NG, 2])
    halo_valid_in = din("halo_valid", [128, 1])
    bdm_in = din("bdm", [8, 1024])
    pidx_in = din("pidx", [128, 1])
    Lmat_in = din("Lmat", [128, 128])
    Emat_in = din("Emat", [NS, 2, 128])
    b_moT_in = din("b_moT", [128, 8])
    hnwT_in = din("hnwT", [128, 8])

    y_p = dout("y_p", [1024, D])
    y_s = dout("y_s", [NS, D])
    k_p = dout("k_p", [1024, 1024])
    v_p = dout("v_p", [1024, 1024])
    lf_p = dout("lf_p", [1024, 8])
    C_p = dout("C_p", [4, 256, 128])
    n_p = dout("n_p", [4, 128])
    m_p = dout("m_p", [4, 1])
    cv_p = dout("cv_p", [2, DFF])
    k_s = dout("k_s", [NS, 1024])
    v_s = dout("v_s", [NS, 1024])
    lf_s = dout("lf_s", [NS, 8])
    C_s = dout("C_s", [NS, 4, 256, 128])
    n_s = dout("n_s", [NS, 512])
    m_s = dout("m_s", [NS, 4])
    cv_s = dout("cv_s", [NS, 2, DFF])
    dbg_ymT = dout("dbg_ymT", [128, 8, GT], BF16)
    dbg_yfT = dout("dbg_yfT", [128, 8, GT], BF16)
    dbg_x1 = dout("dbg_x1", [GT, D])

    kT_scr = dscr("kT_scr", [8, 128, T], BF16)
    v_scr = dscr("v_scr", [8, 128, 32, 130], BF16)
    F_scr = dscr("F_scr", [8, T])
    F3_scr = dscr("F3_scr", [3, 8, T], BF16)
    U_scr = dscr("U_scr", [4, T])
    B_scr = dscr("B_scr", [4, T])
    Ae_scr = dscr("Ae_scr", [4, 33])
    G_scr = dscr("G_scr", [8, GT])
    gsm_scr = dscr("gsm_scr", [NS, 2, D], BF16)

    es = contextlib.ExitStack()
    with es:
        tk = Trk(nc, es)

        es_loop = contextlib.ExitStack()
        es_samp = contextlib.ExitStack()

        def sb(name, shape, dt=F32, st=None):
            return (st or es).enter_context(nc.sbuf_tensor("sb_" + name, list(shape), dt))

        def ps(name, shape, dt=F32):
            return es.enter_context(nc.psum_tensor("pp_" + name, list(shape), dt))

        uid = [0]

        class Buf:
            def __init__(self, t, n, k=None):
                self.t = t
                self.n = n
                self.k = k or n

            def __getitem__(self, k):
                return self.t[k]

        def sbuf(name, shape, dt=F32, st=None):
            uid[0] += 1
            t = (st or es).enter_context(nc.sbuf_tensor("sb%d_%s" % (uid[0], name), list(shape), dt))
            return Buf(t, "%s_%d" % (name, uid[0]), name)

        class Phase:
            def __init__(self):
                self.st = contextlib.ExitStack()

            def sb(self, name, shape, dt=F32):
                return sbuf(name, shape, dt, st=self.st)

            def close(self):
                tk.barrier()
                self.st.close()

        ident = sbuf("ident", [128, 128])
        identb = sbuf("identb", [128, 128], BF16)
        maskT = sbuf("maskT", [128, 128])
        maskTb = sbuf("maskTb", [128, 128], BF16)
        ones_b = sbuf("ones_b", [128, 128], BF16)
        cTs = sbuf("cTs", [128, KC, NS + 1])
        cTb = sbuf("cTb", [128, KC, NS + 1], BF16)
        modT = sbuf("modT", [128, 4, KC, NS + 1])
        grep = sbuf("grep", [128, 2, D], BF16)
        b_adaT_s = sbuf("b_adaT_s", [128, 96])
        brow = [sbuf("brow0", [1, 512], BF16), sbuf("brow1", [1, 512], BF16)]
        n1wT_s = sbuf("n1wT_s", [128, KC])
        n2wT_s = sbuf("n2wT_s", [128, KC])
        A1 = sbuf("A1", [128, KC, NS + 1])
        A2 = sbuf("A2", [128, KC, NS + 1])
        b_gT_s = sbuf("b_gT_s", [128, 32])
        b_gates_s = sbuf("b_gates_s", [8, 2])
        hnw_s = sbuf("hnw_s", [128, 1024])
        qnw_s = sbuf("qnw_s", [128, 128])
        knw_s = sbuf("knw_s", [128, 128])
        cwT_s = sbuf("cwT_s", [128, FC, 3])
        cbT_s = sbuf("cbT_s", [128, FC])
        wg_m = sbuf("wg_m", [128, KC, 8], BF16)
        wg_f = sbuf("wg_f", [128, KC, 8], BF16)
        st1 = sbuf("st1", [128, 8])
        nrm8 = sbuf("nrm8", [128, 8])
        St = sbuf("St", [128, 4, 257])
        aprev = sbuf("aprev", [128, FC, 2])
        nFk = sbuf("nFk", [128, 8, 32])
        zcol = sbuf("zcol", [8, 1])
        kmask = sbuf("kmask", [128, 32])
        vgate = sbuf("vgate", [8, NG, 2])
        halo_valid = sbuf("halo_valid", [128, 1])
        Bc = sbuf("Bc", [4, 1])
        Ac = sbuf("Ac", [4, 1])
        Fc = sbuf("Fc", [8, 1])
        Ucol = sbuf("Ucol", [128, 4, 4])
        Bcol = sbuf("Bcol", [128, 4, 4])
        ALb = sbuf("ALb", [128, 4, 5])
        wcol = sbuf("wcol", [128, 4, 4])
        sccol = sbuf("sccol", [128, 4, 4])
        thrcol = sbuf("thrcol", [128, 4, 4])
        xn = sbuf("xn", [128, D], BF16)
        wb = [sbuf("wb0", [128, 8192], BF16), sbuf("wb1", [128, 8192], BF16)]
        pss = [Buf(ps("ps%d" % i, [128, 512]), "ps%d" % i) for i in range(8)]

        def names(bufs):
            return [b if isinstance(b, str) else b.n for b in bufs]

        def load(q, dst, dst_ap, src):
            tk.dma(q, dst.k, dst_ap, src, writes=[dst.n])

        def store(q, dst, src, src_ap, extra_writes=(), extra_reads=()):
            tk.dma(q, "st_" + src.k, dst, src_ap, reads=[src.n] + list(extra_reads), writes=list(extra_writes))

        wslot = [0]

        def load_w(src_ap, shape_view, bias_ap=None):
            i = wslot[0] % 2
            wslot[0] += 1
            if bias_ap is not None:
                tk.dma('pool', brow[i].k, brow[i][0:1, 0:bias_ap.shape[-1]], bias_ap, writes=[brow[i].n])
            n = 1
            for s_ in shape_view[1:]:
                n *= s_
            flat = wb[i][:, 0:n]
            view = flat.rearrange("p (a b) -> p a b", a=shape_view[1]) if len(shape_view) == 3 else flat
            tk.dma('pool', wb[i].k, view, src_ap, writes=[wb[i].n])
            return view, wb[i], brow[i]

        def mm(out, out_ap, lhsT, rhs, reads, start, stop, sig=None):
            tk.op('pe', lambda e: e.matmul(out_ap, lhsT, rhs, start=start, stop=stop), reads=names(reads), writes=[out.n],
                  signal=True)

        def tr(out, out_ap, src, in_ap, idt, idt_ap):
            tk.op('pe', lambda e: e.transpose(out_ap, in_ap, idt_ap), reads=[src.n, idt.n], writes=[out.n])

        def act(out, out_ap, in_ap, reads, func, bias=None, scale=None, accum=None, extra_writes=()):
            kw = {}
            if bias is not None:
                kw['bias'] = bias
            if scale is not None:
                kw['scale'] = scale
            if accum is not None:
                kw['accum_out'] = accum
            tk.op('act', lambda e: e.activation(out_ap, in_ap, func, **kw), reads=names(reads),
                  writes=[out.n] + names(extra_writes))

        def tt(out, out_ap, a, b, op, reads, eng='dve'):
            tk.op(eng, lambda e: e.tensor_tensor(out_ap, a, b, op), reads=names(reads), writes=[out.n])

        def ts(out, out_ap, a, s1, s2, op0, op1, reads, eng='dve'):
            if op1 is None:
                tk.op(eng, lambda e: e.tensor_scalar(out_ap, a, s1, None, op0), reads=names(reads), writes=[out.n])
            else:
                tk.op(eng, lambda e: e.tensor_scalar(out_ap, a, s1, s2, op0, op1), reads=names(reads), writes=[out.n])

        def stt(out, out_ap, in0, scalar, in1, op0, op1, reads):
            tk.op('dve', lambda e: e.scalar_tensor_tensor(out_ap, in0, scalar, in1, op0, op1), reads=names(reads),
                  writes=[out.n])

        def cp(out, out_ap, a, reads, eng='dve'):
            tk.op(eng, lambda e: e.tensor_copy(out_ap, a), reads=names(reads), writes=[out.n])

        def memset(out, out_ap, v):
            tk.op('dve', lambda e: e.memset(out_ap, v), writes=[out.n])

        def logsig(dst, dst_ap, src_ap, reads):
            act(dst, dst_ap, src_ap, reads, AF.Exp, scale=-1.0)
            act(dst, dst_ap, dst_ap, [dst], AF.Ln, bias=1.0)
            ts(dst, dst_ap, dst_ap, -1.0, None, ALU.mult, None, [dst])

        def rsqrt_mean(dst, dst_ap, src_ap, reads, n):
            ts(dst, dst_ap, src_ap, 1.0 / n, EPS, ALU.mult, ALU.add, reads)
            act(dst, dst_ap, dst_ap, [dst], AF.Ln)
            act(dst, dst_ap, dst_ap, [dst], AF.Exp, scale=-0.5)

        NCA = nc.allow_non_contiguous_dma(reason="tiny re-layout DMAs")
        es.enter_context(NCA)

        for (dst, src) in [(ident, ident_in), (maskT, maskT_in), (cTs, cT_in), (b_adaT_s, b_adaT), (n1wT_s, n1wT),
                           (n2wT_s, n2wT), (b_gT_s, b_gT), (b_gates_s, b_gates), (hnw_s, hnw_rep), (qnw_s, qnw_rep),
                           (knw_s, knw_rep), (cwT_s, cwT), (cbT_s, cbT), (kmask, kmask_in), (vgate, vgate_in),
                           (halo_valid, halo_valid_in)]:
            load('sp', dst, dst[:], src)
        w_in_v = w_in.rearrange("(kc p) c -> p kc c", p=128)
        tk.dma('pool', wg_m.k, wg_m[:], w_in_v[:, :, C_MI:C_MI + 8], writes=[wg_m.n])
        tk.dma('pool', wg_f.k, wg_f[:], w_in_v[:, :, C_FF:C_FF + 8], writes=[wg_f.n])
        cp(identb, identb[:], ident[:], [ident])
        cp(maskTb, maskTb[:], maskT[:], [maskT])
        memset(ones_b, ones_b[:], 1.0)
        memset(zcol, zcol[:], 0.0)
        memset(Bc, Bc[:], 0.0)
        memset(Ac, Ac[:], 0.0)
        memset(Fc, Fc[:], 0.0)
        memset(St, St[:], 0.0)
        memset(aprev, aprev[:], 0.0)
        store('sp', Ae_scr[:, 0:1], zcol, zcol[0:4, 0:1], extra_writes=["Ae_scr"])

        ph = Phase()
        cTrep = ph.sb("cTrep", [128, KC, 128], BF16)
        gsm = ph.sb("gsm", [NS, 2, D], BF16)
        act(cTb, cTb[:], cTs[:], [cTs], AF.Silu)
        for kc in range(KC):
            cp(cTrep, cTrep[:, kc, :], cTb[:, kc, NS:NS + 1].to_broadcast([128, 128]), [cTb])
        w_ada_v = w_ada.rearrange("(kc p) c -> p kc c", p=128)
        fm_slots = {0: 0, 1: 1, 3: 2, 4: 3}
        for part in range(6):
            for cb in range(4):
                c0 = part * D + cb * 512
                wv, wn, br_ = load_w(w_ada_v[:, :, c0:c0 + 512], [128, KC, 512], b_ada_row[0:1, c0:c0 + 512])
                if part in fm_slots:
                    slot = fm_slots[part]
                    for ct in range(4):
                        pt = pss[ct % 2]
                        for kc in range(KC):
                            mm(pt, pt[:, 0:NS + 1], wv[:, kc, ct * 128:(ct + 1) * 128], cTb[:, kc, :], [wn, cTb],
                               kc == 0, kc == KC - 1)
                        ch = cb * 4 + ct
                        act(modT, modT[:, slot, ch, :], pt[:, 0:NS + 1], [pt, b_adaT_s], AF.Identity,
                            bias=b_adaT_s[:, part * 16 + ch:part * 16 + ch + 1])
                else:
                    gi = 0 if part == 2 else 1
                    for (lh, M, dst, pidx) in [(cTrep, 128, grep, 2), (cTb, NS, gsm, 3)]:
                        pt = pss[pidx]
                        for kc in range(KC):
                            lhs = lh[:, kc, :] if M == 128 else lh[:, kc, 0:NS]
                            mm(pt, pt[0:M, :], lhs, wv[:, kc, :], [wn, lh], kc == 0, False)
                        mm(pt, pt[0:M, :], ones_b[0:1, 0:M], br_[0:1, :], [ones_b, br_], False, True)
                        cp(dst, dst[0:M, gi, cb * 512:(cb + 1) * 512], pt[0:M, :], [pt])
        for (A, slot, nw) in [(A1, 1, n1wT_s), (A2, 3, n2wT_s)]:
            ts(A, A[:], modT[:, slot, :, :], 1.0, None, ALU.add, None, [modT])
            tt(A, A[:], A[:], nw[:].unsqueeze(2).to_broadcast([128, KC, NS + 1]), ALU.mult, [A, nw])
        store('sp', gsm_scr, gsm, gsm[:], extra_writes=["gsm_scr"])
        ph.close()

        def norm_tile(x, x_ap, M, A, shslot, hdst, col0, mcol):
            act(xn, xn[0:M, :], x_ap, [x], AF.Square, accum=st1[0:M, 0:1], extra_writes=[st1])
            rsqrt_mean(st1, st1[0:M, 1:2], st1[0:M, 0:1], [st1], D)
            ts(xn, xn[0:M, :], x_ap, st1[0:M, 1:2], None, ALU.mult, None, [x, st1])
            for q4 in range(4):
                pt = pss[4 + (q4 % 2)]
                ptb = pt[:].bitcast(BF16)
                for i in range(4):
                    kc = q4 * 4 + i
                    tr(pt, ptb[:, i * 128:i * 128 + M], xn, xn[0:M, kc * 128:(kc + 1) * 128], identb, identb[0:M, 0:M])
                for i in range(4):
                    kc = q4 * 4 + i
                    if mcol is not None:
                        act(hdst, hdst[:, kc, col0:col0 + M], ptb[:, i * 128:i * 128 + M], [pt, modT, A], AF.Identity,
                            bias=modT[:, shslot, kc, mcol:mcol + 1], scale=A[:, kc, mcol:mcol + 1])
                    else:
                        tt(hdst, hdst[:, kc, col0:col0 + M], ptb[:, i * 128:i * 128 + M], A[:, kc, 0:NS], ALU.mult, [pt, A])
                        tt(hdst, hdst[:, kc, col0:col0 + M], hdst[:, kc, col0:col0 + M], modT[:, shslot, kc, 0:NS], ALU.add,
                           [hdst, modT])

        def qknorm(kb, k3, M, wt, scr):
            tq = scr[0:M, :].rearrange("p (h d) -> p h d", h=4)
            tt(scr, tq, k3, k3, ALU.mult, [kb])
            tk.op('dve', lambda e: e.tensor_reduce(nrm8[0:M, 0:4], tq, AX.X, ALU.add), reads=[scr.n], writes=[nrm8.n])
            rsqrt_mean(nrm8, nrm8[0:M, 0:4], nrm8[0:M, 0:4], [nrm8], 128)
            tt(kb, k3, k3, nrm8[0:M, 0:4].unsqueeze(2).to_broadcast([M, 4, 128]), ALU.mult, [kb, nrm8])
            tt(kb, k3, k3, wt[0:M, :].unsqueeze(1).to_broadcast([M, 4, 128]), ALU.mult, [kb, wt])

        def proj_tok(hsrc, subs, w_v, b_row, c0, nblk, consume):
            pending = [None]
            for blk in range(nblk):
                cc = c0 + blk * 512
                wv, wn, br_ = load_w(w_v[:, :, cc:cc + 512], [128, KC, 512], None if b_row is None else b_row[0:1, cc:cc + 512])
                for si, (M, col0, idx) in enumerate(subs):
                    pt = pss[si % 4]
                    for kc in range(KC):
                        mm(pt, pt[0:M, :], hsrc[:, kc, col0:col0 + M], wv[:, kc, :], [wn, hsrc], kc == 0,
                           (kc == KC - 1) and b_row is None)
                    if b_row is not None:
                        mm(pt, pt[0:M, :], ones_b[0:1, 0:M], br_[0:1, :], [ones_b, br_], False, True)
                    if pending[0] is not None:
                        pending[0]()
                    pending[0] = consume(M, idx, blk, pt)
            if pending[0] is not None:
                pending[0]()
                pending[0] = None

        def gates(wg, bcol, dst, hsrc, n_):
            pt = pss[6]
            for kc in range(KC):
                mm(pt, pt[0:8, 0:n_], wg[:, kc, :], hsrc[:, kc, 0:n_], [wg, hsrc], kc == 0, kc == KC - 1)
            act(dst, dst[:, 0:n_], pt[0:8, 0:n_], [pt, b_gates_s], AF.Identity, bias=b_gates_s[:, bcol:bcol + 1])

        dbg_hook = [None]

        def dense_tail(subs, ntok, hsrc, ymT_, yfT_, xres, g1src, g2src, sample, y_dst_fn, post_fn=None, halo=False):
            ph_ = Phase()
            mgT = ph_.sb("mgT", [128, KC, ntok], BF16)
            sg = ph_.sb("sg", [128, 4, ntok])
            tA = ph_.sb("tA", [128, 4, ntok])
            w_pa_v = w_pa.rearrange("(kc p) c -> p kc c", p=128)
            w_pb_v = w_pb.rearrange("(kc p) c -> p kc c", p=128)
            for cb in range(4):
                for br_i, (wp_v, ysrc, cg, boff) in enumerate([(w_pa_v, ymT_, C_GA, 0), (w_pb_v, yfT_, C_GB, 16)]):
                    wv, wn, _ = load_w(wp_v[:, :, cb * 512:(cb + 1) * 512], [128, 8, 512])
                    for ct in range(4):
                        pt = pss[ct]
                        for kc in range(8):
                            mm(pt, pt[:, 0:ntok], wv[:, kc, ct * 128:(ct + 1) * 128], ysrc[:, kc, 0:ntok], [wn, ysrc],
                               kc == 0, kc == 7)
                    wv2, wn2, _ = load_w(w_in_v[:, :, cg + cb * 512:cg + (cb + 1) * 512], [128, KC, 512])
                    for ct in range(4):
                        pt = pss[4 + ct]
                        for kc in range(KC):
                            mm(pt, pt[:, 0:ntok], wv2[:, kc, ct * 128:(ct + 1) * 128], hsrc[:, kc, 0:ntok], [wn2, hsrc],
                               kc == 0, kc == KC - 1)
                    for ct in range(4):
                        ch = cb * 4 + ct
                        act(sg, sg[:, ct, :], pss[4 + ct][:, 0:ntok], [pss[4 + ct], b_gT_s], AF.Sigmoid,
                            bias=b_gT_s[:, boff + ch:boff + ch + 1])
                        if br_i == 0:
                            tt(tA, tA[:, ct, :], pss[ct][:, 0:ntok], sg[:, ct, :], ALU.mult, [pss[ct], sg])
                        else:
                            tt(sg, sg[:, ct, :], pss[ct][:, 0:ntok], sg[:, ct, :], ALU.mult, [pss[ct], sg])
                            tt(mgT, mgT[:, ch, :], sg[:, ct, :], tA[:, ct, :], ALU.add, [sg, tA])
            tmp = ph_.sb("tmp", [128, 512])
            w_out_v = w_out.rearrange("(kc p) c -> p kc c", p=128)

            def cons_o(M, idx, blk, pt):
                xb, xap = xres(idx)
                gb_, gap = g1src(idx, blk * 512, 512)
                tt(tmp, tmp[0:M, :], pt[0:M, :], gap, ALU.mult, [pt, gb_])
                tt(xb, xap[:, blk * 512:(blk + 1) * 512], xap[:, blk * 512:(blk + 1) * 512], tmp[0:M, :], ALU.add, [xb, tmp])

            proj_tok(mgT, subs, w_out_v, None, 0, 4, cons_o)
            if dbg_hook[0] is not None:
                dbg_hook[0]()
            ph_.close()
            ph_ = Phase()
            tmp = ph_.sb("tmp", [128, 512])
            for (M, col0, idx) in subs:
                xb, xap = xres(idx)
                norm_tile(xb, xap, M, A2, 2, hsrc, col0, None if sample else NS)
            uT = None if halo else ph_.sb("uT", [128, FC, ntok], BF16)
            ab = ph_.sb("ab", [128, ntok + 2])
            t1 = ph_.sb("t1", [128, ntok])
            t2 = ph_.sb("t2", [128, ntok])
            if sample:
                cvT = ph_.sb("cvT", [128, FC, NS, 2])
                load('sp', cvT, cvT[:], st_convT)
                aS = ph_.sb("aS", [128, FC, NS])
            w_f1_v = w_f1.rearrange("(kc p) c -> p kc c", p=128)
            for fb in range(FC // 4):
                wa, wan, _ = load_w(w_f1_v[:, :, fb * 512:(fb + 1) * 512], [128, KC, 512])
                if not halo:
                    wg_, wgn, _ = load_w(w_f1_v[:, :, DFF + fb * 512:DFF + (fb + 1) * 512], [128, KC, 512])
                for ci in range(4):
                    fc = fb * 4 + ci
                    pa = pss[ci % 2]
                    pg = pss[2 + ci % 2]
                    for kc in range(KC):
                        mm(pa, pa[:, 0:ntok], wa[:, kc, ci * 128:(ci + 1) * 128], hsrc[:, kc, 0:ntok], [wan, hsrc], kc == 0,
                           kc == KC - 1)
                    if halo:
                        cp(aprev, aprev[:, fc, :], pa[:, ntok - 2:ntok], [pa])
                        continue
                    for kc in range(KC):
                        mm(pg, pg[:, 0:ntok], wg_[:, kc, ci * 128:(ci + 1) * 128], hsrc[:, kc, 0:ntok], [wgn, hsrc], kc == 0,
                           kc == KC - 1)
                    w0 = cwT_s[:, fc, 0:1]
                    w1 = cwT_s[:, fc, 1:2]
                    w2 = cwT_s[:, fc, 2:3]
                    if not sample:
                        cp(ab, ab[:, 0:2], aprev[:, fc, :], [aprev])
                        cp(ab, ab[:, 2:2 + ntok], pa[:, 0:ntok], [pa])
                        cp(aprev, aprev[:, fc, :], ab[:, ntok:ntok + 2], [ab])
                        ts(t1, t1[:], ab[:, 0:ntok], w0, cbT_s[:, fc:fc + 1], ALU.mult, ALU.add, [ab, cwT_s, cbT_s])
                        stt(t1, t1[:], ab[:, 1:1 + ntok], w1, t1[:], ALU.mult, ALU.add, [ab, t1, cwT_s])
                        stt(t1, t1[:], ab[:, 2:2 + ntok], w2, t1[:], ALU.mult, ALU.add, [ab, t1, cwT_s])
                    else:
                        cp(aS, aS[:, fc, :], pa[:, 0:ntok], [pa])
                        ts(t1, t1[:], cvT[:, fc, :, 0], w0, cbT_s[:, fc:fc + 1], ALU.mult, ALU.add, [cvT, cwT_s, cbT_s])
                        stt(t1, t1[:], cvT[:, fc, :, 1], w1, t1[:], ALU.mult, ALU.add, [cvT, t1, cwT_s])
                        stt(t1, t1[:], aS[:, fc, :], w2, t1[:], ALU.mult, ALU.add, [aS, t1, cwT_s])
                    tt(t2, t2[:], t1[:], t1[:], ALU.mult, [t1])
                    ts(t2, t2[:], t2[:], 0.044715, 1.0, ALU.mult, ALU.add, [t2])
                    tt(t2, t2[:], t2[:], t1[:], ALU.mult, [t2, t1])
                    act(t2, t2[:], t2[:], [t2], AF.Sigmoid, scale=1.5957691216057308)
                    tt(t2, t2[:], t2[:], t1[:], ALU.mult, [t2, t1])
                    tt(uT, uT[:, fc, :], t2[:], pg[:, 0:ntok], ALU.mult, [t2, pg])
            w_f2_v = w_f2.rearrange("(kc p) c -> p kc c", p=128)
            for oc in range(0 if halo else KC):
                wv, wn, _ = load_w(w_f2_v[:, :, oc * 128:(oc + 1) * 128], [128, FC, 128])
                for si, (M, col0, idx) in enumerate(subs):
                    pt = pss[si % 4]
                    for kc in range(FC):
                        mm(pt, pt[0:M, 0:128], uT[:, kc, col0:col0 + M], wv[:, kc, :], [wn, uT], kc == 0, kc == FC - 1)
                    xb, xap = xres(idx)
                    gb_, gap = g2src(idx, oc * 128, 128)
                    tt(tmp, tmp[0:M, 0:128], pt[0:M, 0:128], gap, ALU.mult, [pt, gb_])
                    tt(xb, xap[:, oc * 128:(oc + 1) * 128], xap[:, oc * 128:(oc + 1) * 128], tmp[0:M, 0:128], ALU.add,
                       [xb, tmp])
            y_dst_fn()
            if post_fn is not None:
                post_fn(aS if sample else None, ph_)
            ph_.close()

        php = Phase()
        xt = php.sb("xt", [128, 4, D])
        hT = php.sb("hT", [128, KC, GT], BF16)
        ymT = php.sb("ymT", [128, 8, GT], BF16)
        yfT = php.sb("yfT", [128, 8, GT], BF16)
        Cst = php.sb("Cst", [128, 128])
        xw_v = xw.rearrange("(g t p) d -> g p t d", p=128, t=4)
        yp_v = y_p.rearrange("(g t p) d -> g p t d", p=128, t=4)
        kp_v = k_p.rearrange("(g t p) d -> g p t d", p=128, t=4)
        vp_v = v_p.rearrange("(g t p) d -> g p t d", p=128, t=4)
        ngroups = ng
        xt_loaded = [False]
        psubs = [(128, t4 * 128, t4) for t4 in range(4)]
        for g in range(ngroups):
            mode = 'prefix' if g < ngroups - 3 else ('halo' if g == ngroups - 3 else 'own')
            out_tiles = [] if mode == 'prefix' else ([3] if mode == 'halo' else [0, 1, 2, 3])
            osubs = [(128, t4 * 128, t4) for t4 in out_tiles]
            lo = g - (ngroups - 2)
            if not xt_loaded[0]:
                load('sp', xt, xt[:], xw_v[g])
            xt_loaded[0] = False
            for t4 in range(4):
                norm_tile(xt, xt[:, t4, :], 128, A1, 0, hT, t4 * 128, NS)
            if mode == 'prefix' and g + 1 < ngroups:
                load('sp', xt, xt[:], xw_v[g + 1])
                xt_loaded[0] = True
            phr = Phase()
            gm = phr.sb("gm", [8, GT])
            gf = phr.sb("gf", [8, GT])
            Ff = phr.sb("Ff", [8, GT])
            Fr = phr.sb("Fr", [8, GT])
            F3 = phr.sb("F3", [8, 3, GT], BF16)
            ig4 = phr.sb("ig4", [4, GT])
            mf4 = phr.sb("mf4", [4, GT])
            Bm = phr.sb("Bm", [4, GT])
            Um = phr.sb("Um", [4, GT])
            Am = phr.sb("Am", [4, GT])
            zrow = phr.sb("zrow", [8, GT])
            memset(zrow, zrow[:], 0.0)
            gates(wg_m, 0, gm, hT, GT)
            gates(wg_f, 1, gf, hT, GT)
            logsig(gf, gf[:], gf[:], [gf])
            ts(gf, gf[:], gf[:], vgate[:, g, 0:1], None, ALU.mult, None, [gf, vgate])
            if mode == 'own':
                store('sp', lf_p[lo * GT:(lo + 1) * GT, :].rearrange("t h -> h t"), gf, gf[:])
            tk.op('dve', lambda e: e.tensor_tensor_scan(Ff[:], gf[:], zrow[:], Fc[:], ALU.add, ALU.add),
                  reads=[gf.n, zrow.n, Fc.n], writes=[Ff.n])
            cp(Fc, Fc[:], Ff[:, GT - 1:GT], [Ff])
            store('sp', F_scr[:, g * GT:(g + 1) * GT], Ff, Ff[:], extra_writes=["F_scr"])
            cp(F3, F3[:, 0, :], Ff[:], [Ff])
            tt(Fr, Fr[:], Ff[:], F3[:, 0, :], ALU.subtract, [Ff, F3])
            cp(F3, F3[:, 1, :], Fr[:], [Fr])
            tt(Fr, Fr[:], Fr[:], F3[:, 1, :], ALU.subtract, [Fr, F3])
            cp(F3, F3[:, 2, :], Fr[:], [Fr])
            store('sp', F3_scr[:, :, g * GT:(g + 1) * GT].rearrange("j h t -> h j t"), F3, F3[:], extra_writes=["F3_scr"])
            for h in range(8):
                tk.dma('sp', nFk.k, nFk[:, h, 4 * g:4 * g + 4],
                       F_scr[h, g * GT:(g + 1) * GT].rearrange("(t p) -> p t", p=128), reads=["F_scr", nFk.n],
                       writes=["nFk_part%d" % h])
            ts(nFk, nFk[:, :, 4 * g:4 * g + 4], nFk[:, :, 4 * g:4 * g + 4], -1.0, None, ALU.mult, None,
               [nFk] + ["nFk_part%d" % h for h in range(8)])
            tt(nFk, nFk[:, :, 4 * g:4 * g + 4], nFk[:, :, 4 * g:4 * g + 4],
               kmask[:, 4 * g:4 * g + 4].unsqueeze(1).to_broadcast([128, 8, 4]), ALU.add, [nFk, kmask])
            store('sp', G_scr[:, 0:GT], gm, gm[:, 0:GT], extra_writes=["G_scr"])
            tk.dma('sp', ig4.k, ig4[:], G_scr[0:4, 0:GT], reads=["G_scr"], writes=[ig4.n])
            tk.dma('sp', mf4.k, mf4[:], G_scr[4:8, 0:GT], reads=["G_scr"], writes=[mf4.n])
            logsig(mf4, mf4[:], mf4[:], [mf4])
            ts(mf4, mf4[:], mf4[:], vgate[0:4, g, 0:1], None, ALU.mult, None, [mf4, vgate])
            ts(ig4, ig4[:], ig4[:], vgate[0:4, g, 0:1], vgate[0:4, g, 1:2], ALU.mult, ALU.add, [ig4, vgate])
            tk.op('dve', lambda e: e.tensor_tensor_scan(Bm[:], mf4[:], zrow[0:4, :], Bc[:], ALU.add, ALU.add),
                  reads=[mf4.n, zrow.n, Bc.n], writes=[Bm.n])
            tt(Um, Um[:], ig4[:], Bm[:], ALU.subtract, [ig4, Bm])
            tk.op('dve', lambda e: e.tensor_tensor_scan(Am[:], Um[:], Um[:], Ac[:], ALU.max, ALU.max),
                  reads=[Um.n, Ac.n], writes=[Am.n])
            cp(Bc, Bc[:], Bm[:, GT - 1:GT], [Bm])
            cp(Ac, Ac[:], Am[:, GT - 1:GT], [Am])
            store('sp', U_scr[:, g * GT:(g + 1) * GT], Um, Um[:], extra_writes=["U_scr"])
            store('sp', B_scr[:, g * GT:(g + 1) * GT], Bm, Bm[:], extra_writes=["B_scr"])
            store('sp', Ae_scr[:, 4 * g + 1:4 * g + 5], Am, Am[:].rearrange("h (t p) -> h t p", p=128)[:, :, 127],
                  extra_writes=["Ae_scr"])
            for h in range(4):
                tk.dma('sp', Ucol.k, Ucol[:, h, :], U_scr[h, g * GT:(g + 1) * GT].rearrange("(t p) -> p t", p=128),
                       reads=["U_scr", Ucol.n], writes=["Ucol_part%d" % h])
                tk.dma('sp', Bcol.k, Bcol[:, h, :], B_scr[h, g * GT:(g + 1) * GT].rearrange("(t p) -> p t", p=128),
                       reads=["B_scr", Bcol.n], writes=["Bcol_part%d" % h])
            tk.dma('sp', ALb.k, ALb[:], Ae_scr[:, 4 * g:4 * g + 5].unsqueeze(0).to_broadcast([128, 4, 5]),
                   reads=["Ae_scr"], writes=[ALb.n])
            tt(wcol, wcol[:], Ucol[:], ALb[:, :, 1:5], ALU.subtract, [Ucol, ALb] + ["Ucol_part%d" % h for h in range(4)])
            act(wcol, wcol[:], wcol[:], [wcol], AF.Exp)
            tt(sccol, sccol[:], ALb[:, :, 0:4], ALb[:, :, 1:5], ALU.subtract, [ALb])
            act(sccol, sccol[:], sccol[:], [sccol], AF.Exp)
            tt(thrcol, thrcol[:], Bcol[:], ALb[:, :, 1:5], ALU.add, [Bcol, ALb] + ["Bcol_part%d" % h for h in range(4)])
            act(thrcol, thrcol[:], thrcol[:], [thrcol], AF.Exp, scale=-1.0)

            ph = phr
            kmt = ph.sb("kmt", [128, 4, 512], BF16)
            vm = ph.sb("vm", [128, 4, 1024], BF16)
            qm = ph.sb("qm", [128, 4, 512], BF16)
            smo = ph.sb("smo", [128, 4, 1024], BF16)
            qT = ph.sb("qT", [128, 4, GT], BF16)
            kT = ph.sb("kT", [128, 4, GT], BF16)
            Vp = ph.sb("Vp", [128, 257], BF16)
            Stb = ph.sb("Stb", [128, 257], BF16)
            Sm = ph.sb("Sm", [128, 128], BF16)
            hm = ph.sb("hm", [128, 256])
            hj = ph.sb("hj", [128, 256], BF16)
            ymt = ph.sb("ymt", [128, 256], BF16)
            dn = ph.sb("dn", [128, 4])

            proj_tok(hT, psubs, w_in_v, b_in_row, C_MK, 1,
                     lambda M, idx, blk, pt: act(kmt, kmt[:, idx, :], pt[0:M, :], [pt], AF.Identity, scale=128 ** -0.5))
            proj_tok(hT, psubs, w_in_v, b_in_row, C_MV, 2,
                     lambda M, idx, blk, pt: cp(vm, vm[:, idx, blk * 512:(blk + 1) * 512], pt[0:M, :], [pt]))
            if osubs:
                proj_tok(hT, osubs, w_in_v, b_in_row, C_MQ, 1,
                         lambda M, idx, blk, pt: cp(qm, qm[:, idx, :], pt[0:M, :], [pt]))
                proj_tok(hT, osubs, w_in_v, b_in_row, C_MO, 2,
                         lambda M, idx, blk, pt: act(smo, smo[:, idx, blk * 512:(blk + 1) * 512], pt[0:M, :], [pt], AF.Sigmoid))
            for t4 in out_tiles:
                for (src, dst, pi) in [(qm, qT, 4), (kmt, kT, 5)]:
                    pt = pss[pi]
                    ptb = pt[:].bitcast(BF16)
                    for h in range(4):
                        tr(pt, ptb[:, h * 128:(h + 1) * 128], src, src[:, t4, h * 128:(h + 1) * 128], identb, identb[:])
                    cp(dst, dst[:, :, t4 * 128:(t4 + 1) * 128], ptb[:, 0:512].rearrange("p (h t) -> p h t", h=4), [pt])
            for t4 in range(4):
                tc0 = t4 * 128
                for h in range(4):
                    ts(Vp, Vp[:, 0:256], vm[:, t4, h * 256:(h + 1) * 256], wcol[:, h, t4:t4 + 1], None, ALU.mult, None,
                       [vm, wcol])
                    cp(Vp, Vp[:, 256:257], wcol[:, h, t4:t4 + 1], [wcol])
                    ts(St, St[:, h, :], St[:, h, :], sccol[:, h, t4:t4 + 1], None, ALU.mult, None, [St, sccol])
                    if t4 not in out_tiles:
                        p2 = pss[2]
                        mm(p2, p2[:, 0:257], kmt[:, t4, h * 128:(h + 1) * 128], Vp[:], [kmt, Vp], True, True)
                        tt(St, St[:, h, :], St[:, h, :], p2[:, 0:257], ALU.add, [St, p2])
                        continue
                    cp(Stb, Stb[:], St[:, h, :], [St])
                    p0 = pss[0]
                    mm(p0, p0[:, 0:128], kT[:, h, tc0:tc0 + 128], qT[:, h, tc0:tc0 + 128], [kT, qT], True, True)
                    tt(Sm, Sm[:], p0[:, 0:128], maskT[:], ALU.mult, [p0, maskT])
                    p1 = pss[1]
                    mm(p1, p1[:, 0:257], Sm[:], Vp[:], [Sm, Vp], True, False)
                    mm(p1, p1[:, 0:257], qT[:, h, tc0:tc0 + 128], Stb[:], [qT, Stb], False, True)
                    p2 = pss[2]
                    mm(p2, p2[:, 0:257], kmt[:, t4, h * 128:(h + 1) * 128], Vp[:], [kmt, Vp], True, True)
                    tt(St, St[:, h, :], St[:, h, :], p2[:, 0:257], ALU.add, [St, p2])
                    act(dn, dn[:, 0:1], p1[:, 256:257], [p1], AF.Abs)
                    ts(dn, dn[:, 0:1], dn[:, 0:1], thrcol[:, h, t4:t4 + 1], None, ALU.max, None, [dn, thrcol])
                    tk.op('dve', lambda e: e.reciprocal(dn[:, 1:2], dn[:, 0:1]), reads=[dn.n], writes=[dn.n])
                    ts(hm, hm[:], p1[:, 0:256], dn[:, 1:2], None, ALU.mult, None, [p1, dn])
                    act(hj, hj[:], hm[:], [hm], AF.Square, accum=dn[:, 2:3], extra_writes=[dn])
                    rsqrt_mean(dn, dn[:, 3:4], dn[:, 2:3], [dn], 256)
                    stt(hm, hm[:], hm[:], dn[:, 3:4], hnw_s[:, h * 256:(h + 1) * 256], ALU.mult, ALU.mult, [hm, dn, hnw_s])
                    tt(ymt, ymt[:], hm[:], smo[:, t4, h * 256:(h + 1) * 256], ALU.mult, [hm, smo])
                    p3 = pss[3]
                    p3b = p3[:].bitcast(BF16)
                    for a in range(2):
                        tr(p3, p3b[:, a * 128:(a + 1) * 128], ymt, ymt[:, a * 128:(a + 1) * 128], identb, identb[:])
                    cp(ymT, ymT[:, 2 * h:2 * h + 2, tc0:tc0 + 128], p3b[:, 0:256].rearrange("p (a t) -> p a t", a=2), [p3])
            ph.close()

            ph = Phase()
            kst = [ph.sb("kst0", [128, 512]), ph.sb("kst1", [128, 512])]
            scr = ph.sb("scr", [128, 512])
            kb16s = [ph.sb("kb16a", [128, 512], BF16), ph.sb("kb16b", [128, 512], BF16)]
            kTst = ph.sb("kTst", [128, 4, 8, 128], BF16)
            vaug = ph.sb("vaug", [128, 4, 8, 130], BF16)
            QT = ph.sb("QT", [128, 8, GT], BF16)
            KThs = [ph.sb("KTh0", [128, T], BF16), ph.sb("KTh1", [128, T], BF16)]
            Vhs = [ph.sb("Vh0", [128, 32, 130], BF16), ph.sb("Vh1", [128, 32, 130], BF16)]
            fq3s = [ph.sb("fq3a", [3, GT], BF16), ph.sb("fq3b", [3, GT], BF16)]
            eT = [ph.sb("eT0", [128, 512], BF16), ph.sb("eT1", [128, 512], BF16)]
            yft = ph.sb("yft", [128, 128], BF16)
            rd = ph.sb("rd", [128, 1])
            memset(vaug, vaug[:, :, :, 128:130], 1.0)

            def cons_k(M, idx, blk, pt):
                i = (idx + blk) % 2
                cp(kst[i], kst[i][:], pt[0:M, :], [pt])
                qknorm(kst[i], kst[i][:].rearrange("p (h d) -> p h d", h=4), 128, knw_s, scr)
                if mode == 'own':
                    store('sp', kp_v[lo][:, idx, blk * 512:(blk + 1) * 512], kst[i], kst[i][:])
                kb16 = kb16s[i]
                cp(kb16, kb16[:], kst[i][:], [kst[i]])

                def later():
                    p5 = pss[5]
                    p5b = p5[:].bitcast(BF16)
                    for hh in range(4):
                        tr(p5, p5b[:, hh * 128:(hh + 1) * 128], kb16, kb16[:, hh * 128:(hh + 1) * 128], identb, identb[:])
                    cp(kTst, kTst[:, idx, blk * 4:(blk + 1) * 4, :], p5b[:, 0:512].rearrange("p (h t) -> p h t", h=4), [p5])
                return later

            def cons_v(M, idx, blk, pt):
                i = (idx + blk) % 2
                cp(kst[i], kst[i][:], pt[0:M, :], [pt])
                if mode == 'own':
                    store('sp', vp_v[lo][:, idx, blk * 512:(blk + 1) * 512], kst[i], kst[i][:])
                cp(vaug, vaug[:, idx, blk * 4:(blk + 1) * 4, 0:128], kst[i][:].rearrange("p (h d) -> p h d", h=4), [kst[i]])

            def cons_q(M, idx, blk, pt):
                i = (idx + blk) % 2
                cp(kst[i], kst[i][:], pt[0:M, :], [pt])
                qknorm(kst[i], kst[i][:].rearrange("p (h d) -> p h d", h=4), 128, qnw_s, scr)
                kb16 = kb16s[i]
                act(kb16, kb16[:], kst[i][:], [kst[i]], AF.Identity, scale=128 ** -0.5)

                def later():
                    p5 = pss[5]
                    p5b = p5[:].bitcast(BF16)
                    for hh in range(4):
                        tr(p5, p5b[:, hh * 128:(hh + 1) * 128], kb16, kb16[:, hh * 128:(hh + 1) * 128], identb, identb[:])
                    cp(QT, QT[:, blk * 4:(blk + 1) * 4, idx * 128:(idx + 1) * 128],
                       p5b[:, 0:512].rearrange("p (h t) -> p h t", h=4), [p5])
                return later

            proj_tok(hT, psubs, w_in_v, b_in_row, C_FK, 2, cons_k)
            proj_tok(hT, psubs, w_in_v, b_in_row, C_FV, 2, cons_v)
            if osubs:
                proj_tok(hT, osubs, w_in_v, b_in_row, C_FQ, 2, cons_q)
            for t4 in range(4):
                c0_ = g * GT + t4 * 128
                store('sp', kT_scr[:, :, c0_:c0_ + 128].rearrange("h p t -> p h t"), kTst, kTst[:, t4, :, :],
                      extra_writes=["kT_scr"])
                store('sp', v_scr[:, :, 4 * g + t4, :].rearrange("h p c -> p h c"), vaug, vaug[:, t4, :, :],
                      extra_writes=["v_scr"])
            nkt = 4 * (g + 1)
            qis = out_tiles
            def load_head(hh):
                K_, V_, f_ = KThs[hh % 2], Vhs[hh % 2], fq3s[hh % 2]
                tk.dma('sp', K_.k, K_[:, 0:nkt * 128], kT_scr[hh, :, 0:nkt * 128], reads=["kT_scr"], writes=[K_.n])
                tk.dma('sp', V_.k, V_[:, 0:nkt, :], v_scr[hh, :, 0:nkt, :], reads=["v_scr"], writes=[V_.n])
                tk.dma('sp', f_.k, f_[:], F3_scr[:, hh, g * GT:(g + 1) * GT], reads=["F3_scr"], writes=[f_.n])

            if qis:
                load_head(0)
            for h in (range(8) if qis else []):
                if h + 1 < 8:
                    load_head(h + 1)
                KTh, Vh, fq3 = KThs[h % 2], Vhs[h % 2], fq3s[h % 2]
                def att_geom(kt):
                    d = kt - 4 * g
                    ql = [qi for qi in qis if qi >= d]
                    c0_ = max(0, d, min(qis)) * 128
                    return d, ql, c0_, GT - c0_

                def emit_qk(kt):
                    d, ql, c0_, n_ = att_geom(kt)
                    psn = pss[kt % 2]
                    mm(psn, psn[:, 0:n_], KTh[:, kt * 128:(kt + 1) * 128], QT[:, h, c0_:GT], [KTh, QT], True, False)
                    mm(psn, psn[:, 0:n_], ones_b[0:3, 0:128], fq3[0:3, c0_:GT], [ones_b, fq3], False, True)

                def emit_pv(kt):
                    d, ql, c0_, n_ = att_geom(kt)
                    psn = pss[kt % 2]
                    e_ = eT[kt % 2]
                    act(e_, e_[:, 0:n_], psn[:, 0:n_], [psn, nFk], AF.Exp, bias=nFk[:, h, kt:kt + 1])
                    if d >= 0 and d in ql:
                        tt(e_, e_[:, 0:128], e_[:, 0:128], maskTb[:], ALU.mult, [e_, maskTb], eng='pool')
                    for qi in ql:
                        po = pss[2 + qi]
                        mm(po, po[:, 0:130], e_[:, qi * 128 - c0_:(qi + 1) * 128 - c0_], Vh[:, kt, :], [e_, Vh],
                           kt == 0, kt == 4 * g + qi, sig=True)

                kts = [kt for kt in range(nkt) if att_geom(kt)[1]]
                emit_qk(kts[0])
                for ki, kt in enumerate(kts):
                    if ki + 1 < len(kts):
                        emit_qk(kts[ki + 1])
                    emit_pv(kt)
                for qi in qis:
                    po = pss[2 + qi]
                    tk.op('dve', lambda e, po=po: e.reciprocal(rd[:], po[:, 128:129]), reads=[po.n], writes=[rd.n])
                    ts(yft, yft[:], po[:, 0:128], rd[:, 0:1], None, ALU.mult, None, [po, rd])
                    p7 = pss[7]
                    p7b = p7[:].bitcast(BF16)
                    tr(p7, p7b[:, 0:128], yft, yft[:], identb, identb[:])
                    cp(yfT, yfT[:, h, qi * 128:(qi + 1) * 128], p7b[:, 0:128], [p7])
            ph.close()

            if mode == 'own' and lo == 0:
                store('sp', dbg_ymT, ymT, ymT[:])
                store('sp', dbg_yfT, yfT, yfT[:])
            if mode == 'own':
                def ydst():
                    store('sp', yp_v[lo], xt, xt[:])

                dbg_hook[0] = (lambda: store('sp', dbg_x1.rearrange("(t p) d -> p t d", p=128), xt, xt[:])) if lo == 0 else None

                dense_tail(psubs, GT, hT, ymT, yfT, lambda idx: (xt, xt[:, idx, :]),
                           lambda idx, c, n: (grep, grep[:, 0, c:c + n]), lambda idx, c, n: (grep, grep[:, 1, c:c + n]),
                           False, ydst)
            elif mode == 'halo':
                phh = Phase()
                hTh = phh.sb("hTh", [128, KC, 128], BF16)
                ymTh = phh.sb("ymTh", [128, 8, 128], BF16)
                yfTh = phh.sb("yfTh", [128, 8, 128], BF16)
                cp(hTh, hTh[:], hT[:, :, 384:512], [hT])
                cp(ymTh, ymTh[:], ymT[:, :, 384:512], [ymT])
                cp(yfTh, yfTh[:], yfT[:, :, 384:512], [yfT])
                dbg_hook[0] = None
                dense_tail([(128, 0, 3)], 128, hTh, ymTh, yfTh, lambda idx: (xt, xt[:, idx, :]),
                           lambda idx, c, n: (grep, grep[:, 0, c:c + n]), lambda idx, c, n: (grep, grep[:, 1, c:c + n]),
                           False, lambda: None, None, True)
                ts(aprev, aprev[:], aprev[:], -1.0e30, 1.0e30, ALU.max, ALU.min, [aprev])
                ts(aprev, aprev[:], aprev[:], halo_valid[:, 0:1], None, ALU.mult, None, [aprev, halo_valid])
                phh.close()

        if stage >= 1:
            for h in range(4):
                for half in range(2):
                    pt = pss[7]
                    tr(pt, pt[:, 0:128], St, St[:, h, half * 128:(half + 1) * 128], ident, ident[:])
                    cp(Cst, Cst[:], pt[:, 0:128], [pt])
                    store('sp', C_p[h, half * 128:(half + 1) * 128, :], Cst, Cst[:])
                store('sp', n_p[h:h + 1, :].rearrange("o d -> d o"), St, St[:, h, 256:257])
            tt(Bc, Bc[:], Bc[:], Ac[:], ALU.add, [Bc, Ac])
            store('sp', m_p, Bc, Bc[:])
            if stage >= 3:
                for r in range(2):
                    store('sp', cv_p[r].rearrange("(c p) -> p c", p=128), aprev, aprev[:, :, r])

        php.close()
        if stage >= 4:
            phs = Phase()
            xst = phs.sb("xst", [NS, D])
            hTs = phs.sb("hTs", [128, KC, NS], BF16)
            ymTs = phs.sb("ymTs", [128, 8, NS], BF16)
            yfTs = phs.sb("yfTs", [128, 8, NS], BF16)
            gsm = phs.sb("gsm", [NS, 2, D], BF16)
            gms = phs.sb("gms", [8, NS])
            gfs = phs.sb("gfs", [8, NS])
            ksm = phs.sb("ksm", [NS, 1024])
            vsm = phs.sb("vsm", [NS, 1024])
            qsm = phs.sb("qsm", [NS, 1024])
            qsb = phs.sb("qsb", [NS, 1024], BF16)
            kmf = phs.sb("kmf", [NS, 512])
            qmf = phs.sb("qmf", [NS, 512])
            kms = phs.sb("kms", [NS, 512], BF16)
            qms = phs.sb("qms", [NS, 512], BF16)
            vms = phs.sb("vms", [NS, 1024], BF16)
            smoT = phs.sb("smoT", [128, 8, NS])
            scr16 = phs.sb("scr16", [NS, 512])
            b_moT = phs.sb("b_moT", [128, 8])
            hnwT = phs.sb("hnwT", [128, 8])
            sel_f = phs.sb("sel_f", [NS, NS, 128])
            sel_b = phs.sb("sel_b", [NS, NS, 128], BF16)
            ones_f = phs.sb("ones_f", [128, 128])
            load('sp', xst, xst[:], xs)
            tk.dma('sp', gsm.k, gsm[:], gsm_scr, reads=["gsm_scr"], writes=[gsm.n])
            load('sp', b_moT, b_moT[:], b_moT_in)
            load('sp', hnwT, hnwT[:], hnwT_in)
            load('sp', sel_f, sel_f[:], sel_in)
            cp(sel_b, sel_b[:], sel_f[:], [sel_f])
            memset(ones_f, ones_f[:], 1.0)
            norm_tile(xst, xst[:], NS, A1, 0, hTs, 0, None)
            ssubs = [(NS, 0, 0)]
            gates(wg_m, 0, gms, hTs, NS)
            gates(wg_f, 1, gfs, hTs, NS)
            logsig(gfs, gfs[:], gfs[:], [gfs])
            store('sp', lf_s.rearrange("i h -> h i"), gfs, gfs[:])

            def s_k(M, idx, blk, pt):
                cp(ksm, ksm[:, blk * 512:(blk + 1) * 512], pt[0:M, :], [pt])
                qknorm(ksm, ksm[:, blk * 512:(blk + 1) * 512].rearrange("p (h d) -> p h d", h=4), NS, knw_s, scr16)

            def s_q(M, idx, blk, pt):
                cp(qsm, qsm[:, blk * 512:(blk + 1) * 512], pt[0:M, :], [pt])
                qknorm(qsm, qsm[:, blk * 512:(blk + 1) * 512].rearrange("p (h d) -> p h d", h=4), NS, qnw_s, scr16)
                ts(qsm, qsm[:, blk * 512:(blk + 1) * 512], qsm[:, blk * 512:(blk + 1) * 512], 128 ** -0.5, None, ALU.mult,
                   None, [qsm])
                cp(qsb, qsb[:, blk * 512:(blk + 1) * 512], qsm[:, blk * 512:(blk + 1) * 512], [qsm])

            def s_mk(M, idx, blk, pt):
                act(kmf, kmf[:], pt[0:M, :], [pt], AF.Identity, scale=128 ** -0.5)
                cp(kms, kms[:], kmf[:], [kmf])

            def s_mq(M, idx, blk, pt):
                cp(qmf, qmf[:], pt[0:M, :], [pt])
                cp(qms, qms[:], qmf[:], [qmf])

            proj_tok(hTs, ssubs, w_in_v, b_in_row, C_FK, 2, s_k)
            store('sp', k_s, ksm, ksm[:])
            proj_tok(hTs, ssubs, w_in_v, b_in_row, C_FV, 2,
                     lambda M, idx, blk, pt: cp(vsm, vsm[:, blk * 512:(blk + 1) * 512], pt[0:M, :], [pt]))
            store('sp', v_s, vsm, vsm[:])
            proj_tok(hTs, ssubs, w_in_v, b_in_row, C_FQ, 2, s_q)
            proj_tok(hTs, ssubs, w_in_v, b_in_row, C_MK, 1, s_mk)
            proj_tok(hTs, ssubs, w_in_v, b_in_row, C_MQ, 1, s_mq)
            proj_tok(hTs, ssubs, w_in_v, b_in_row, C_MV, 2,
                     lambda M, idx, blk, pt: cp(vms, vms[:, blk * 512:(blk + 1) * 512], pt[0:M, :], [pt]))
            for blk in range(2):
                wv, wn, _ = load_w(w_in_v[:, :, C_MO + blk * 512:C_MO + (blk + 1) * 512], [128, KC, 512])
                for ct in range(4):
                    pt = pss[ct]
                    for kc in range(KC):
                        mm(pt, pt[:, 0:NS], wv[:, kc, ct * 128:(ct + 1) * 128], hTs[:, kc, :], [wn, hTs], kc == 0, kc == KC - 1)
                    ch = blk * 4 + ct
                    act(smoT, smoT[:, ch, :], pt[:, 0:NS], [pt, b_moT], AF.Sigmoid, bias=b_moT[:, ch:ch + 1])

            phm = Phase()
            m0s = phm.sb("m0s", [NS, 4])
            n0s = phm.sb("n0s", [NS, 512])
            gts = phm.sb("gts", [NS, 8])
            sm = phm.sb("sm", [NS, 8, 4])
            bcs = phm.sb("bcs", [NS, 16])
            t16 = phm.sb("t16", [NS, 512])
            vTs = phm.sb("vTs", [128, 8, NS], BF16)
            Cts = [phm.sb("Ct0", [128, 8, 128]), phm.sb("Ct1", [128, 8, 128])]
            Cu = phm.sb("Cu", [128, 8, 128])
            scw = phm.sb("scw", [128, 16])
            wv8 = phm.sb("wv8", [128, 8])
            c0q = phm.sb("c0q", [128, 8])
            t8 = phm.sb("t8", [128, 8])
            hAll = phm.sb("hAll", [128, 8, NS])
            hsq = phm.sb("hsq", [128, 8, NS])
            ssum = phm.sb("ssum", [128, 4, NS])
            load('sp', m0s, m0s[:], st_m)
            load('sp', n0s, n0s[:], st_n)
            pt = pss[6]
            tr(pt, pt[0:NS, 0:8], gms, gms[:], ident, ident[0:8, 0:8])
            cp(gts, gts[:], pt[0:NS, 0:8], [pt])
            logsig(gts, gts[:, 4:8], gts[:, 4:8], [gts])
            tt(sm, sm[:, 0, :], m0s[:], gts[:, 4:8], ALU.add, [m0s, gts])
            tt(sm, sm[:, 1, :], sm[:, 0, :], gts[:, 0:4], ALU.max, [sm, gts])
            store('sp', m_s, sm, sm[:, 1, :])
            tt(sm, sm[:, 2, :], sm[:, 0, :], sm[:, 1, :], ALU.subtract, [sm])
            act(sm, sm[:, 2, :], sm[:, 2, :], [sm], AF.Exp)
            tt(sm, sm[:, 3, :], gts[:, 0:4], sm[:, 1, :], ALU.subtract, [sm, gts])
            act(sm, sm[:, 3, :], sm[:, 3, :], [sm], AF.Exp)
            act(sm, sm[:, 4, :], sm[:, 1, :], [sm], AF.Exp, scale=-1.0)
            q3 = qmf[:].rearrange("p (h d) -> p h d", h=4)
            k3 = kmf[:].rearrange("p (h d) -> p h d", h=4)
            n3 = n0s[:].rearrange("p (h d) -> p h d", h=4)
            t3 = t16[:].rearrange("p (h d) -> p h d", h=4)
            tt(t16, t3, q3, k3, ALU.mult, [qmf, kmf])
            tk.op('dve', lambda e: e.tensor_reduce(sm[:, 5, :], t3, AX.X, ALU.add), reads=[t16.n], writes=[sm.n])
            tt(t16, t3, q3, n3, ALU.mult, [qmf, n0s])
            tk.op('dve', lambda e: e.tensor_reduce(sm[:, 6, :], t3, AX.X, ALU.add), reads=[t16.n], writes=[sm.n])
            tt(sm, sm[:, 7, :], sm[:, 3, :], sm[:, 5, :], ALU.mult, [sm])
            tt(sm, sm[:, 6, :], sm[:, 6, :], sm[:, 2, :], ALU.mult, [sm])
            tt(sm, sm[:, 6, :], sm[:, 6, :], sm[:, 7, :], ALU.add, [sm])
            act(sm, sm[:, 6, :], sm[:, 6, :], [sm], AF.Abs)
            tt(sm, sm[:, 6, :], sm[:, 6, :], sm[:, 4, :], ALU.max, [sm])
            tk.op('dve', lambda e: e.reciprocal(sm[:, 6, :], sm[:, 6, :]), reads=[sm.n], writes=[sm.n])
            cp(bcs, bcs[:, 0:4], sm[:, 2, :], [sm])
            cp(bcs, bcs[:, 4:8], sm[:, 3, :], [sm])
            cp(bcs, bcs[:, 8:12], sm[:, 6, :], [sm])
            cp(bcs, bcs[:, 12:16], sm[:, 7, :], [sm])
            tt(n0s, n3, n3, sm[:, 2, :].unsqueeze(2).to_broadcast([NS, 4, 128]), ALU.mult, [n0s, sm])
            tt(t16, t3, k3, sm[:, 3, :].unsqueeze(2).to_broadcast([NS, 4, 128]), ALU.mult, [kmf, sm])
            tt(n0s, n0s[:], n0s[:], t16[:], ALU.add, [n0s, t16])
            store('sp', n_s, n0s, n0s[:])
            ptb = pss[6][:].bitcast(BF16)
            for c8 in range(8):
                tr(pss[6], ptb[:, c8 * NS:(c8 + 1) * NS], vms, vms[:, c8 * 128:(c8 + 1) * 128], identb, identb[0:NS, 0:NS])
            cp(vTs, vTs[:].rearrange("p c i -> p (c i)"), ptb[:, 0:8 * NS], [pss[6]])
            tk.dma('sp', Cts[0].k, Cts[0][:], st_C[0].rearrange("h (a p) d -> p (h a) d", p=128), writes=[Cts[0].n])
            for i in range(NS):
                Ct = Cts[i % 2]
                if i + 1 < NS:
                    Cn = Cts[(i + 1) % 2]
                    tk.dma('sp', Cn.k, Cn[:], st_C[i + 1].rearrange("h (a p) d -> p (h a) d", p=128), writes=[Cn.n])
                mm(pss[0], pss[0][:, :], sel_b[:, i, :], kms[:, :], [sel_b, kms], True, True)
                mm(pss[2], pss[2][:, :], sel_b[:, i, :], qms[:, :], [sel_b, qms], True, True)
                mm(pss[1], pss[1][:, 0:16], sel_f[:, i, :], bcs[:, :], [sel_f, bcs], True, True)
                cp(scw, scw[:], pss[1][:, 0:16], [pss[1]])
                C4 = Ct[:].rearrange("p (h a) d -> p h a d", h=4)
                U4 = Cu[:].rearrange("p (h a) d -> p h a d", h=4)
                tt(Cu, U4, C4, pss[2][:, :].rearrange("p (h d) -> p h d", h=4).unsqueeze(2).to_broadcast([128, 4, 2, 128]),
                   ALU.mult, [Ct, pss[2]])
                tk.op('dve', lambda e: e.tensor_reduce(c0q[:], Cu[:], AX.X, ALU.add), reads=[Cu.n], writes=[c0q.n])
                v3 = vTs[:, :, i].rearrange("p (h a) -> p h a", h=4)
                tt(wv8, wv8[:].rearrange("p (h a) -> p h a", h=4), v3, scw[:, 4:8].unsqueeze(2).to_broadcast([128, 4, 2]),
                   ALU.mult, [vTs, scw])
                tt(t8, t8[:].rearrange("p (h a) -> p h a", h=4), v3, scw[:, 12:16].unsqueeze(2).to_broadcast([128, 4, 2]),
                   ALU.mult, [vTs, scw])
                tt(c0q, c0q[:].rearrange("p (h a) -> p h a", h=4), c0q[:].rearrange("p (h a) -> p h a", h=4),
                   scw[:, 0:4].unsqueeze(2).to_broadcast([128, 4, 2]), ALU.mult, [c0q, scw])
                tt(c0q, c0q[:], c0q[:], t8[:], ALU.add, [c0q, t8])
                tt(hAll, hAll[:, :, i].rearrange("p (h a) -> p h a", h=4), c0q[:].rearrange("p (h a) -> p h a", h=4),
                   scw[:, 8:12].unsqueeze(2).to_broadcast([128, 4, 2]), ALU.mult, [c0q, scw])
                tt(Ct, C4, C4, scw[:, 0:4].unsqueeze(2).unsqueeze(3).to_broadcast([128, 4, 2, 128]), ALU.mult, [Ct, scw])
                tt(Cu, U4, pss[0][:, :].rearrange("p (h d) -> p h d", h=4).unsqueeze(2).to_broadcast([128, 4, 2, 128]),
                   wv8[:].rearrange("p (h a) -> p h a", h=4).unsqueeze(3).to_broadcast([128, 4, 2, 128]), ALU.mult,
                   [pss[0], wv8, Cu])
                tt(Ct, Ct[:], Ct[:], Cu[:], ALU.add, [Ct, Cu], eng='pool')
                store('sp', C_s[i].rearrange("h (a p) d -> p (h a) d", p=128), Ct, Ct[:])
            tt(hsq, hsq[:], hAll[:], hAll[:], ALU.mult, [hAll])
            mm(pss[3], pss[3][:, 0:8 * NS], ones_f[:], hsq[:].rearrange("p c i -> p (c i)"), [ones_f, hsq], True, True)
            p3v = pss[3][:, 0:8 * NS].rearrange("p (h a i) -> p h a i", h=4, a=2)
            cp(ssum, ssum[:], p3v[:, :, 0, :], [pss[3]])
            tt(ssum, ssum[:], ssum[:], p3v[:, :, 1, :], ALU.add, [ssum, pss[3]])
            rsqrt_mean(ssum, ssum[:], ssum[:], [ssum], 256)
            h4 = hAll[:].rearrange("p (h a) i -> p h a i", h=4)
            tt(hAll, h4, h4, ssum[:].unsqueeze(2).to_broadcast([128, 4, 2, NS]), ALU.mult, [hAll, ssum])
            tt(hAll, hAll[:], hAll[:], hnwT[:].unsqueeze(2).to_broadcast([128, 8, NS]), ALU.mult, [hAll, hnwT])
            tt(ymTs, ymTs[:], hAll[:], smoT[:], ALU.mult, [hAll, smoT])
            phm.close()

            pha = Phase()
            NP = NS * 16
            ptb_i = pha.sb("ptb_i", [128, NP], I32)
            ptb_f = pha.sb("ptb_f", [128, NP])
            pidx = pha.sb("pidx", [128, 1])
            idx_all = pha.sb("idx_all", [128, NP], I32)
            pcol = pha.sb("pcol", [128, 2], I32)
            LF = pha.sb("LF", [128, 2, 1024])
            tot = pha.sb("tot", [128, 2, 8])
            later = pha.sb("later", [128, 2, 8])
            Lmat = pha.sb("Lmat", [128, 128])
            Emat = pha.sb("Emat", [NS, 2, 128])
            gft = pha.sb("gft", [NS, 8])
            DT = pha.sb("DT", [128, NP, 8])
            sc_all = pha.sb("sc_all", [128, NP, 8])
            e_all = pha.sb("e_all", [128, NP, 8], BF16)
            qbf = pha.sb("qbf", [128, 1024], BF16)
            NPB = 4
            Kpg = [pha.sb("Kpg%d" % q_, [128, 1024], BF16) for q_ in range(NPB)]
            Vpg = [pha.sb("Vpg%d" % q_, [128, 1024], BF16) for q_ in range(NPB)]
            prod = pha.sb("prod", [128, 1024], BF16)
            bdm = pha.sb("bdm", [8, 1024])
            hsel = pha.sb("hsel", [8, NS, NS])
            msk = pha.sb("msk", [8, 1024])
            pdd = pha.sb("pdd", [8, 8])
            pdc = pha.sb("pdc", [8, 1])
            enew = pha.sb("enew", [NS, 8])
            yacc = pha.sb("yacc", [NS, 1024])
            dacc = pha.sb("dacc", [NS, 8])
            yfb = pha.sb("yfb", [NS, 1024], BF16)
            zscan = pha.sb("zscan", [128, 128])
            memset(zscan, zscan[:], 0.0)
            load('sp', ptb_i, ptb_i[:], ptab[0:1, :].to_broadcast([128, NP]))
            load('sp', pidx, pidx[:], pidx_in)
            load('sp', pcol, pcol[:], ptab_col)
            load('sp', Lmat, Lmat[:], Lmat_in)
            load('sp', Emat, Emat[:], Emat_in)
            load('sp', bdm, bdm[:], bdm_in)
            load('sp', hsel, hsel[:], hsel_in)
            cp(ptb_f, ptb_f[:], ptb_i[:], [ptb_i])
            ts(ptb_f, ptb_f[:], ptb_f[:], 128.0, pidx[:, 0:1], ALU.mult, ALU.add, [ptb_f, pidx])
            cp(idx_all, idx_all[:], ptb_f[:], [ptb_f])

            def gather(dst, dst_ap, src_ap, idx_buf, idx_ap):
                tk._wait('pool', tk._deps([idx_buf.n], [dst.n]))
                key = dst.k
                if key not in tk.dsem:
                    tk.dsem[key] = es.enter_context(nc.semaphore('d_' + key))
                    tk.dcnt[key] = 0
                ins = nc.gpsimd.indirect_dma_start(out=dst_ap, out_offset=None, in_=src_ap,
                                                   in_offset=bass.IndirectOffsetOnAxis(ap=idx_ap, axis=0))
                tk.dcnt[key] += 16
                ins.then_inc(tk.dsem[key], 16)
                tk._commit((key, tk.dcnt[key]), [idx_buf.n], [dst.n])

            for half in range(2):
                gather(LF, LF[:, half, :], cache_lf[:, :], pcol, pcol[:, half:half + 1])
            for half in range(2):
                for h in range(8):
                    v2 = LF[:, half, :].rearrange("p (k h) -> p h k", h=8)[:, h, :]
                    tk.op('dve', lambda e, v2=v2: e.tensor_tensor_scan(v2, v2, zscan[:], 0.0, ALU.add, ALU.add),
                          reads=[LF.n, zscan.n], writes=[LF.n])
            LF4 = LF[:].rearrange("p a (k h) -> p a k h", h=8)
            cp(tot, tot[:], LF4[:, :, 127, :], [LF])
            tt(LF, LF4, tot[:].unsqueeze(2).to_broadcast([128, 2, 128, 8]), LF4, ALU.subtract, [tot, LF])
            mm(pss[0], pss[0][:, 0:16], Lmat[:], tot[:].rearrange("p a h -> p (a h)"), [Lmat, tot], True, True)
            cp(later, later[:].rearrange("p a h -> p (a h)"), pss[0][:, 0:16], [pss[0]])
            tr(pss[1], pss[1][0:NS, 0:8], gfs, gfs[:], ident, ident[0:8, 0:8])
            cp(gft, gft[:], pss[1][0:NS, 0:8], [pss[1]])
            for half in range(2):
                mm(pss[2], pss[2][:, half * 8:(half + 1) * 8], Emat[:, half, :], gft[:], [Emat, gft], True, True)
            tt(later, later[:].rearrange("p a h -> p (a h)"), later[:].rearrange("p a h -> p (a h)"), pss[2][:, 0:16], ALU.add,
               [later, pss[2]])
            tt(LF, LF4, LF4, later[:].unsqueeze(2).to_broadcast([128, 2, 128, 8]), ALU.add, [LF, later])
            for half in range(2):
                for h in range(8):
                    pt = pss[4 + (h % 2)]
                    tr(pt, pt[:, 0:128], LF, LF4[:, half, :, h], ident, ident[:])
                    cp(DT, DT[:, half * 128:(half + 1) * 128, h], pt[:, 0:128], [pt])
            tt(scr16, scr16[:], qsm[:, 0:512], ksm[:, 0:512], ALU.mult, [qsm, ksm])
            tk.op('dve', lambda e: e.tensor_reduce(enew[:, 0:4], scr16[:].rearrange("p (h d) -> p h d", h=4), AX.X, ALU.add),
                  reads=[scr16.n], writes=[enew.n])
            tt(scr16, scr16[:], qsm[:, 512:1024], ksm[:, 512:1024], ALU.mult, [qsm, ksm])
            tk.op('dve', lambda e: e.tensor_reduce(enew[:, 4:8], scr16[:].rearrange("p (h d) -> p h d", h=4), AX.X, ALU.add),
                  reads=[scr16.n], writes=[enew.n])
            act(enew, enew[:], enew[:], [enew], AF.Exp)
            for i in range(NS):
                for hb in range(2):
                    mm(pss[hb], pss[hb][:, :], sel_b[:, i, :], qsb[:, hb * 512:(hb + 1) * 512], [sel_b, qsb], True, True)
                    cp(qbf, qbf[:, hb * 512:(hb + 1) * 512], pss[hb][:, :], [pss[hb]])
                for pg in range(16):
                    j = i * 16 + pg
                    gather(Kpg[j % NPB], Kpg[j % NPB][:, :], cache_k[:, :], idx_all, idx_all[:, j:j + 1])
                    gather(Vpg[j % NPB], Vpg[j % NPB][:, :], cache_v[:, :], idx_all, idx_all[:, j:j + 1])
                    tt(prod, prod[:], Kpg[j % NPB][:], qbf[:], ALU.mult, [Kpg[j % NPB], qbf])
                    tk.op('dve', lambda e, j=j: e.tensor_reduce(sc_all[:, j, :], prod[:].rearrange("p (h d) -> p h d", h=8),
                                                                AX.X, ALU.add), reads=[prod.n], writes=[sc_all.n])
                    tt(sc_all, sc_all[:, j, :], sc_all[:, j, :], DT[:, j, :], ALU.add, [sc_all, DT])
                    act(e_all, e_all[:, j, :], sc_all[:, j, :], [sc_all], AF.Exp)
                    for hb in range(2):
                        mm(pss[2 + hb], pss[2 + hb][0:8, :], e_all[:, j, :], Vpg[j % NPB][:, hb * 512:(hb + 1) * 512],
                           [e_all, Vpg[j % NPB]], pg == 0, pg == 15, sig=True)
                    mm(pss[4], pss[4][0:8, 0:1], e_all[:, j, :], ones_b[:, 0:1], [e_all, ones_b], pg == 0, pg == 15, sig=True)
                for hb in range(2):
                    tt(msk, msk[:, hb * 512:(hb + 1) * 512], pss[2 + hb][0:8, :], bdm[:, hb * 512:(hb + 1) * 512], ALU.mult,
                       [pss[2 + hb], bdm])
                cp(pdc, pdc[:], pss[4][0:8, 0:1], [pss[4]])
                ts(pdd, pdd[:], ident[0:8, 0:8], pdc[:, 0:1], None, ALU.mult, None, [ident, pdc])
                for hb in range(2):
                    mm(pss[5 + hb], pss[5 + hb][0:NS, :], hsel[:, i, :], msk[:, hb * 512:(hb + 1) * 512], [hsel, msk],
                       i == 0, i == NS - 1, sig=True)
                mm(pss[7], pss[7][0:NS, 0:8], hsel[:, i, :], pdd[:], [hsel, pdd], i == 0, i == NS - 1, sig=True)
            for hb in range(2):
                cp(yacc, yacc[:, hb * 512:(hb + 1) * 512], pss[5 + hb][0:NS, :], [pss[5 + hb]])
            tt(dacc, dacc[:], pss[7][0:NS, 0:8], enew[:], ALU.add, [pss[7], enew])
            tk.op('dve', lambda e: e.reciprocal(dacc[:], dacc[:]), reads=[dacc.n], writes=[dacc.n])
            v3s = vsm[:].rearrange("p (h d) -> p h d", h=8)
            y3s = yacc[:].rearrange("p (h d) -> p h d", h=8)
            tt(vsm, v3s, v3s, enew[:].unsqueeze(2).to_broadcast([NS, 8, 128]), ALU.mult, [vsm, enew])
            tt(yacc, yacc[:], yacc[:], vsm[:], ALU.add, [yacc, vsm])
            tt(yacc, y3s, y3s, dacc[:].unsqueeze(2).to_broadcast([NS, 8, 128]), ALU.mult, [yacc, dacc])
            cp(yfb, yfb[:], yacc[:], [yacc])
            ptb = pss[0][:].bitcast(BF16)
            for c8 in range(8):
                tr(pss[0], ptb[:, c8 * NS:(c8 + 1) * NS], yfb, yfb[:, c8 * 128:(c8 + 1) * 128], identb, identb[0:NS, 0:NS])
            cp(yfTs, yfTs[:].rearrange("p c i -> p (c i)"), ptb[:, 0:8 * NS], [pss[0]])
            pha.close()

            def ydst_s():
                store('sp', y_s, xst, xst[:])

            def post_s(aS, ph_):
                cvst = ph_.sb("cvst", [NS, 512])
                for fb in range(FC // 4):
                    pt = pss[4 + fb % 2]
                    for ci in range(4):
                        tr(pt, pt[0:NS, ci * 128:(ci + 1) * 128], aS, aS[:, fb * 4 + ci, :], ident, ident[:])
                    cp(cvst, cvst[:], pt[0:NS, :], [pt])
                    store('sp', cv_s[:, 1, fb * 512:(fb + 1) * 512], cvst, cvst[:])

            dbg_hook[0] = None
            dense_tail(ssubs, NS, hTs, ymTs, yfTs, lambda idx: (xst, xst[:]),
                       lambda idx, c, n: (gsm, gsm[:, 0, c:c + n]), lambda idx, c, n: (gsm, gsm[:, 1, c:c + n]),
                       True, ydst_s, post_s)
            tk.dma('sp', "cvs0", cv_s[:, 0, :], st_conv[:, 1, :])
            phs.close()

        tk.finish()
    return nc


_CACHE = {}


def _prep_inputs(inp):
    f = np.float32
    w = {k: np.asarray(v) for k, v in inp.items()}
    ident = np.eye(128, dtype=f)
    maskT = np.triu(np.ones((128, 128), f))
    sel = np.zeros((NS, NS, 128), f)
    for i in range(NS):
        sel[i, i, :] = 1.0
    hsel = np.zeros((8, NS, NS), f)
    for i in range(NS):
        hsel[:, i, i] = 1.0
    bdm = np.repeat(np.eye(8, dtype=f), 128, axis=1)
    pidx = np.arange(128, dtype=f)[:, None].copy()
    Lmat = np.zeros((128, 128), f)
    for a in range(128):
        for b_ in range(128):
            if a // 16 == b_ // 16 and a > b_:
                Lmat[a, b_] = 1.0
    Emat = np.zeros((NS, 2, 128), f)
    for half in range(2):
        for j in range(128):
            Emat[half * 8 + j // 16, half, j] = 1.0
    b_in = w['b_in'][0]
    b_gT = np.concatenate([b_in[C_GA:C_GA + D].reshape(16, 128).T, b_in[C_GB:C_GB + D].reshape(16, 128).T], axis=1)
    b_gates = np.stack([b_in[C_MI:C_MI + 8], b_in[C_FF:C_FF + 8]], axis=1)
    common = dict(
        w_ada=w['w_ada'][0], w_in=w['w_in'][0], w_pa=w['w_proj_a'][0], w_pb=w['w_proj_b'][0], w_out=w['w_out'][0],
        w_f1=w['w_ffn_in'][0], w_f2=w['w_ffn_out'][0],
        b_adaT=np.ascontiguousarray(w['b_ada'][0].reshape(96, 128).T), b_ada_row=w['b_ada'][0][None, :].copy(),
        n1wT=np.ascontiguousarray(w['norm1_w'][0].reshape(16, 128).T), n2wT=np.ascontiguousarray(w['norm2_w'][0].reshape(16, 128).T),
        b_in_row=b_in[None, :].copy(), b_gT=np.ascontiguousarray(b_gT), b_gates=np.ascontiguousarray(b_gates),
        hnw_rep=np.ascontiguousarray(np.broadcast_to(w['m_hnorm_w'][0][None, :], (128, 1024))),
        qnw_rep=np.ascontiguousarray(np.broadcast_to(w['f_qnorm_w'][0][None, :], (128, 128))),
        knw_rep=np.ascontiguousarray(np.broadcast_to(w['f_knorm_w'][0][None, :], (128, 128))),
        cwT=np.ascontiguousarray(w['conv_w'][0].reshape(3, FC, 128).transpose(2, 1, 0)),
        cbT=np.ascontiguousarray(w['conv_b'][0].reshape(FC, 128).T),
        ident=ident, maskT=maskT, sel=sel, hsel=hsel, bdm=bdm, pidx=pidx, Lmat=Lmat, Emat=Emat,
        b_moT=np.ascontiguousarray(b_in[C_MO:C_MO + 1024].reshape(8, 128).T),
        hnwT=np.ascontiguousarray(w['m_hnorm_w'][0].reshape(8, 128).T),
        cache_k=w['cache_k'][0].reshape(NPOOL * 128, 1024), cache_v=w['cache_v'][0].reshape(NPOOL * 128, 1024),
        cache_lf=w['cache_logf'][0].reshape(NPOOL, 1024),
    )
    maps = []
    for c in range(8):
        sl = slice(c * NS, (c + 1) * NS)
        b, j = c // 4, c % 4
        nvalid = 1024 * (j + 1)
        xwin = np.zeros((T, D), f)
        xwin[T - nvalid:] = w['x_prompt'][b, :nvalid]
        vt = (np.arange(32) >= 32 - 8 * (j + 1)).astype(f)
        kmask = np.ascontiguousarray(np.broadcast_to(((vt - 1.0) * 30000.0)[None, :], (128, 32)))
        vg = vt.reshape(NG, 4)[:, 0]
        vgate = np.ascontiguousarray(np.broadcast_to(np.stack([vg, (vg - 1.0) * 1.0e4], axis=1)[None], (8, NG, 2)))
        halo_valid = np.full((128, 1), 1.0 if j > 0 else 0.0, f)
        cc = np.concatenate([w['c_sample'][sl], w['c_prompt'][b:b + 1]], axis=0)
        cT = np.ascontiguousarray(cc.reshape(NS + 1, KC, 128).transpose(2, 1, 0))
        pt = w['page_table'][sl].astype(np.int32)
        m = dict(common)
        m.update(
            xw=xwin, kmask=kmask, vgate=vgate.astype(f), halo_valid=halo_valid,
            xs=np.ascontiguousarray(w['x_sample'][sl, 0, :]), cT=cT,
            ptab=pt.reshape(1, NS * 16).copy(),
            ptab_col=np.ascontiguousarray(pt.reshape(2, 128).T),
            st_C=np.ascontiguousarray(w['state_C'][0, sl]), st_n=np.ascontiguousarray(w['state_n'][0, sl].reshape(NS, 512)),
            st_m=np.ascontiguousarray(w['state_m'][0, sl]),
            st_convT=np.ascontiguousarray(w['state_conv'][0, sl].reshape(NS, 2, FC, 128).transpose(3, 2, 0, 1)),
            st_conv=np.ascontiguousarray(w['state_conv'][0, sl]),
        )
        maps.append(m)
    return maps


def kernel(**inputs):
    stage = inputs.pop('_stage', 99)
    ng = inputs.pop('_ng', NG)
    if (stage, ng) not in _CACHE:
        _CACHE[(stage, ng)] = build_program(stage, ng)
    nc = _CACHE[(stage, ng)]
    maps = _prep_inputs(inputs)
    res = run_bass_kernel_spmd(nc, maps, core_ids=list(range(8)))
    r = res.results
    f = np.float32
    global _LAST
    _LAST = r

    def cat(name):
        return np.concatenate([np.asarray(r[c][name]) for c in range(8)], axis=0)

    def catp(name, shape):
        return np.stack([np.concatenate([np.asarray(r[b * 4 + j][name]) for j in range(4)], axis=0) for b in range(2)]).reshape(shape).astype(f)

    y_prompt = catp('y_p', (2, T, D))
    y_sample = cat('y_s').reshape(128, 1, D).astype(f)
    k_prompt = catp('k_p', (1, 2, T, 8, 128))
    v_prompt = catp('v_p', (1, 2, T, 8, 128))
    lf_prompt = catp('lf_p', (1, 2, T, 8))
    C_prompt = np.stack([r[4 * b + 3]['C_p'] for b in range(2)]).reshape(1, 2, 4, 256, 128).astype(f)
    n_prompt = np.stack([r[4 * b + 3]['n_p'] for b in range(2)]).reshape(1, 2, 4, 128).astype(f)
    m_prompt = np.stack([r[4 * b + 3]['m_p'] for b in range(2)]).reshape(1, 2, 4).astype(f)
    cv_prompt = np.stack([r[4 * b + 3]['cv_p'] for b in range(2)]).reshape(1, 2, 2, DFF).astype(f)
    k_sample = cat('k_s').reshape(1, 128, 1, 8, 128).astype(f)
    v_sample = cat('v_s').reshape(1, 128, 1, 8, 128).astype(f)
    lf_sample = cat('lf_s').reshape(1, 128, 1, 8).astype(f)
    C_sample = cat('C_s').reshape(1, 128, 4, 256, 128).astype(f)
    n_sample = cat('n_s').reshape(1, 128, 4, 128).astype(f)
    m_sample = cat('m_s').reshape(1, 128, 4).astype(f)
    cv_sample = cat('cv_s').reshape(1, 128, 2, DFF).astype(f)
    return (y_prompt, y_sample, k_prompt, v_prompt, lf_prompt, C_prompt, n_prompt, m_prompt, cv_prompt,
            k_sample, v_sample, lf_sample, C_sample, n_sample, m_sample, cv_sample)
```
